# Optimizing a Trainium2 kernel written in Bass

```python
import math
import jax
import jax.numpy as jnp
from jax import lax
import numpy as np

D_MODEL = 1024
BATCH = 16
SEQ = 2048
DEPTH = 2
DEC_BATCH = 32
DEC_SEQ = 32
PAST_LEN = 1024

CHUNK = 64
N_META = 16
QBLOCK = 128
MLA_HEADS = 4
MLA_Q_LORA = 256
MLA_KV_LORA = 128
MLA_NOPE = 64
MLA_ROPE = 32
MLA_V = 64
ROPE_THETA = 10000.0
SB_HEADS = 4
SB_DH = 64
DIFF_HEADS = 4
DIFF_DQK = 32
DIFF_DV = 64
DSA_HEADS = 4
DSA_DH = 64
IDX_HEADS = 8
IDX_DIM = 32
DSA_TOPK = 256
N_BRANCH = 4
BR_W = 256
D_FF = 4 * D_MODEL
T5_BUCKETS = 32
T5_MAX_DIST = 128
LN_EPS = 1e-5
RMS_EPS = 1e-6
NEG_INF = -1e30
DN_ALPHA = (2 * DEPTH) ** 0.25
DN_BETA = (8 * DEPTH) ** -0.25
IN_SIZES = (MLA_Q_LORA, MLA_KV_LORA, MLA_ROPE,
            SB_HEADS * SB_DH, SB_HEADS * SB_DH, SB_HEADS * SB_DH,
            DIFF_HEADS * 2 * DIFF_DQK, DIFF_HEADS * 2 * DIFF_DQK, DIFF_HEADS * DIFF_DV,
            DSA_HEADS * DSA_DH, DSA_HEADS * DSA_DH, DSA_HEADS * DSA_DH,
            IDX_HEADS * IDX_DIM, IDX_DIM, IDX_HEADS,
            N_BRANCH * D_MODEL)
IN_SPLITS = tuple(int(s) for s in np.cumsum(IN_SIZES)[:-1])
D_IN = int(sum(IN_SIZES))

kernel_name = 'hybrid_streaming_encoder_step'


def _layer_norm(x, g, b):
    xf = x.astype(jnp.float32)
    mu = jnp.mean(xf, axis=-1, keepdims=True)
    var = jnp.mean(jnp.square(xf - mu), axis=-1, keepdims=True)
    y = (xf - mu) * lax.rsqrt(var + LN_EPS) * g.astype(jnp.float32) + b.astype(jnp.float32)
    return y.astype(x.dtype)


def _rms_norm(x, g):
    xf = x.astype(jnp.float32)
    y = xf * lax.rsqrt(jnp.mean(jnp.square(xf), axis=-1, keepdims=True) + RMS_EPS)
    return (y * g.astype(jnp.float32)).astype(x.dtype)


def _rope(x, pos):
    half = x.shape[-1] // 2
    inv_freq = ROPE_THETA ** (-jnp.arange(half, dtype=jnp.float32) / half)
    ang = pos.astype(jnp.float32)[:, None] * inv_freq[None, :]
    shape = (pos.shape[0],) + (1,) * (x.ndim - 3) + (half,)
    cos = jnp.cos(ang).reshape(shape)
    sin = jnp.sin(ang).reshape(shape)
    xf = x.astype(jnp.float32)
    x1, x2 = xf[..., :half], xf[..., half:]
    return jnp.concatenate([x1 * cos - x2 * sin, x2 * cos + x1 * sin], axis=-1).astype(x.dtype)


def _chunk_id(pos):
    return jnp.floor_divide(pos - N_META, CHUNK)


def _chunk_mask(qpos, kpos):
    return _chunk_id(kpos)[None, :] <= _chunk_id(qpos)[:, None]


def _t5_bucket(rel):
    nb = T5_BUCKETS // 2
    max_exact = nb // 2
    n = jnp.abs(rel)
    nf = jnp.maximum(n, 1).astype(jnp.float32)
    large = max_exact + (jnp.log(nf / max_exact) / math.log(T5_MAX_DIST / max_exact)
                         * (nb - max_exact)).astype(jnp.int32)
    large = jnp.minimum(large, nb - 1)
    return jnp.where(rel > 0, nb, 0) + jnp.where(n < max_exact, n, large)


def _query_blocks(fn, qpos, *qs):
    lq = qpos.shape[0]
    if lq <= QBLOCK:
        return fn(qpos, *qs)
    nb = -(-lq // QBLOCK)
    pad = nb * QBLOCK - lq
    qpos_b = jnp.pad(qpos, (0, pad), mode='edge').reshape(nb, QBLOCK)
    qs_b = tuple(
        jnp.moveaxis(jnp.pad(q, [(0, 0), (0, pad)] + [(0, 0)] * (q.ndim - 2))
                     .reshape((q.shape[0], nb, QBLOCK) + q.shape[2:]), 1, 0)
        for q in qs)
    out = lax.map(lambda a: fn(a[0], *a[1:]), (qpos_b,) + qs_b)
    out = jnp.moveaxis(out, 0, 1)
    out = out.reshape((out.shape[0], nb * QBLOCK) + out.shape[3:])
    return out[:, :lq]


def _mla_attend(qpos, q_lat, q_pe, kpos, ckv, kpe):
    s = (jnp.einsum('bqhc,bkc->bhqk', q_lat, ckv)
         + jnp.einsum('bqhr,bkr->bhqk', q_pe, kpe)).astype(jnp.float32)
    s = jnp.where(_chunk_mask(qpos, kpos)[None, None], s * (MLA_NOPE + MLA_ROPE) ** -0.5, NEG_INF)
    p = jax.nn.softmax(s, axis=-1).astype(ckv.dtype)
    return jnp.einsum('bhqk,bkc->bqhc', p, ckv)


def _sb_attend(qpos, q, kpos, k, v):
    z = jnp.einsum('bqhd,bkhd->bhqk', q, k).astype(jnp.float32) * SB_DH ** -0.5
    causal = (kpos[None, :] < qpos[:, None])[None, None]
    log_1m = jnp.where(causal, jax.nn.log_sigmoid(-z), 0.0)
    tail = lax.cumsum(log_1m, axis=3, reverse=True) - log_1m
    a = jnp.where(causal, jnp.exp(jax.nn.log_sigmoid(z) + tail), 0.0)
    return jnp.einsum('bhqk,bkhd->bqhd', a.astype(v.dtype), v)


def _diff_attend(qpos, q, kpos, k, v, lam, bias_tab):
    s = jnp.einsum('bqhid,bkhid->bhiqk', q, k).astype(jnp.float32) * DIFF_DQK ** -0.5
    bias = bias_tab[_t5_bucket(kpos[None, :] - qpos[:, None])]
    s = s + jnp.transpose(bias, (2, 0, 1))[None, :, None].astype(jnp.float32)
    s = jnp.where(_chunk_mask(qpos, kpos)[None, None, None], s, NEG_INF)
    p = jax.nn.softmax(s, axis=-1)
    a = p[:, :, 0] - lam * p[:, :, 1]
    return jnp.einsum('bhqk,bkhd->bqhd', a.astype(v.dtype), v)


def _dsa_attend(qpos, q, qi, wi, kpos, k, v, ki, bias_tab, topk):
    idx_s = jnp.einsum('bqhd,bkd->bqhk', qi, ki).astype(jnp.float32) * IDX_DIM ** -0.5
    score = jnp.einsum('bqh,bqhk->bqk', wi.astype(jnp.float32), jax.nn.relu(idx_s))
    score = jnp.where(_chunk_mask(qpos, kpos)[None], score, NEG_INF)
    _, sel = lax.top_k(score, topk)
    gather = jax.vmap(lambda arr, idx: arr[idx])
    k_sel = gather(k, sel)
    v_sel = gather(v, sel)
    pos_sel = kpos[sel]
    s = jnp.einsum('bqhd,bqthd->bhqt', q, k_sel).astype(jnp.float32) * DSA_DH ** -0.5
    bias = bias_tab[_t5_bucket(pos_sel - qpos[None, :, None])]
    s = s + jnp.transpose(bias, (0, 3, 1, 2)).astype(jnp.float32)
    adm = (_chunk_id(pos_sel) <= _chunk_id(qpos)[None, :, None])[:, None]
    p = jax.nn.softmax(jnp.where(adm, s, NEG_INF), axis=-1).astype(v.dtype)
    return jnp.einsum('bhqt,bqthd->bqhd', p, v_sel)


def _layer(x, qpos, kpos_past, past, layer_idx, topk, rel_bias,
           w_in, qn_g, w_uq, kvn_g, w_uk, w_uv, lam_p, subln_g, w_br, w_out,
           ln1_g, ln1_b, w_ff1, b_ff1, w_ff2, b_ff2, ln2_g, ln2_b):
    bsz, nq, _ = x.shape
    proj = jnp.einsum('bqd,de->bqe', x, w_in)
    (a_cq, a_ckv, a_kpe, b_q, b_k, b_v, c_q, c_k, c_v,
     d_q, d_k, d_v, d_qi, d_ki, d_wi, gates) = jnp.split(proj, IN_SPLITS, axis=-1)
    kpos = qpos if past is None else jnp.concatenate([kpos_past, qpos])

    def keys(new, i):
        return new if past is None else jnp.concatenate([past[i].astype(new.dtype), new], axis=1)

    q_a = jnp.einsum('bqc,ce->bqe', _rms_norm(a_cq, qn_g), w_uq)
    q_a = q_a.reshape(bsz, nq, MLA_HEADS, MLA_NOPE + MLA_ROPE)
    q_lat = jnp.einsum('bqhd,chd->bqhc', q_a[..., :MLA_NOPE], w_uk)
    q_pe = _rope(q_a[..., MLA_NOPE:], qpos)
    ckv_new = _rms_norm(a_ckv, kvn_g)
    kpe_new = _rope(a_kpe, qpos)
    ckv_all, kpe_all = keys(ckv_new, 0), keys(kpe_new, 1)
    o_lat = _query_blocks(lambda qp, ql, qr: _mla_attend(qp, ql, qr, kpos, ckv_all, kpe_all),
                          qpos, q_lat, q_pe)
    o_a = jnp.einsum('bqhc,chd->bqhd', o_lat, w_uv).reshape(bsz, nq, BR_W)

    sb_q = b_q.reshape(bsz, nq, SB_HEADS, SB_DH)
    sb_k_new = b_k.reshape(bsz, nq, SB_HEADS, SB_DH)
    sb_v_new = b_v.reshape(bsz, nq, SB_HEADS, SB_DH)
    sb_k_all, sb_v_all = keys(sb_k_new, 2), keys(sb_v_new, 3)
    o_b = _query_blocks(lambda qp, q: _sb_attend(qp, q, kpos, sb_k_all, sb_v_all),
                        qpos, sb_q).reshape(bsz, nq, BR_W)

    lam_init = 0.8 - 0.6 * math.exp(-0.3 * layer_idx)
    lp = lam_p.astype(jnp.float32)
    lam = jnp.exp(jnp.sum(lp[0] * lp[1])) - jnp.exp(jnp.sum(lp[2] * lp[3])) + lam_init
    df_q = c_q.reshape(bsz, nq, DIFF_HEADS, 2, DIFF_DQK)
    df_k_new = c_k.reshape(bsz, nq, DIFF_HEADS, 2, DIFF_DQK)
    df_v_new = c_v.reshape(bsz, nq, DIFF_HEADS, DIFF_DV)
    df_k_all, df_v_all = keys(df_k_new, 4), keys(df_v_new, 5)
    diff_bias = rel_bias[:, :DIFF_HEADS]
    o_c = _query_blocks(lambda qp, q: _diff_attend(qp, q, kpos, df_k_all, df_v_all, lam, diff_bias),
                        qpos, df_q)
    o_c = (_rms_norm(o_c, subln_g) * (1.0 - lam_init)).reshape(bsz, nq, BR_W)

    ds_q = d_q.reshape(bsz, nq, DSA_HEADS, DSA_DH)
    ds_k_new = d_k.reshape(bsz, nq, DSA_HEADS, DSA_DH)
    ds_v_new = d_v.reshape(bsz, nq, DSA_HEADS, DSA_DH)
    ds_qi = d_qi.reshape(bsz, nq, IDX_HEADS, IDX_DIM)
    ds_ki_new = d_ki
    ds_wi = d_wi * IDX_HEADS ** -0.5
    ds_k_all, ds_v_all, ds_ki_all = keys(ds_k_new, 6), keys(ds_v_new, 7), keys(ds_ki_new, 8)
    dsa_bias = rel_bias[:, DIFF_HEADS:]
    o_d = _query_blocks(
        lambda qp, q, qi, wi: _dsa_attend(qp, q, qi, wi, kpos, ds_k_all, ds_v_all, ds_ki_all, dsa_bias, topk),
        qpos, ds_q, ds_qi, ds_wi).reshape(bsz, nq, BR_W)

    o = jnp.stack([o_a, o_b, o_c, o_d], axis=2)
    br = jnp.einsum('bqnw,nwd->bqnd', o, w_br)
    g = jax.nn.sigmoid(gates.reshape(bsz, nq, N_BRANCH, D_MODEL))
    mix = jnp.einsum('bqd,de->bqe', jnp.sum(g * br, axis=2), w_out)
    x = _layer_norm(DN_ALPHA * x + mix, ln1_g, ln1_b)
    h = jnp.square(jax.nn.relu(jnp.einsum('bqd,df->bqf', x, w_ff1) + b_ff1))
    x = _layer_norm(DN_ALPHA * x + jnp.einsum('bqf,fd->bqd', h, w_ff2) + b_ff2, ln2_g, ln2_b)
    return x, (ckv_new, kpe_new, sb_k_new, sb_v_new, df_k_new, df_v_new, ds_k_new, ds_v_new, ds_ki_new)


def setup_inputs(seed: int = 0) -> dict:
    key = jax.random.key(seed)
    ks = jax.random.split(key, 40)

    def nrm(i, shape, scale):
        return jax.random.normal(ks[i], shape, jnp.float32) * scale

    def gain(i, shape):
        return 1.0 + nrm(i, shape, 0.02)

    c = (DEPTH, DEC_BATCH, PAST_LEN)
    return {
        'x_prompt': nrm(0, (BATCH, SEQ, D_MODEL), 1.0),
        'x_sample': nrm(1, (DEC_BATCH, DEC_SEQ, D_MODEL), 1.0),
        'cache_mla_kv': nrm(2, c + (MLA_KV_LORA,), 1.0),
        'cache_mla_pe': nrm(3, c + (MLA_ROPE,), 1.0),
        'cache_sb_k': nrm(4, c + (SB_HEADS, SB_DH), 1.0),
        'cache_sb_v': nrm(5, c + (SB_HEADS, SB_DH), 1.0),
        'cache_diff_k': nrm(6, c + (DIFF_HEADS, 2, DIFF_DQK), 1.0),
        'cache_diff_v': nrm(7, c + (DIFF_HEADS, DIFF_DV), 1.0),
        'cache_dsa_k': nrm(8, c + (DSA_HEADS, DSA_DH), 1.0),
        'cache_dsa_v': nrm(9, c + (DSA_HEADS, DSA_DH), 1.0),
        'cache_dsa_kidx': nrm(10, c + (IDX_DIM,), 1.0),
        'meta': nrm(11, (N_META, D_MODEL), 1.0),
        'ln_in_g': gain(12, (D_MODEL,)),
        'ln_in_b': nrm(13, (D_MODEL,), 0.02),
        'w_in': nrm(14, (DEPTH, D_MODEL, D_IN), D_MODEL ** -0.5),
        'mla_qnorm_g': gain(15, (DEPTH, MLA_Q_LORA)),
        'mla_w_uq': nrm(16, (DEPTH, MLA_Q_LORA, MLA_HEADS * (MLA_NOPE + MLA_ROPE)), MLA_Q_LORA ** -0.5),
        'mla_kvnorm_g': gain(17, (DEPTH, MLA_KV_LORA)),
        'mla_w_uk': nrm(18, (DEPTH, MLA_KV_LORA, MLA_HEADS, MLA_NOPE), MLA_KV_LORA ** -0.5),
        'mla_w_uv': nrm(19, (DEPTH, MLA_KV_LORA, MLA_HEADS, MLA_V), MLA_KV_LORA ** -0.5),
        'diff_lambda': nrm(20, (DEPTH, 4, DIFF_DQK), 0.1),
        'diff_subln_g': gain(21, (DEPTH, DIFF_DV)),
        'rel_bias': nrm(22, (T5_BUCKETS, DIFF_HEADS + DSA_HEADS), 0.1),
        'w_br': nrm(23, (DEPTH, N_BRANCH, BR_W, D_MODEL), BR_W ** -0.5),
        'w_out': nrm(24, (DEPTH, D_MODEL, D_MODEL), DN_BETA * D_MODEL ** -0.5),
        'ln1_g': gain(25, (DEPTH, D_MODEL)),
        'ln1_b': nrm(26, (DEPTH, D_MODEL), 0.02),
        'w_ff1': nrm(27, (DEPTH, D_MODEL, D_FF), D_MODEL ** -0.5),
        'b_ff1': nrm(28, (DEPTH, D_FF), 0.02),
        'w_ff2': nrm(29, (DEPTH, D_FF, D_MODEL), DN_BETA * D_FF ** -0.5),
        'b_ff2': nrm(30, (DEPTH, D_MODEL), 0.02),
        'ln2_g': gain(31, (DEPTH, D_MODEL)),
        'ln2_b': nrm(32, (DEPTH, D_MODEL), 0.02),
    }


def reference(x_prompt, x_sample, cache_mla_kv, cache_mla_pe, cache_sb_k, cache_sb_v,
              cache_diff_k, cache_diff_v, cache_dsa_k, cache_dsa_v, cache_dsa_kidx,
              meta, ln_in_g, ln_in_b, w_in, mla_qnorm_g, mla_w_uq, mla_kvnorm_g, mla_w_uk, mla_w_uv,
              diff_lambda, diff_subln_g, rel_bias, w_br, w_out, ln1_g, ln1_b,
              w_ff1, b_ff1, w_ff2, b_ff2, ln2_g, ln2_b):
    def run(x, qpos, kpos_past, caches, topk):
        x = _layer_norm(x, ln_in_g, ln_in_b)
        rows = []
        for l in range(DEPTH):
            past = None if caches is None else tuple(cc[l] for cc in caches)
            x, new = _layer(x, qpos, kpos_past, past, l, topk, rel_bias,
                            w_in[l], mla_qnorm_g[l], mla_w_uq[l], mla_kvnorm_g[l], mla_w_uk[l], mla_w_uv[l],
                            diff_lambda[l], diff_subln_g[l], w_br[l], w_out[l], ln1_g[l], ln1_b[l],
                            w_ff1[l], b_ff1[l], w_ff2[l], b_ff2[l], ln2_g[l], ln2_b[l])
            rows.append(new)
        return x, [jnp.stack([r[i] for r in rows], axis=0) for i in range(len(rows[0]))]

    bsz_p, seq_p, _ = x_prompt.shape
    meta_b = jnp.broadcast_to(meta[None].astype(x_prompt.dtype), (bsz_p, N_META, D_MODEL))
    xp = jnp.concatenate([meta_b, x_prompt], axis=1)
    pos_p = jnp.arange(N_META + seq_p, dtype=jnp.int32)
    yp, (p_mla_kv, p_mla_pe, p_sb_k, p_sb_v, p_diff_k, p_diff_v,
         p_dsa_k, p_dsa_v, p_dsa_kidx) = run(xp, pos_p, None, None, min(DSA_TOPK, SEQ // 4))
    y_prompt = yp[:, N_META:]

    past_len = cache_mla_kv.shape[2]
    dec_seq = x_sample.shape[1]
    pos_past = N_META + jnp.arange(past_len, dtype=jnp.int32)
    pos_s = N_META + past_len + jnp.arange(dec_seq, dtype=jnp.int32)
    caches = (cache_mla_kv, cache_mla_pe, cache_sb_k, cache_sb_v, cache_diff_k, cache_diff_v,
              cache_dsa_k, cache_dsa_v, cache_dsa_kidx)
    y_sample, (s_mla_kv, s_mla_pe, s_sb_k, s_sb_v, s_diff_k, s_diff_v,
               s_dsa_k, s_dsa_v, s_dsa_kidx) = run(x_sample, pos_s, pos_past, caches,
                                                   min(DSA_TOPK, (past_len + dec_seq) // 4))

    return (y_prompt, y_sample,
            p_mla_kv, p_mla_pe, p_sb_k, p_sb_v, p_diff_k, p_diff_v, p_dsa_k, p_dsa_v, p_dsa_kidx,
            s_mla_kv, s_mla_pe, s_sb_k, s_sb_v, s_diff_k, s_diff_v, s_dsa_k, s_dsa_v, s_dsa_kidx)
```

```python
import math
import contextlib
import numpy as np
import concourse.bass as bass
import concourse.mybir as mybir
from concourse.bass_utils import run_bass_kernel_spmd

F32 = mybir.dt.float32
BF16 = mybir.dt.bfloat16
AF = mybir.ActivationFunctionType
ALU = mybir.AluOpType
AX = mybir.AxisListType

D = 1024
KC = 8
NMETA = 16
CHUNK = 64
H = 4
D_IN = 7112
DFF = 4096
GATE0 = 3016
LN_EPS = 1e-5
RMS_EPS = 1e-6
NBIS = 16
NEGBIG = -1.0e30

CACHE_NAMES = ["mla_kv", "mla_pe", "sb_k", "sb_v", "diff_k", "diff_v", "dsa_k", "dsa_v", "dsa_kidx"]
CACHE_F = [128, 32, 256, 256, 256, 256, 256, 256, 32]


class Cfg:
    def __init__(self, SEQ=2048, NB=2, DEC=32, NS=4, PAST=1024, DEPTH=2, BATCH=16, DEC_BATCH=32):
        self.SEQ, self.NB, self.DEC, self.NS, self.PAST, self.DEPTH = SEQ, NB, DEC, NS, PAST, DEPTH
        self.BATCH, self.DEC_BATCH = BATCH, DEC_BATCH
        self.TT = SEQ + NMETA
        self.TOPK_P = min(256, SEQ // 4)
        self.TOPK_S = min(256, (PAST + DEC) // 4)
        self.DN_ALPHA = (2 * DEPTH) ** 0.25
        self.NCORES = BATCH // NB
        assert DEC_BATCH // NS == self.NCORES


class Sched:
    def __init__(s, nc, st):
        s.nc = nc
        s.E = dict(pe=nc.tensor, act=nc.scalar, dve=nc.vector, pool=nc.gpsimd, sp=nc.sync)
        s.csem = {e: st.enter_context(nc.semaphore("c_" + e)) for e in ("pe", "act", "dve", "pool")}
        s.ccnt = {e: 0 for e in s.csem}
        s.NQ = 8
        s.qsem = {q: [st.enter_context(nc.semaphore("q_%s%d" % (q, i))) for i in range(s.NQ)]
                  for q in ("sp", "pool")}
        s.qn = {q: 0 for q in s.qsem}
        s.qsb = {q: [0] * s.NQ for q in s.qsem}
        s.seen = {}
        s.lastw = {}
        s.readers = {}
        s.nins = 0

    def _wait(s, eng, tok):
        key, h, val = tok
        k = (eng, key)
        if s.seen.get(k, 0) >= val:
            return
        s.E[eng].wait_ge(h, val)
        s.seen[k] = val
        s.nins += 1

    def _deps(s, eng, reads, writes):
        own = ("c", eng)
        for r in reads:
            w = s.lastw.get(r)
            if w is not None and not (w[0] == own and eng == "pe"):
                s._wait(eng, w)
        for r in writes:
            w = s.lastw.get(r)
            if w is not None and not (w[0] == own and eng == "pe"):
                s._wait(eng, w)
            for t in s.readers.get(r, {}).values():
                if not (t[0] == own and eng == "pe"):
                    s._wait(eng, t)

    def _commit(s, tok, reads, writes, rk):
        for r in writes:
            s.lastw[r] = tok
            s.readers[r] = {}
        for r in reads:
            s.readers.setdefault(r, {})[rk] = tok

    def op(s, eng, fn, reads=(), writes=()):
        pr = [r for r in reads if isinstance(r, tuple) and r[0] in ("psS", "psO")]
        if pr:
            writes = list(writes) + [r for r in pr if r not in writes]
        s._deps(eng, reads, writes)
        ins = fn(s.E[eng])
        s.ccnt[eng] += 1
        ins.then_inc(s.csem[eng], 1)
        s.nins += 1
        tok = (("c", eng), s.csem[eng], s.ccnt[eng])
        s._commit(tok, reads, writes, eng)

    def dma(s, q, out, in_, reads=(), writes=(), dram_only=False, **kw):
        s._deps(q, reads, writes)
        i = s.qn[q]
        slot = i % s.NQ
        gen = i // s.NQ
        h = s.qsem[q][slot]
        key = ("q", q, slot)
        if gen > 0:
            s._wait(q, (key, h, 16 * gen))
        s.E[q].dma_start(out=out, in_=in_, **kw).then_inc(h, 16)
        s.nins += 1
        s.qn[q] += 1
        tok = (key, h, 16 * (gen + 1))
        if not dram_only:
            s.qsb[q][slot] = 16 * (gen + 1)
        s._commit(tok, reads, writes, key)

    def all_tokens(s):
        toks = []
        for e in s.csem:
            if s.ccnt[e] > 0:
                toks.append((("c", e), s.csem[e], s.ccnt[e]))
        for q in s.qsem:
            for slot in range(s.NQ):
                n = s.qn[q]
                cnt = n // s.NQ + (1 if slot < n % s.NQ else 0)
                if cnt > 0:
                    toks.append((("q", q, slot), s.qsem[q][slot], 16 * cnt))
        return toks

    def barrier(s):
        toks = []
        for e in s.csem:
            if s.ccnt[e] > 0:
                toks.append((("c", e), s.csem[e], s.ccnt[e]))
        for q in s.qsem:
            for slot in range(s.NQ):
                if s.qsb[q][slot] > 0:
                    toks.append((("q", q, slot), s.qsem[q][slot], s.qsb[q][slot]))
        for eng in ("pe", "act", "dve", "pool", "sp"):
            for t in toks:
                s._wait(eng, t)
        s.lastw = {k: v for k, v in s.lastw.items() if isinstance(k, tuple) and k[0] in ("wb", "cb")}
        s.readers = {}

    def finish(s):
        for t in s.all_tokens():
            s._wait("sp", t)

    def mm(s, out, lhsT, rhs, start, stop, r, w, skip=False):
        if skip:
            s.op("pe", lambda e: e.matmul(out, lhsT=lhsT, rhs=rhs, start=start, stop=stop, skip_group_check=True), r, w)
        else:
            s.op("pe", lambda e: e.matmul(out, lhsT=lhsT, rhs=rhs, start=start, stop=stop), r, w)

    def act(s, out, in_, func, r, w, bias=0.0, scale=1.0, accum_out=None):
        if accum_out is None:
            s.op("act", lambda e: e.activation(out=out, in_=in_, func=func, bias=bias, scale=scale), r, w)
        else:
            s.op("act", lambda e: e.activation(out=out, in_=in_, func=func, bias=bias, scale=scale,
                                               accum_out=accum_out), r, w)

    def tt(s, eng, out, in0, in1, op, r, w):
        s.op(eng, lambda e: e.tensor_tensor(out=out, in0=in0, in1=in1, op=op), r, w)

    def ts(s, eng, out, in0, s1, s2, op0, op1, r, w, accum_out=None):
        if op1 is None:
            s.op(eng, lambda e: e.tensor_scalar(out=out, in0=in0, scalar1=s1, scalar2=None, op0=op0), r, w)
        elif accum_out is None:
            s.op(eng, lambda e: e.tensor_scalar(out=out, in0=in0, scalar1=s1, scalar2=s2, op0=op0, op1=op1), r, w)
        else:
            s.op(eng, lambda e: e.tensor_scalar(out=out, in0=in0, scalar1=s1, scalar2=s2, op0=op0, op1=op1,
                                                accum_out=accum_out), r, w)

    def stt(s, eng, out, in0, scalar, in1, op0, op1, r, w):
        s.op(eng, lambda e: e.scalar_tensor_tensor(out=out, in0=in0, scalar=scalar, in1=in1, op0=op0, op1=op1), r, w)

    def copy(s, eng, out, in_, r, w):
        if eng == "act":
            s.op("act", lambda e: e.activation(out=out, in_=in_, func=AF.Copy), r, w)
        else:
            s.op(eng, lambda e: e.tensor_copy(out=out, in_=in_), r, w)

    def memset(s, eng, ap, val, w):
        s.op(eng, lambda e: e.memset(ap, val), (), w)

    def reduce(s, out, in_, op, r, w):
        s.op("dve", lambda e: e.tensor_reduce(out=out, in_=in_, axis=AX.X, op=op), r, w)

    def recip(s, out, in_, r, w):
        s.op("dve", lambda e: e.reciprocal(out=out, in_=in_), r, w)


def _t5_bucket_np(rel):
    import jax
    import jax.numpy as jnp
    with jax.default_device(jax.devices("cpu")[0]):
        return _t5_bucket_impl(jnp, rel)


def _t5_bucket_impl(jnp, rel):
    rel = jnp.asarray(np.asarray(rel), dtype=jnp.int32)
    nb = 16
    max_exact = 8
    n = jnp.abs(rel)
    nf = jnp.maximum(n, 1).astype(jnp.float32)
    large = max_exact + (jnp.log(nf / max_exact) / math.log(128 / max_exact) * (nb - max_exact)).astype(jnp.int32)
    large = jnp.minimum(large, nb - 1)
    return np.asarray(jnp.where(rel > 0, nb, 0) + jnp.where(n < max_exact, n, large))


def host_constants(cfg):
    c = {}
    c["c_ident"] = np.eye(128, dtype=np.float32)
    c["c_J"] = np.eye(128, dtype=np.float32)[::-1].copy()
    j = np.arange(128)
    c["c_utri"] = (j[:, None] >= j[None, :]).astype(np.float32)
    c["c_cmask"] = (j[:, None] < j[None, :]).astype(np.float32)
    c["c_ones"] = np.ones((128, 128), np.float32)
    p = np.arange(128)
    mA = ((p // 32) % 2 == 0).astype(np.float32)
    c["c_mab"] = np.stack([mA, 1.0 - mA], axis=1).astype(np.float32)
    rel = 127 - np.arange(384)
    b = _t5_bucket_np(rel)
    oh = np.zeros((32, 384), np.float32)
    oh[b, np.arange(384)] = 1.0
    c["c_oh"] = oh
    bf = int(_t5_bucket_np(np.array([-1000]))[0])
    ohf = np.zeros((32, 128), np.float32)
    ohf[bf, :] = 1.0
    c["c_ohfar"] = ohf
    c["c_pow2"] = np.tile((0.5 ** (np.arange(NBIS) + 1)).astype(np.float32)[None], (128, 1))
    half = 16
    inv = (10000.0 ** (-np.arange(half, dtype=np.float32) / half)).astype(np.float32)

    def tabs(pos, scale):
        ang = pos.astype(np.float32)[:, None] * inv[None, :]
        cs = np.cos(ang).astype(np.float32)
        sn = np.sin(ang).astype(np.float32)
        cos2 = np.concatenate([cs, cs], axis=1) * scale
        sin2 = np.concatenate([-sn, sn], axis=1) * scale
        return np.concatenate([cos2, sin2], axis=1).astype(np.float32)

    pos_p = np.concatenate([NMETA + np.arange(cfg.SEQ), np.arange(NMETA)])
    pos_s = np.tile(NMETA + cfg.PAST + np.arange(cfg.DEC), cfg.NS)
    qs = 96.0 ** -0.5
    c["c_rope_p"] = np.concatenate([tabs(pos_p, 1.0), tabs(pos_p, qs)], axis=1)
    c["c_rope_s"] = np.concatenate([tabs(pos_s, 1.0), tabs(pos_s, qs)], axis=1)
    return c


class Group:
    pass


class _Stop(Exception):
    pass


def _stop(tag):
    import os
    return os.environ.get("KSTOP") == tag


_UID = [0]


def build(cfg):
    nc = bass.Bass("TRN2", target_bir_lowering=False)
    SEQ, NB, DEC, NS, PAST, DEPTH, TT = cfg.SEQ, cfg.NB, cfg.DEC, cfg.NS, cfg.PAST, cfg.DEPTH, cfg.TT
    NPT = PAST // 128
    ALPHA = cfg.DN_ALPHA
    dr = {}

    def din(name, shape):
        dr[name] = nc.dram_tensor(name, list(shape), F32, kind="ExternalInput").ap()

    def dout(name, shape):
        dr[name] = nc.dram_tensor(name, list(shape), F32, kind="ExternalOutput").ap()

    din("xp", [NB, SEQ, D])
    din("xs", [NS * DEC, D])
    for n, f in zip(CACHE_NAMES, CACHE_F):
        din("c_" + n, [DEPTH, NS, PAST, f])
    din("meta", [NMETA, D])
    din("ln_in_g", [1, D]); din("ln_in_b", [1, D])
    din("w_in", [DEPTH, D, D_IN])
    din("qn_g", [DEPTH, 256]); din("w_uq", [DEPTH, 256, 384]); din("kvn_g", [DEPTH, 128])
    din("w_uk", [DEPTH, 128, 256]); din("w_uv", [DEPTH, 128, 256])
    din("lam_p", [DEPTH, 128]); din("subln_g", [DEPTH, 64]); din("rel_bias", [32, 8])
    din("w_br", [DEPTH, 4, 256, D]); din("w_out", [DEPTH, D, D])
    din("ln1_g", [DEPTH, D]); din("ln1_b", [DEPTH, D])
    din("w_ff1", [DEPTH, D, DFF]); din("b_ff1_t", [DEPTH, 128, 32]); din("w_ff2", [DEPTH, DFF, D])
    din("b_ff2", [DEPTH, D]); din("ln2_g", [DEPTH, D]); din("ln2_b", [DEPTH, D])
    for n, shp in (("c_ident", [128, 128]), ("c_J", [128, 128]), ("c_utri", [128, 128]), ("c_cmask", [128, 128]),
                   ("c_ones", [128, 128]), ("c_mab", [128, 2]), ("c_oh", [32, 384]), ("c_ohfar", [32, 128]),
                   ("c_pow2", [128, NBIS]), ("c_rope_p", [TT, 128]), ("c_rope_s", [NS * DEC, 128])):
        din(n, shp)
    dout("y_p", [NB, SEQ, D])
    dout("y_s", [NS * DEC, D])
    for n, f in zip(CACHE_NAMES, CACHE_F):
        dout("p_" + n, [DEPTH, NB, TT, f])
        dout("s_" + n, [DEPTH, NS * DEC, f])
    bias_scr = nc.dram_tensor("bias_scr", [8, 384], F32, kind="Internal").ap()
    WB = {}
    for nm, shp in (("w_in", [DEPTH, D, D_IN]), ("w_uq", [DEPTH, 256, 384]), ("w_uk", [DEPTH, 128, 256]),
                    ("w_uv", [DEPTH, 128, 256]), ("w_br", [DEPTH, 4 * 256, D]), ("w_out", [DEPTH, D, D]),
                    ("w_ff1", [DEPTH, D, DFF]), ("w_ff2", [DEPTH, DFF, D])):
        WB[nm] = nc.dram_tensor("wb_" + nm, list(shp), BF16, kind="Internal").ap()
    dr["WB"] = WB
    CB = {}
    for n, f in zip(CACHE_NAMES, CACHE_F):
        CB[n] = nc.dram_tensor("cb_" + n, [DEPTH, NS, PAST, f], BF16, kind="Internal").ap()
    dr["CB"] = CB
    xres_scr = nc.dram_tensor("xres_scr", [TT, D], F32, kind="Internal").ap()

    st = contextlib.ExitStack()
    with st:
        S = Sched(nc, st)

        def sb(name, shape, dt, stack=st):
            _UID[0] += 1
            return stack.enter_context(nc.sbuf_tensor("%s_%d" % (name, _UID[0]), list(shape), dt))

        psS = [st.enter_context(nc.psum_tensor("psS%d" % i, [128, 512], F32)) for i in range(5)]
        psO = [st.enter_context(nc.psum_tensor("psO%d" % i, [128, 512], F32)) for i in range(3)]
        pctr = {"S": 0, "O": 0}

        def ps_s():
            i = pctr["S"] % len(psS); pctr["S"] += 1
            return psS[i], ("psS", i)

        def ps_o():
            i = pctr["O"] % len(psO); pctr["O"] += 1
            return psO[i], ("psO", i)

        ident = sb("ident", [128, 128], BF16)
        utri = sb("utri", [128, 128], BF16)
        cmask = sb("cmask", [128, 128], BF16)
        ones = sb("ones", [128, 128], BF16)
        mab = sb("mab", [128, 2], F32)
        pow2 = sb("pow2", [128, NBIS], F32)
        BT = sb("BT", [128, 8, 256], F32)
        cbias = sb("cbias", [128, 8], F32)
        cst = sb("cst", [128, 4], F32)
        S.memset("dve", cst[:, 0:1], LN_EPS, ["cst"])
        S.memset("dve", cst[:, 1:2], RMS_EPS, ["cst"])
        S.memset("dve", cst[:, 2:3], 1.0, ["cst"])
        S.dma("pool", ident[:], dr["c_ident"], (), ["ident"])
        S.dma("pool", utri[:], dr["c_utri"], (), ["utri"])
        S.dma("pool", cmask[:], dr["c_cmask"], (), ["cmask"])
        S.dma("pool", ones[:], dr["c_ones"], (), ["ones"])
        S.dma("sp", mab[:], dr["c_mab"], (), ["mab"])
        S.dma("sp", pow2[:], dr["c_pow2"], (), ["pow2"])
        with contextlib.ExitStack() as st0:
            Jt = sb("Jt", [128, 128], F32, st0)
            relb = sb("relb", [32, 8], F32, st0)
            oh = sb("oh", [32, 384], F32, st0)
            ohf = sb("ohf", [32, 128], F32, st0)
            arev = sb("arev", [8, 384], F32, st0)
            Gt = sb("Gt", [128, 8, 256], F32, st0)
            S.dma("sp", Jt[:], dr["c_J"], (), ["Jt"])
            S.dma("sp", relb[:], dr["rel_bias"], (), ["relb"])
            S.dma("sp", oh[:], dr["c_oh"], (), ["oh"])
            S.dma("sp", ohf[:], dr["c_ohfar"], (), ["ohf"])
            p, pk = ps_s()
            S.mm(p[0:8, 0:384], relb[:], oh[:], True, True, ["relb", "oh"], [pk])
            S.copy("act", arev[:], p[0:8, 0:384], [pk], ["arev"])
            S.dma("sp", bias_scr, arev[:], ["arev"], ["bias_scr"])
            src = bass.AP(tensor=bias_scr.tensor, offset=0, ap=[[1, 128], [384, 8], [1, 256]])
            S.dma("sp", Gt[:], src, ["bias_scr"], ["Gt"])
            for h in range(8):
                p, pk = ps_s()
                S.mm(p[:, 0:256], Jt[:], Gt[:, h, :], True, True, ["Jt", "Gt"], [pk])
                S.copy("act", BT[:, h, :], p[:, 0:256], [pk], ["BT"])
            p, pk = ps_s()
            S.mm(p[:, 0:8], ohf[:], relb[:], True, True, ["ohf", "relb"], [pk])
            S.copy("act", cbias[:], p[:, 0:8], [pk], ["cbias"])
            S.barrier()

        def conv(nm, l, rows_per, deps=()):
            src = dr[nm][l] if nm != "w_br" else dr[nm][l].rearrange("n w e -> (n w) e")
            dst = WB[nm][l]
            nrow = dst.shape[0]
            for r0 in range(0, nrow, rows_per):
                S.dma("pool", dst[r0:r0 + rows_per, :], src[r0:r0 + rows_per, :], list(deps), [("wb", nm, l)], dram_only=True)

        conv("w_in", 0, 128)
        conv("w_uq", 0, 256); conv("w_uk", 0, 128); conv("w_uv", 0, 128)
        first = [("wb", "w_in", 0), ("wb", "w_uq", 0), ("wb", "w_uk", 0), ("wb", "w_uv", 0)]
        conv("w_br", 0, 256, first); conv("w_out", 0, 256, first); conv("w_ff1", 0, 128, first); conv("w_ff2", 0, 512, first)
        for l_ in range(1, DEPTH):
            conv("w_in", l_, 128, first)
            conv("w_uq", l_, 256, first); conv("w_uk", l_, 128, first); conv("w_uv", l_, 128, first)
            conv("w_br", l_, 256, first); conv("w_out", l_, 256, first); conv("w_ff1", l_, 128, first); conv("w_ff2", l_, 512, first)
        for l_ in range(DEPTH):
            for n_ in CACHE_NAMES:
                for s_ in range(NS):
                    S.dma("pool", CB[n_][l_, s_], dr["c_" + n_][l_, s_], first, [("cb", n_, l_)], dram_only=True)

        xT = sb("xT", [128, KC, TT], BF16)
        oT = sb("oT", [128, KC, TT], BF16)
        if not _stop("const"):
            _main(locals())
        S.finish()
    return nc, S


def _main(L):
    (nc, S, cfg, dr, st, sb, xT, oT, ident, utri, cmask, ones, mab, pow2, BT, cbias, cst, ps_s, ps_o,
     xres_scr) = [L[k] for k in ("nc", "S", "cfg", "dr", "st", "sb", "xT", "oT", "ident", "utri", "cmask", "ones", "mab",
                                 "pow2", "BT", "cbias", "cst", "ps_s", "ps_o", "xres_scr")]
    first_pa = [False]
    SEQ, NB, DEC, NS, PAST, DEPTH, TT = cfg.SEQ, cfg.NB, cfg.DEC, cfg.NS, cfg.PAST, cfg.DEPTH, cfg.TT
    if True:

        def r_xT(c0):
            return ("xT", c0 // 128)

        def r_oT(n, c0):
            return ("oT", n, c0 // 128)

        groups = []
        for b in range(NB):
            g = Group()
            g.kind = "p"; g.idx = b; g.ntok = TT
            g.tiles = [(128 * i, 128) for i in range(SEQ // 128)] + [(SEQ, NMETA)]
            g.ln_tiles = list(g.tiles)
            g.blocks = [(512 * i, 512) for i in range(SEQ // 512)] + [(SEQ, NMETA)]
            g.rope = dr["c_rope_p"]
            g.topk = cfg.TOPK_P
            groups.append(g)
        g = Group()
        g.kind = "s"; g.idx = 0; g.ntok = NS * DEC
        g.tiles = [(DEC * s_, DEC) for s_ in range(NS)]
        g.ln_tiles = [(0, NS * DEC)]
        g.blocks = [(0, NS * DEC)]
        g.rope = dr["c_rope_s"]
        g.topk = cfg.TOPK_S
        groups.append(g)

        def out_rows(g, name, l, c0, rows):
            if g.kind == "p":
                t = dr["p_" + name]
                if c0 >= SEQ:
                    return t[l, g.idx, 0:rows, :]
                return t[l, g.idx, NMETA + c0:NMETA + c0 + rows, :]
            return dr["s_" + name][l, c0:c0 + rows, :]

        def ln_stats(z, rows, zkey, wk):
            st_ = wk["st"]
            S.memset("dve", st_[0:rows, 0:8], 0.0, [wk["stk"]])
            S.act(wk["junk"][0:rows, :], z, AF.Copy, [zkey, wk["stk"]], [wk["junkk"], wk["stk"] + "a"],
                  accum_out=st_[0:rows, 0:1])
            S.act(wk["junk"][0:rows, :], z, AF.Square, [zkey, wk["stk"]], [wk["junkk"], wk["stk"] + "b"],
                  accum_out=st_[0:rows, 1:2])
            rd = [wk["stk"], wk["stk"] + "a", wk["stk"] + "b"]
            S.ts("dve", st_[0:rows, 2:3], st_[0:rows, 0:1], 1.0 / D, None, ALU.mult, None, rd, [wk["stk"] + "c"])
            S.stt("dve", st_[0:rows, 3:4], st_[0:rows, 2:3], -1.0, st_[0:rows, 2:3], ALU.mult, ALU.mult,
                  [wk["stk"] + "c"], [wk["stk"] + "d"])
            S.stt("dve", st_[0:rows, 4:5], st_[0:rows, 1:2], 1.0 / D, st_[0:rows, 3:4], ALU.mult, ALU.add,
                  rd + [wk["stk"] + "d"], [wk["stk"] + "e"])
            S.act(st_[0:rows, 5:6], st_[0:rows, 4:5], AF.Ln, [wk["stk"] + "e", "cst"], [wk["stk"] + "f"],
                  bias=cst[0:rows, 0:1])
            S.act(st_[0:rows, 5:6], st_[0:rows, 5:6], AF.Exp, [wk["stk"] + "f"], [wk["stk"] + "f"], scale=-0.5)

        def ln_apply(z, rows, zkey, gt, bt, gkeys, out, okey, wk, beng="dve"):
            st_ = wk["st"]
            S.ts("dve", out, z, st_[0:rows, 2:3], st_[0:rows, 5:6], ALU.subtract, ALU.mult,
                 [zkey, wk["stk"] + "c", wk["stk"] + "f"], [okey])
            S.tt("dve", out, out, gt[0:rows, :], ALU.mult, [okey] + gkeys, [okey])
            S.tt(beng, out, out, bt[0:rows, :], ALU.add, [okey] + gkeys, [okey])

        def layer_norm(z, rows, zkey, gt, bt, gkeys, out, okey, wk):
            ln_stats(z, rows, zkey, wk)
            ln_apply(z, rows, zkey, gt, bt, gkeys, out, okey, wk)

        def ln_pipeline(items, gt, bt, gkeys, after, beng="dve", ceng="act"):
            n = len(items)
            for t in range(n + 2):
                if t < n:
                    ap, rows, key, c0, wk, ex = items[t]
                    ln_stats(ap, rows, key, wk)
                if 1 <= t <= n:
                    ap, rows, key, c0, wk, ex = items[t - 1]
                    ln_apply(ap, rows, key, gt, bt, gkeys, ap, key, wk, beng)
                    do_x = after(items[t - 1])
                    items[t - 1] = items[t - 1] + (do_x,)
                if 2 <= t:
                    it_ = items[t - 2]
                    if it_[6]:
                        to_xT(it_[0], it_[1], it_[2], it_[3], it_[4], ceng=ceng)

        def to_xT(src, rows, skey, c0, wk, dst=None, dkeyf=None, ceng="act"):
            dst = xT if dst is None else dst
            dkey = r_xT(c0) if dkeyf is None else dkeyf
            xb = wk["xb"]
            S.copy(ceng, xb[0:rows, :], src, [skey], [wk["xbk"]])
            for hf in range(2):
                p, pk = ps_s()
                for j in range(4):
                    kc = hf * 4 + j
                    S.mm(p[:, j * 128:j * 128 + rows], xb[0:rows, kc * 128:(kc + 1) * 128], ident[0:rows, 0:rows],
                         True, True, [wk["xbk"], "ident"], [pk])
                pv = p[:].rearrange("p (j t) -> p j t", j=4)[:, :, 0:rows]
                S.copy("dve" if hf == 0 else "act", dst[:, hf * 4:hf * 4 + 4, c0:c0 + rows], pv, [pk], [dkey])

        def bcast_row(dst, src_row, key, q="sp"):
            S.dma(q, dst, src_row.partition_broadcast(128) if len(src_row.shape) == 1 else
                  src_row.broadcast_to([128] + list(src_row.shape[1:])), (), [key])

        for g in groups:
            with contextlib.ExitStack() as s0:
                gt = sb("lng", [128, D], F32, s0)
                bt = sb("lnb", [128, D], F32, s0)
                zt = [sb("z0_%d" % i, [128, D], F32, s0) for i in range(3)]
                ot = [sb("o0_%d" % i, [128, D], F32, s0) for i in range(3)]
                junk0 = sb("junk0", [128, D], BF16, s0)
                wks = [dict(st=sb("st0%d" % i, [128, 8], F32, s0), stk="st0%d" % i, junk=junk0,
                            junkk="junk0", xb=sb("xb0%d" % i, [128, D], BF16, s0), xbk="xb0%d" % i) for i in range(3)]
                bcast_row(gt[:], dr["ln_in_g"], "lng")
                bcast_row(bt[:], dr["ln_in_b"], "lnb")
                pend = None
                for ti, (c0, rows) in enumerate(g.ln_tiles):
                    z = zt[ti % 3]; o = ot[ti % 3]
                    zk = "z0_%d" % (ti % 3); ok = "o0_%d" % (ti % 3)
                    if g.kind == "p":
                        src = dr["meta"] if c0 >= SEQ else dr["xp"][g.idx, c0:c0 + rows, :]
                    else:
                        src = dr["xs"][c0:c0 + rows, :]
                    S.dma("sp", z[0:rows, :], src, (), [zk])
                    wk = wks[ti % 3]
                    layer_norm(z[0:rows, :], rows, zk, gt, bt, ["lng", "lnb"], o[0:rows, :], ok, wk)
                    S.dma("sp", xres_scr[c0:c0 + rows, :], o[0:rows, :], [ok], [("xres", c0 // 128)])
                    if pend is not None:
                        to_xT(*pend)
                    pend = (o[0:rows, :], rows, ok, c0, wk)
                if pend is not None:
                    to_xT(*pend)
                S.barrier()
                if _stop("s0"):
                    return

            for l in range(DEPTH):
                cb_ = None
                if phase_a(nc, S, cfg, dr, g, l, xT, oT, ident, utri, cmask, ones, mab, pow2, BT, cbias,
                           ps_s, ps_o, out_rows, cst, cb_):
                    return
                S.barrier()
                if _stop("pa"):
                    return
                phase_b(nc, S, cfg, dr, g, l, xT, oT, ident, ps_s, ps_o, xres_scr, layer_norm, to_xT, bcast_row, ln_pipeline)
                S.barrier()
                if _stop("pb"):
                    return


def phase_a(nc, S, cfg, dr, g, l, xT, oT, ident, utri, cmask, ones, mab, pow2, BT, cbias, ps_s, ps_o, out_rows, cst, conv_cb=None):
    SEQ, NB, DEC, NS, PAST, DEPTH, TT = cfg.SEQ, cfg.NB, cfg.DEC, cfg.NS, cfg.PAST, cfg.DEPTH, cfg.TT
    NPT = PAST // 128
    NT = len(g.tiles)
    isP = g.kind == "p"
    TTg = g.ntok
    KW = TT if isP else max(PAST + DEC, 128)
    WB = dr["WB"]
    CB = dr["CB"]
    direct = conv_cb is not None
    if direct:
        w_in = dr["w_in"][l].rearrange("(kc p) e -> p kc e", p=128)
        wsrc = lambda nm: dr[nm][l]
        wq, wdep = "pool", lambda nm: []
    else:
        w_in = WB["w_in"][l].rearrange("(kc p) e -> p kc e", p=128)
        wsrc = lambda nm: WB[nm][l]
        wq, wdep = "sp", lambda nm: [("wb", nm, l)]
    lam_init = 0.8 - 0.6 * math.exp(-0.3 * l)
    stA = contextlib.ExitStack()

    def sb(name, shape, dt, stack=None):
        _UID[0] += 1
        return (stack or stA).enter_context(nc.sbuf_tensor("%s_%d" % (name, _UID[0]), list(shape), dt))

    def r_oT(n, c0):
        return ("oT", n, c0 // 128)

    units = []
    if isP:
        u = Group()
        u.sidx = 0
        u.qblocks = []
        for qb in range(SEQ // 512):
            u.qblocks.append((512 * qb, 512, 512 * qb, [(128 * a, 128, 4 * qb + a) for a in range(4)]))
        u.qblocks.append((SEQ, NMETA, -NMETA, [(0, NMETA, SEQ // 128)]))
        u.keys = [dict(own=True, c0=SEQ, nk=NMETA, fk0=-NMETA, vi=SEQ // 128, sc0=0)]
        for kt in range(SEQ // 128):
            u.keys.append(dict(own=True, c0=128 * kt, nk=128, fk0=128 * kt, vi=kt, sc0=NMETA + 128 * kt))
        units.append(u)
    else:
        for s_ in range(NS):
            u = Group()
            u.sidx = s_
            u.qblocks = [(DEC * s_, DEC, PAST, [(0, DEC, s_)])]
            u.keys = [dict(own=False, c0=128 * j, nk=128, fk0=128 * j, vi=j, sc0=128 * j) for j in range(NPT)]
            u.keys.append(dict(own=True, c0=DEC * s_, nk=DEC, fk0=PAST, vi=s_, sc0=PAST))
            units.append(u)

    def visible(kt, fq_lo, fq_hi, causal):
        fk0 = kt["fk0"]
        if fk0 < 0:
            return fq_lo
        if fq_lo < 0:
            return None
        if fk0 >= fq_hi:
            return None
        return max(fq_lo, fk0)

    def load_w(dst, c0, n, key, dcol=0):
        S.dma(wq, dst[:, :, dcol:dcol + n], w_in[:, :, c0:c0 + n], wdep("w_in"), [key])

    def proj_fm(wt, wkey, wc0, M, c0, n):
        p, pk = ps_s()
        for kc in range(KC):
            S.mm(p[0:M, 0:n], wt[:, kc, wc0:wc0 + M], xT[:, kc, c0:c0 + n], kc == 0, kc == KC - 1,
                 [wkey, ("xT", c0 // 128)] + ([("xT", (c0 + n - 1) // 128)] if n > 128 else []), [pk])
        return p, pk

    def xT_keys(c0, n):
        return [("xT", i) for i in range(c0 // 128, (c0 + n - 1) // 128 + 1)]

    def proj_fm2(wt, wkey, wc0, M, c0, n):
        p, pk = ps_s()
        rk = [wkey] + xT_keys(c0, n)
        for kc in range(KC):
            S.mm(p[0:M, 0:n], wt[:, kc, wc0:wc0 + M], xT[:, kc, c0:c0 + n], kc == 0, kc == KC - 1, rk, [pk])
        return p, pk

    def proj_tm(wt, wkey, wc0, ncols, c0, rows):
        p, pk = ps_s()
        rk = [wkey] + xT_keys(c0, rows)
        for kc in range(KC):
            S.mm(p[0:rows, 0:ncols], xT[:, kc, c0:c0 + rows], wt[:, kc, wc0:wc0 + ncols], kc == 0, kc == KC - 1,
                 rk, [pk])
        return p, pk

    def transpose_into(dst_ap, dkey, src_ap, skey, rows, fcols, eng="dve", ps=None):
        p, pk = ps if ps is not None else ps_s()
        S.mm(p[0:fcols, 0:rows], src_ap, ident[0:rows, 0:rows], True, True, [skey, "ident"], [pk])
        S.copy(eng, dst_ap, p[0:fcols, 0:rows], [pk], [dkey])

    rot = {}
    cur = [stA]

    def rot_buf(name, n, shape, dt):
        if name not in rot:
            rot[name] = [[sb("%s%d" % (name, i), shape, dt, cur[0]) for i in range(n)], 0]
        lst, i = rot[name]
        rot[name][1] = i + 1
        return lst[i % n], "%s%d" % (name, i % n)

    def finalize_o(ostage, okey, n):
        for ti, (c0, rows) in enumerate(g.tiles):
            p, pk = ps_s()
            for wc in range(2):
                S.mm(p[:, wc * 128:wc * 128 + rows], ostage[0:rows, ti, wc * 128:(wc + 1) * 128],
                     ident[0:rows, 0:rows], True, True, [(okey, ti), "ident"], [pk])
            pv = p[:, 0:256].rearrange("p (w t) -> p w t", w=2)[:, :, 0:rows]
            S.copy("act" if ti % 2 else "dve", oT[:, 2 * n:2 * n + 2, c0:c0 + rows], pv, [pk], [r_oT(n, c0)])

    def pipeline(tasks, skew):
        n = len(tasks)
        ns = max(len(t) for t in tasks) if tasks else 0
        for t in range(n + (ns - 1) * skew):
            for j in reversed(range(ns)):
                k = t - j * skew
                if 0 <= k < n and j < len(tasks[k]):
                    tasks[k][j]()

    def attn_softmax(u, qk_fn, v_fn, scale, bias_h, maskT, out_fn, kq_keys, tag, collect=None):
        def do_qb(qb):
            qc0, nq, fq0, subs = qb
            O, ok = ps_o()
            o_started = [False]
            vis = []
            for ki, kt in enumerate(u.keys):
                f = visible(kt, fq0, fq0 + nq, False)
                if f is None:
                    continue
                vis.append((ki, kt, f - fq0 if fq0 >= 0 else 0))

            def stage_a(ctx, ki, kt, ql):
                nk = kt["nk"]
                p, pk = ps_s()
                pairs = qk_fn(kt, qc0, ql, nq)
                for i_, (lt, rh) in enumerate(pairs):
                    S.mm(p[0:nk, ql:nq], lt, rh, i_ == 0, i_ == len(pairs) - 1, kq_keys, [pk])
                PT, ptk = rot_buf("PT", 7, [128, 512], BF16)
                ctx["PT"], ctx["ptk"] = PT, ptk
                if bias_h is None:
                    segs = [("plain", ql, nq)]
                else:
                    x0 = fq0 - kt["fk0"]
                    n_lo = max(ql, -x0)
                    n_hi = min(nq, 256 - x0)
                    segs = []
                    if n_hi > n_lo:
                        segs.append(("near", n_lo, n_hi))
                        if n_hi < nq:
                            segs.append(("far", n_hi, nq))
                    else:
                        segs.append(("far", ql, nq))
                dst_is_tmp = maskT is not None
                if dst_is_tmp:
                    ET, etk = rot_buf("ET", 4, [128, 512], F32)
                for kind, a_, b_ in segs:
                    dst = (ET if dst_is_tmp else PT)
                    dk = etk if dst_is_tmp else ptk
                    if kind == "plain":
                        S.act(dst[0:nk, a_:b_], p[0:nk, a_:b_], AF.Exp, [pk], [dk], scale=scale)
                    elif kind == "far":
                        S.act(dst[0:nk, a_:b_], p[0:nk, a_:b_], AF.Exp, [pk, "cbias"], [dk], scale=scale,
                              bias=cbias[0:nk, bias_h:bias_h + 1])
                    else:
                        x0 = fq0 - kt["fk0"]
                        TB, tbk = rot_buf("TB", 4, [128, 256], F32)
                        S.stt("dve", TB[0:nk, 0:b_ - a_], p[0:nk, a_:b_], scale, BT[0:nk, bias_h, a_ + x0:b_ + x0],
                              ALU.mult, ALU.add, [pk, "BT"], [tbk])
                        S.act(dst[0:nk, a_:b_], TB[0:nk, 0:b_ - a_], AF.Exp, [tbk], [dk])
                if maskT is not None:
                    MT, mkf = maskT
                    S.tt("dve", PT[0:nk, ql:nq], ET[0:nk, ql:nq], MT[0:nk, ki, ql:nq], ALU.mult,
                         [etk] + mkf(ki), [ptk])
                elif fq0 >= 0 and kt["fk0"] >= fq0 and nk > CHUNK:
                    S.memset("dve", PT[CHUNK:nk, ql:ql + CHUNK], 0.0, [ptk])

            def stage_b(ctx, ki, kt, ql):
                nk = kt["nk"]
                PT, ptk = ctx["PT"], ctx["ptk"]
                vap, vkey = v_fn(kt)
                for (so, rows, oti) in subs:
                    if so + rows <= ql:
                        continue
                    sidx = so // 128
                    S.mm(O[0:rows, sidx * 65:sidx * 65 + 65], PT[0:nk, so:so + rows], vap, not o_started[0], True,
                         [ptk, vkey], [ok], skip=True)
                    o_started[0] = True

            tasks = []
            for (ki, kt, ql) in vis:
                ctx = {}
                tasks.append([lambda c=ctx, a=ki, b_=kt, d=ql, fa=stage_a: fa(c, a, b_, d),
                              lambda c=ctx, a=ki, b_=kt, d=ql, fb=stage_b: fb(c, a, b_, d)])
            if collect is None:
                pipeline(tasks, 3)
                out_fn(qb, O, ok)
            else:
                tasks[-1].append(lambda q_=qb, o_=O, k_=ok, f_=out_fn: f_(q_, o_, k_))
                collect.extend(tasks)

        for qb_ in u.qblocks:
            do_qb(qb_)

    with stA:
        qng = sb("qng", [128, 256], F32)
        kvg = sb("kvg", [128, 128], F32)
        slg = sb("slg", [128, 64], F32)
        lamt = sb("lamt", [128, 128], F32)
        lamw = sb("lamw", [128, 8], F32)
        S.dma("sp", qng[:], dr["qn_g"][l:l + 1, :].broadcast_to([128, 256]), (), ["qng"])
        S.dma("sp", kvg[:], dr["kvn_g"][l:l + 1, :].broadcast_to([128, 128]), (), ["kvg"])
        S.dma("sp", slg[:], dr["subln_g"][l:l + 1, :].broadcast_to([128, 64]), (), ["slg"])
        S.dma("sp", lamt[:], dr["lam_p"][l:l + 1, :].broadcast_to([128, 128]), (), ["lamt"])
        lt4 = lamt[:].rearrange("p (a b d) -> p a b d", a=2, b=2)
        lpr = sb("lpr", [128, 2, 32], F32)
        S.tt("dve", lpr[:], lt4[:, :, 0, :], lt4[:, :, 1, :], ALU.mult, ["lamt"], ["lpr"])
        S.reduce(lamw[:, 0:2], lpr[:], ALU.add, ["lpr"], ["lamw0"])
        S.act(lamw[:, 2:4], lamw[:, 0:2], AF.Exp, ["lamw0"], ["lamw1"])
        S.ts("dve", lamw[:, 4:5], lamw[:, 2:3], lamw[:, 3:4], lam_init, ALU.subtract, ALU.add, ["lamw1"], ["lamw2"])
        S.ts("dve", lamw[:, 5:6], lamw[:, 4:5], -1.0, None, ALU.mult, None, ["lamw2"], ["lamw3"])
        S.ts("dve", slg[:], slg[:], 1.0 - lam_init, None, ALU.mult, None, ["slg"], ["slg"])

        if _stop("pre"):
            return True
        stg = sb("stg", [128, 2, 512], F32)
        stg_i = [0]

        def stage_out(p, pk, rows, ncols, name_cols, c0):
            i = stg_i[0] % 2; stg_i[0] += 1
            sk = ("stg", i)
            S.copy("act", stg[0:rows, i, 0:ncols], p[0:rows, 0:ncols], [pk], [sk])
            for (nm, cc, n) in name_cols:
                S.dma("sp", out_rows(g, nm, l, c0, rows), stg[0:rows, i, cc:cc + n], [sk], [("out", nm, c0)])
            return stg[0:rows, i, :], sk

        with contextlib.ExitStack() as sm:
            cur[0] = sm
            wt = sb("w_sb", [128, KC, 768], BF16, sm)
            load_w(wt, 416, 768, "w_sb")
            QT = sb("sbQT", [128, 2, TTg], BF16, sm)
            NQT = sb("sbNQT", [128, 2, TTg], BF16, sm)
            KT = sb("sbKT", [128, 2, TTg], BF16, sm)
            Vt = sb("sbV", [128, NT, 256], BF16, sm)
            ostage = sb("sbO", [128, NT, 256], BF16, sm)
            sc = 64.0 ** -0.5
            for (c0, n) in g.blocks:
                for ch in range(2):
                    p, pk = proj_fm2(wt, "w_sb", ch * 128, 128, c0, n)
                    S.act(QT[:, ch, c0:c0 + n], p[:, 0:n], AF.Copy, [pk], [("sbQT", c0 // 512)], scale=sc)
                    S.ts("dve", NQT[:, ch, c0:c0 + n], p[:, 0:n], -sc, None, ALU.mult, None, [pk], [("sbNQT", c0 // 512)])
                    p, pk = proj_fm2(wt, "w_sb", 256 + ch * 128, 128, c0, n)
                    S.copy("act" if ch else "dve", KT[:, ch, c0:c0 + n], p[:, 0:n], [pk], [("sbKT", c0 // 512)])
            for ti, (c0, rows) in enumerate(g.tiles):
                p, pk = proj_tm(wt, "w_sb", 256, 512, c0, rows)
                sa, sk = stage_out(p, pk, rows, 512, [("sb_k", 0, 256), ("sb_v", 256, 256)], c0)
                S.copy("dve", Vt[0:rows, ti, :], sa[:, 256:512], [sk], [("sbV", ti)])
            if _stop("sbproj"):
                return True
            for u in units:
                if not isP:
                    Kc = sb("sbKc", [128, NPT, 256], BF16, sm) if u.sidx == 0 else Kc
                    Vc = sb("sbVc", [128, NPT, 256], BF16, sm) if u.sidx == 0 else Vc
                    KTc = sb("sbKTc", [128, 2, PAST], BF16, sm) if u.sidx == 0 else KTc
                    gs = u.sidx
                    S.dma("sp", Kc[:], CB["sb_k"][l, gs].rearrange("(j p) f -> p j f", p=128), [("cb", "sb_k", l)], ["sbKc"])
                    S.dma("sp", Vc[:], CB["sb_v"][l, gs].rearrange("(j p) f -> p j f", p=128), [("cb", "sb_v", l)], ["sbVc"])
                    for j in range(NPT):
                        for ch in range(2):
                            transpose_into(KTc[:, ch, j * 128:(j + 1) * 128], "sbKTc", Kc[:, j, ch * 128:(ch + 1) * 128],
                                           "sbKc", 128, 128, "act" if ch else "dve")
                def sb_block(hh, qb, tl):
                    ch, po = hh // 2, (hh % 2) * 64
                    if True:
                        qc0, nq, fq0, subs = qb
                        O, ok = ps_o()
                        o_started = [False]
                        vis = []
                        for ki, kt in enumerate(u.keys):
                            fk0, nk = kt["fk0"], kt["nk"]
                            if fk0 < 0:
                                if fq0 < 0:
                                    vis.append((ki, kt, 0, True))
                                else:
                                    vis.append((ki, kt, 0, False))
                                continue
                            if fq0 < 0:
                                continue
                            if fk0 >= fq0 + nq:
                                continue
                            ql = max(0, fk0 - fq0)
                            vis.append((ki, kt, ql, fk0 >= fq0))
                        vis.sort(key=lambda t: -t[1]["fk0"])
                        Ls = []

                        def sb_ops(kt, ql):
                            nk = kt["nk"]
                            if kt["own"]:
                                kap = KT[po:po + 64, ch, kt["c0"]:kt["c0"] + nk]
                                kk = ("sbKT", kt["c0"] // 512)
                                vap = Vt[0:nk, kt["vi"], hh * 64:hh * 64 + 64]; vk = ("sbV", kt["vi"])
                            else:
                                kap = KTc[po:po + 64, ch, kt["c0"]:kt["c0"] + nk]; kk = "sbKTc"
                                vap = Vc[0:nk, kt["vi"], hh * 64:hh * 64 + 64]; vk = "sbVc"
                            qap = QT[po:po + 64, ch, qc0 + ql:qc0 + nq]
                            nqap = NQT[po:po + 64, ch, qc0 + ql:qc0 + nq]
                            qk_ = [("sbQT", qc0 // 512), ("sbNQT", qc0 // 512), kk]
                            return nk, kap, vap, vk, qap, nqap, qk_

                        def sb_a(ctx, ki, kt, ql, diag):
                            nk, kap, vap, vk, qap, nqap, qk_ = sb_ops(kt, ql)
                            p, pk = ps_s()
                            S.mm(p[0:nk, ql:nq], kap, qap, True, True, qk_, [pk])
                            ET, etk = rot_buf("ET", 4, [128, 512], F32)
                            S.act(ET[0:nk, ql:nq], p[0:nk, ql:nq], AF.Exp, [pk], [etk])
                            Lt, ltk = rot_buf("sbL", 24, [128, 512], BF16)
                            S.act(Lt[0:nk, ql:nq], ET[0:nk, ql:nq], AF.Ln, [etk, "cst"], [ltk], bias=cst[0:nk, 2:3])
                            dw = min(nk, nq - ql)
                            if diag:
                                S.tt("dve", Lt[0:nk, ql:ql + dw], Lt[0:nk, ql:ql + dw], cmask[0:nk, 0:dw], ALU.mult,
                                     [ltk, "cmask"], [ltk])
                            ctx["later"] = list(Ls)
                            ctx["L"] = (Lt, ltk)
                            Ls.append((Lt, ltk, nk, ql))

                        def sb_c(ctx, ki, kt, ql, diag):
                            nk, kap, vap, vk, qap, nqap, qk_ = sb_ops(kt, ql)
                            Lt, ltk = ctx["L"]
                            later = ctx["later"]
                            c_, ck = ps_s()
                            S.mm(c_[0:nk, ql:nq], utri[0:nk, 0:nk], Lt[0:nk, ql:nq], True, False, ["utri", ltk], [ck])
                            nlater = len(later)
                            S.mm(c_[0:nk, ql:nq], kap, nqap, False, nlater == 0, qk_, [ck])
                            for li, (L2, l2k, nk2, ql2) in enumerate(later):
                                S.mm(c_[0:nk, ql2:nq], ones[0:nk2, 0:nk], L2[0:nk2, ql2:nq], False, li == nlater - 1,
                                     ["ones", l2k], [ck])
                            PT, ptk = rot_buf("PT", 7, [128, 512], BF16)
                            ctx["PT"] = (PT, ptk)
                            S.act(PT[0:nk, ql:nq], c_[0:nk, ql:nq], AF.Exp, [ck], [ptk], scale=-1.0)
                            dw = min(nk, nq - ql)
                            if diag:
                                S.tt("dve", PT[0:nk, ql:ql + dw], PT[0:nk, ql:ql + dw], cmask[0:nk, 0:dw], ALU.mult,
                                     [ptk, "cmask"], [ptk])

                        def sb_e(ctx, ki, kt, ql, diag):
                            nk, kap, vap, vk, qap, nqap, qk_ = sb_ops(kt, ql)
                            PT, ptk = ctx["PT"]
                            for (so, rows, oti) in subs:
                                if so + rows <= ql:
                                    continue
                                sidx = so // 128
                                S.mm(O[0:rows, sidx * 64:sidx * 64 + 64], PT[0:nk, so:so + rows], vap, not o_started[0], True,
                                     [ptk, vk], [ok], skip=True)
                                o_started[0] = True

                        tasks = []
                        for (ki, kt, ql, diag) in vis:
                            ctx = {}
                            tasks.append([lambda c=ctx, a=ki, b_=kt, d=ql, e=diag: sb_a(c, a, b_, d, e),
                                          lambda c=ctx, a=ki, b_=kt, d=ql, e=diag: sb_c(c, a, b_, d, e),
                                          lambda c=ctx, a=ki, b_=kt, d=ql, e=diag: sb_e(c, a, b_, d, e)])

                        def sb_out():
                            for (so, rows, oti) in subs:
                                sidx = so // 128
                                S.copy("act", ostage[0:rows, oti, hh * 64:hh * 64 + 64], O[0:rows, sidx * 64:sidx * 64 + 64],
                                       [ok], [("sbO", oti)])

                        tasks[-1].append(sb_out)
                        tl.extend(tasks)

                tl = []
                for hh in range(H):
                    for qb in u.qblocks:
                        sb_block(hh, qb, tl)
                pipeline(tl, 4)
            finalize_o(ostage, "sbO", 1)
            rot.pop("sbL", None)
        S.barrier()
        rot.clear()
        if _stop("sb"):
            return True

        with contextlib.ExitStack() as sm:
            cur[0] = sm
            wt = sb("w_mla", [128, KC, 416], BF16, sm)
            load_w(wt, 0, 416, "w_mla")
            wuq = sb("wuq", [128, 2, 384], BF16, sm)
            S.dma(wq, wuq[:], wsrc("w_uq").rearrange("(c p) e -> p c e", p=128), wdep("w_uq"), ["wuq"])
            wuk = sb("wuk", [128, 256], BF16, sm)
            S.dma(wq, wuk[:], wsrc("w_uk"), wdep("w_uk"), ["wuk"])
            wuv = sb("wuv", [128, 256], BF16, sm)
            S.dma(wq, wuv[:], wsrc("w_uv"), wdep("w_uv"), ["wuv"])
            wukT = sb("wukT", [64, 4, 128], BF16, sm)
            for hh in range(H):
                transpose_into(wukT[:, hh, :], "wukT", wuk[:, hh * 64:(hh + 1) * 64], "wuk", 128, 64)
            rope = sb("rope", [128, NT, 128], F32, sm)
            for ti, (c0, rows) in enumerate(g.tiles):
                S.dma("sp", rope[0:rows, ti, :], g.rope[c0:c0 + rows, :], (), ["rope"])
            cqT = sb("cqT", [128, 2, TTg], BF16, sm)
            ckvT = sb("ckvT", [128, TTg], BF16, sm)
            kpeT = sb("kpeT", [128, TTg], BF16, sm)
            Vm = sb("Vm", [128, NT, 4, 65], BF16, sm)
            QA = sb("QA", [128, NT, 4, 96], BF16, sm)
            ostage = sb("mlO", [128, NT, 256], BF16, sm)
            wk_st = sb("mst", [128, 8], F32, sm)
            S.memset("dve", Vm[:, :, :, 64:65], 1.0, [("Vm", i) for i in range(NT)])
            for ti, (c0, rows) in enumerate(g.tiles):
                p, pk = proj_tm(wt, "w_mla", 0, 416, c0, rows)
                jk, jkk = rot_buf("mjunk", 2, [128, 256], BF16)
                st_k = ("mst", ti % 2)
                stc = wk_st[0:rows, (ti % 2) * 4:(ti % 2) * 4 + 4]
                S.memset("dve", stc, 0.0, [st_k])
                S.act(jk[0:rows, 0:256], p[0:rows, 0:256], AF.Square, [pk, st_k], [jkk, (st_k, "a")],
                      accum_out=stc[:, 0:1])
                S.act(jk[0:rows, 0:128], p[0:rows, 256:384], AF.Square, [pk, st_k], [jkk, (st_k, "b")],
                      accum_out=stc[:, 1:2])
                S.act(stc[:, 2:3], stc[:, 0:1], AF.Ln, [st_k, (st_k, "a"), "cst"], [(st_k, "c")], scale=1.0 / 256, bias=cst[0:rows, 1:2])
                S.act(stc[:, 2:3], stc[:, 2:3], AF.Exp, [(st_k, "c")], [(st_k, "c")], scale=-0.5)
                S.act(stc[:, 3:4], stc[:, 1:2], AF.Ln, [st_k, (st_k, "b"), "cst"], [(st_k, "d")], scale=1.0 / 128, bias=cst[0:rows, 1:2])
                S.act(stc[:, 3:4], stc[:, 3:4], AF.Exp, [(st_k, "d")], [(st_k, "d")], scale=-0.5)
                cqn, cqk = rot_buf("cqn", 2, [128, 256], BF16)
                S.stt("dve", cqn[0:rows, :], p[0:rows, 0:256], stc[:, 2:3], qng[0:rows, :], ALU.mult, ALU.mult,
                      [pk, (st_k, "c"), "qng"], [cqk])
                i = stg_i[0] % 2; stg_i[0] += 1
                sk = ("stg", i)
                S.stt("dve", stg[0:rows, i, 0:128], p[0:rows, 256:384], stc[:, 3:4], kvg[0:rows, :], ALU.mult, ALU.mult,
                      [pk, (st_k, "d"), "kvg"], [sk])
                S.tt("dve", stg[0:rows, i, 128:160], p[0:rows, 384:416], rope[0:rows, ti, 0:32], ALU.mult, [pk, "rope"], [sk])
                rt, rtk = rot_buf("rtmp", 2, [128, 4, 32], F32)
                S.tt("dve", rt[0:rows, 0, 0:16], p[0:rows, 400:416], rope[0:rows, ti, 32:48], ALU.mult, [pk, "rope"], [rtk])
                S.tt("dve", rt[0:rows, 0, 16:32], p[0:rows, 384:400], rope[0:rows, ti, 48:64], ALU.mult, [pk, "rope"], [rtk])
                S.tt("dve", stg[0:rows, i, 128:160], stg[0:rows, i, 128:160], rt[0:rows, 0, :], ALU.add, [sk, rtk], [sk])
                S.dma("sp", out_rows(g, "mla_kv", l, c0, rows), stg[0:rows, i, 0:128], [sk], [("out", "mla_kv", c0)])
                S.dma("sp", out_rows(g, "mla_pe", l, c0, rows), stg[0:rows, i, 128:160], [sk], [("out", "mla_pe", c0)])
                kb, kbk = rot_buf("kb", 2, [128, 160], BF16)
                S.copy("act", kb[0:rows, :], stg[0:rows, i, 0:160], [sk], [kbk])
                for cc in range(2):
                    transpose_into(cqT[:, cc, c0:c0 + rows], ("cqT", ti), cqn[0:rows, cc * 128:(cc + 1) * 128], cqk,
                                   rows, 128, "act" if cc else "dve")
                transpose_into(ckvT[:, c0:c0 + rows], ("ckvT", ti), kb[0:rows, 0:128], kbk, rows, 128, "dve")
                transpose_into(kpeT[64:96, c0:c0 + rows], ("kpeT", ti), kb[0:rows, 128:160], kbk, rows, 32, "dve")
                p2, p2k = ps_s()
                S.mm(p2[0:rows, 0:256], ckvT[:, c0:c0 + rows], wuv[:], True, True, [("ckvT", ti), "wuv"], [p2k])
                S.copy("act", Vm[0:rows, ti, :, 0:64], p2[0:rows, 0:256].rearrange("p (h d) -> p h d", h=4), [p2k], [("Vm", ti)])
                p3, p3k = ps_s()
                for cc in range(2):
                    S.mm(p3[0:rows, 0:384], cqT[:, cc, c0:c0 + rows], wuq[:, cc, :], cc == 0, cc == 1,
                         [("cqT", ti), "wuq"], [p3k])
                p3v = p3[0:rows, 0:384].rearrange("p (h e) -> p h e", h=4)
                S.act(QA[0:rows, ti, :, 0:64], p3v[:, :, 0:64], AF.Copy, [p3k], [("QA", ti)], scale=96.0 ** -0.5)
                cosq = rope[0:rows, ti, 64:96].unsqueeze(1).broadcast_to([rows, 4, 32])
                nsin = rope[0:rows, ti, 96:112].unsqueeze(1).broadcast_to([rows, 4, 16])
                psin = rope[0:rows, ti, 112:128].unsqueeze(1).broadcast_to([rows, 4, 16])
                r1, r1k = rot_buf("rtmp", 2, [128, 4, 32], F32)
                r2, r2k = rot_buf("rtmp2", 2, [128, 4, 32], F32)
                S.tt("dve", r1[0:rows], p3v[:, :, 64:96], cosq, ALU.mult, [p3k, "rope"], [r1k])
                S.tt("dve", r2[0:rows, :, 0:16], p3v[:, :, 80:96], nsin, ALU.mult, [p3k, "rope"], [r2k])
                S.tt("dve", r2[0:rows, :, 16:32], p3v[:, :, 64:80], psin, ALU.mult, [p3k, "rope"], [r2k])
                S.tt("dve", QA[0:rows, ti, :, 64:96], r1[0:rows], r2[0:rows], ALU.add, [r1k, r2k], [("QA", ti)])
            for u in units:
                if not isP:
                    gs = u.sidx
                    Kc = sb("mlKc", [128, NPT, 160], BF16, sm) if gs == 0 else Kc
                    ckvTc = sb("ckvTc", [128, PAST], BF16, sm) if gs == 0 else ckvTc
                    kpeTc = sb("kpeTc", [128, PAST], BF16, sm) if gs == 0 else kpeTc
                    Vmc = sb("Vmc", [128, NPT, 4, 65], BF16, sm) if gs == 0 else Vmc
                    if gs == 0:
                        S.memset("dve", Vmc[:, :, :, 64:65], 1.0, ["Vmc"])
                    S.dma("sp", Kc[:, :, 0:128], CB["mla_kv"][l, gs].rearrange("(j p) f -> p j f", p=128), [("cb", "mla_kv", l)], ["mlKc"])
                    S.dma("sp", Kc[:, :, 128:160], CB["mla_pe"][l, gs].rearrange("(j p) f -> p j f", p=128), [("cb", "mla_pe", l)], ["mlKc"])
                    for j in range(NPT):
                        transpose_into(ckvTc[:, j * 128:(j + 1) * 128], "ckvTc", Kc[:, j, 0:128], "mlKc", 128, 128, "dve")
                        transpose_into(kpeTc[64:96, j * 128:(j + 1) * 128], "kpeTc", Kc[:, j, 128:160], "mlKc", 128, 32, "act")
                        p2, p2k = ps_s()
                        S.mm(p2[:, 0:256], ckvTc[:, j * 128:(j + 1) * 128], wuv[:], True, True, ["ckvTc", "wuv"], [p2k])
                        S.copy("act", Vmc[:, j, :, 0:64], p2[:, 0:256].rearrange("p (h d) -> p h d", h=4), [p2k], ["Vmc"])
                for hh in range(H):
                    QhT = sb("QhT", [128, TTg], BF16, sm) if (hh == 0 and u.sidx == 0) else QhT
                    QlT = sb("QlT", [128, TTg], BF16, sm) if (hh == 0 and u.sidx == 0) else QlT
                    own_tiles = [(ti, c0, rows) for ti, (c0, rows) in enumerate(g.tiles)
                                 if isP or ti == u.sidx]
                    for (ti, c0, rows) in own_tiles:
                        transpose_into(QhT[0:96, c0:c0 + rows], ("QhT", ti), QA[0:rows, ti, hh, :], ("QA", ti), rows, 96,
                                       "act" if ti % 2 else "dve")
                    for (ti, c0, rows) in own_tiles:
                        p4, p4k = ps_s()
                        S.mm(p4[:, 0:rows], wukT[:, hh, :], QhT[0:64, c0:c0 + rows], True, True, ["wukT", ("QhT", ti)], [p4k])
                        S.copy("dve" if ti % 2 else "act", QlT[:, c0:c0 + rows], p4[:, 0:rows], [p4k], [("QlT", ti)])

                    def qk_fn(kt, qc0, ql, nq):
                        nk = kt["nk"]
                        if kt["own"]:
                            ka, kb_ = ckvT[:, kt["c0"]:kt["c0"] + nk], kpeT[64:96, kt["c0"]:kt["c0"] + nk]
                        else:
                            ka, kb_ = ckvTc[:, kt["c0"]:kt["c0"] + nk], kpeTc[64:96, kt["c0"]:kt["c0"] + nk]
                        return [(ka, QlT[:, qc0 + ql:qc0 + nq]), (kb_, QhT[64:96, qc0 + ql:qc0 + nq])]

                    def v_fn(kt):
                        if kt["own"]:
                            return Vm[0:kt["nk"], kt["vi"], hh, :], ("Vm", kt["vi"])
                        return Vmc[0:kt["nk"], kt["vi"], hh, :], "Vmc"

                    def out_fn(qb, O, ok):
                        qc0, nq, fq0, subs = qb
                        for (so, rows, oti) in subs:
                            sidx = so // 128
                            rc, rck = rot_buf("rc", 4, [128, 1], F32)
                            S.recip(rc[0:rows, :], O[0:rows, sidx * 65 + 64:sidx * 65 + 65], [ok], [rck])
                            S.ts("dve", ostage[0:rows, oti, hh * 64:hh * 64 + 64], O[0:rows, sidx * 65:sidx * 65 + 64],
                                 rc[0:rows, 0:1], None, ALU.mult, None, [ok, rck], [("mlO", oti)])

                    kq_keys = ([("ckvT", i) for i in range(NT)] + [("kpeT", i) for i in range(NT)]
                               + [("QhT", i) for i in range(NT)] + [("QlT", i) for i in range(NT)]
                               + ["ckvTc", "kpeTc"])
                    tl = []
                    attn_softmax(u, qk_fn, v_fn, 1.0, None, None, out_fn, kq_keys, "ml", collect=tl)
                    pipeline(tl, 4)
            finalize_o(ostage, "mlO", 0)
        S.barrier()
        rot.clear()
        if _stop("mla"):
            return True

        with contextlib.ExitStack() as sm:
            cur[0] = sm
            wt = sb("w_df", [128, KC, 768], BF16, sm)
            load_w(wt, 1184, 768, "w_df")
            QT = sb("dfQT", [128, 2, TTg], BF16, sm)
            KA = sb("dfKA", [128, 2, TTg], BF16, sm)
            KB = sb("dfKB", [128, 2, TTg], BF16, sm)
            Vt = sb("dfV", [128, NT, 4, 65], BF16, sm)
            ostage = sb("dfO", [128, NT, 256], BF16, sm)
            S.memset("dve", Vt[:, :, :, 64:65], 1.0, [("dfV", i) for i in range(NT)])
            sc = 32.0 ** -0.5
            for (c0, n) in g.blocks:
                for ch in range(2):
                    p, pk = proj_fm2(wt, "w_df", ch * 128, 128, c0, n)
                    S.copy("act", QT[:, ch, c0:c0 + n], p[:, 0:n], [pk], [("dfQT", c0 // 512)])
                    p, pk = proj_fm2(wt, "w_df", 256 + ch * 128, 128, c0, n)
                    S.ts("dve", KA[:, ch, c0:c0 + n], p[:, 0:n], mab[:, 0:1], None, ALU.mult, None, [pk, "mab"], [("dfKA", c0 // 512)])
                    S.ts("dve", KB[:, ch, c0:c0 + n], p[:, 0:n], mab[:, 1:2], None, ALU.mult, None, [pk, "mab"], [("dfKB", c0 // 512)])
            for ti, (c0, rows) in enumerate(g.tiles):
                p, pk = proj_tm(wt, "w_df", 256, 512, c0, rows)
                sa, sk = stage_out(p, pk, rows, 512, [("diff_k", 0, 256), ("diff_v", 256, 256)], c0)
                S.copy("dve", Vt[0:rows, ti, :, 0:64], sa[:, 256:512].rearrange("p (h d) -> p h d", h=4), [sk], [("dfV", ti)])
            for u in units:
                if not isP:
                    gs = u.sidx
                    Kc = sb("dfKc", [128, NPT, 256], BF16, sm) if gs == 0 else Kc
                    Vc = sb("dfVc", [128, NPT, 4, 65], BF16, sm) if gs == 0 else Vc
                    Vcs = sb("dfVcs", [128, NPT, 256], BF16, sm) if gs == 0 else Vcs
                    KAc = sb("dfKAc", [128, 2, PAST], BF16, sm) if gs == 0 else KAc
                    KBc = sb("dfKBc", [128, 2, PAST], BF16, sm) if gs == 0 else KBc
                    if gs == 0:
                        S.memset("dve", Vc[:, :, :, 64:65], 1.0, ["dfVc"])
                    S.dma("sp", Kc[:], CB["diff_k"][l, gs].rearrange("(j p) f -> p j f", p=128), [("cb", "diff_k", l)], ["dfKc"])
                    S.dma("sp", Vcs[:], CB["diff_v"][l, gs].rearrange("(j p) f -> p j f", p=128), [("cb", "diff_v", l)], ["dfVcs"])
                    S.copy("dve", Vc[:, :, :, 0:64], Vcs[:].rearrange("p j (h d) -> p j h d", h=4), ["dfVcs"], ["dfVc"])
                    for j in range(NPT):
                        for ch in range(2):
                            p, pk = ps_s()
                            S.mm(p[:, 0:128], Kc[:, j, ch * 128:(ch + 1) * 128], ident[:], True, True, ["dfKc", "ident"], [pk])
                            S.ts("dve", KAc[:, ch, j * 128:(j + 1) * 128], p[:, 0:128], mab[:, 0:1], None, ALU.mult, None,
                                 [pk, "mab"], ["dfKAc"])
                            S.ts("dve", KBc[:, ch, j * 128:(j + 1) * 128], p[:, 0:128], mab[:, 1:2], None, ALU.mult, None,
                                 [pk, "mab"], ["dfKBc"])
                def diff_head(hh, tl):
                    ch, po = hh // 2, (hh % 2) * 64
                    keep = {}

                    def v_fn(kt):
                        if kt["own"]:
                            return Vt[0:kt["nk"], kt["vi"], hh, :], ("dfV", kt["vi"])
                        return Vc[0:kt["nk"], kt["vi"], hh, :], "dfVc"

                    for mi in range(2):
                        def qk_fn(kt, qc0, ql, nq, mi=mi):
                            nk = kt["nk"]
                            if kt["own"]:
                                src = KA if mi == 0 else KB
                            else:
                                src = KAc if mi == 0 else KBc
                            return [(src[po:po + 64, ch, kt["c0"]:kt["c0"] + nk], QT[po:po + 64, ch, qc0 + ql:qc0 + nq])]

                        def out_fn(qb, O, ok, mi=mi):
                            qc0, nq, fq0, subs = qb
                            for (so, rows, oti) in subs:
                                sidx = so // 128
                                rc, rck = rot_buf("rc", 4, [128, 1], F32)
                                S.recip(rc[0:rows, :], O[0:rows, sidx * 65 + 64:sidx * 65 + 65], [ok], [rck])
                                if mi == 0:
                                    d0, d0k = rot_buf("d0_", NT + 2, [128, 64], F32)
                                    keep[(qc0, so)] = (d0, d0k)
                                    S.ts("dve", d0[0:rows, :], O[0:rows, sidx * 65:sidx * 65 + 64], rc[0:rows, 0:1], None,
                                         ALU.mult, None, [ok, rck], [d0k])
                                else:
                                    d0, d0k = keep[(qc0, so)]
                                    d1, d1k = rot_buf("d1_", 2, [128, 64], F32)
                                    S.ts("dve", d1[0:rows, :], O[0:rows, sidx * 65:sidx * 65 + 64], rc[0:rows, 0:1],
                                         lamw[0:rows, 5:6], ALU.mult, ALU.mult, [ok, rck, "lamw3"], [d1k])
                                    S.tt("dve", d1[0:rows, :], d1[0:rows, :], d0[0:rows, :], ALU.add, [d1k, d0k], [d1k])
                                    ss, ssk = rot_buf("ss_", 2, [128, 2], F32)
                                    jk, jkk = rot_buf("dj_", 2, [128, 64], F32)
                                    S.memset("dve", ss[0:rows, :], 0.0, [ssk])
                                    S.act(jk[0:rows, :], d1[0:rows, :], AF.Square, [d1k, ssk], [jkk, (ssk, "a")],
                                          accum_out=ss[0:rows, 0:1])
                                    S.act(ss[0:rows, 1:2], ss[0:rows, 0:1], AF.Ln, [(ssk, "a"), ssk, "cst"], [(ssk, "b")],
                                          scale=1.0 / 64, bias=cst[0:rows, 1:2])
                                    S.act(ss[0:rows, 1:2], ss[0:rows, 1:2], AF.Exp, [(ssk, "b")], [(ssk, "b")], scale=-0.5)
                                    S.stt("dve", ostage[0:rows, oti, hh * 64:hh * 64 + 64], d1[0:rows, :], ss[0:rows, 1:2],
                                          slg[0:rows, :], ALU.mult, ALU.mult, [d1k, (ssk, "b"), "slg"], [("dfO", oti)])

                        kq_keys = ([("dfQT", i) for i in range((TTg + 511) // 512)] + [("dfKA", i) for i in range((TTg + 511) // 512)]
                                   + [("dfKB", i) for i in range((TTg + 511) // 512)] + ["dfKAc", "dfKBc"])
                        attn_softmax(u, qk_fn, v_fn, sc, hh, None, out_fn, kq_keys, "df", collect=tl)

                tl = []
                for hh in range(H):
                    diff_head(hh, tl)
                pipeline(tl, 4)
            finalize_o(ostage, "dfO", 2)
        S.barrier()
        rot.clear()
        if _stop("df"):
            return True

        with contextlib.ExitStack() as sm:
            cur[0] = sm
            QT = sb("dsQT", [128, 2, TTg], BF16, sm)
            KT = sb("dsKT", [128, 2, TTg], BF16, sm)
            QI = sb("dsQI", [128, 2, TTg], BF16, sm)
            KIa = sb("dsKIa", [128, TTg], BF16, sm)
            KIb = sb("dsKIb", [128, TTg], BF16, sm)
            WI = sb("dsWI", [128, NT, 8], F32, sm)
            Vt = sb("dsV", [128, NT, 4, 65], BF16, sm)
            ostage = sb("dsO", [128, NT, 256], BF16, sm)
            smw = contextlib.ExitStack()
            wt = sb("w_ds", [128, KC, 1168], BF16, smw)
            load_w(wt, 1952, 1024, "w_ds")
            for r_ in range(4):
                load_w(wt, 2976, 32, "w_ds", dcol=1024 + 32 * r_)
            load_w(wt, 3008, 8, "w_ds", dcol=1152)
            if conv_cb is not None:
                conv_cb()
            S.memset("dve", Vt[:, :, :, 64:65], 1.0, [("dsV", i) for i in range(NT)])
            sc = 64.0 ** -0.5
            nblk = (TTg + 511) // 512
            for (c0, n) in g.blocks:
                for ch in range(2):
                    p, pk = proj_fm2(wt, "w_ds", ch * 128, 128, c0, n)
                    S.copy("act", QT[:, ch, c0:c0 + n], p[:, 0:n], [pk], [("dsQT", c0 // 512)])
                    p, pk = proj_fm2(wt, "w_ds", 256 + ch * 128, 128, c0, n)
                    S.copy("dve", KT[:, ch, c0:c0 + n], p[:, 0:n], [pk], [("dsKT", c0 // 512)])
                    p, pk = proj_fm2(wt, "w_ds", 768 + ch * 128, 128, c0, n)
                    S.copy("act", QI[:, ch, c0:c0 + n], p[:, 0:n], [pk], [("dsQI", c0 // 512)])
                p, pk = proj_fm2(wt, "w_ds", 1024, 128, c0, n)
                S.ts("dve", KIa[:, c0:c0 + n], p[:, 0:n], mab[:, 0:1], None, ALU.mult, None, [pk, "mab"], [("dsKI", c0 // 512)])
                S.ts("dve", KIb[:, c0:c0 + n], p[:, 0:n], mab[:, 1:2], None, ALU.mult, None, [pk, "mab"], [("dsKI", c0 // 512)])
            for ti, (c0, rows) in enumerate(g.tiles):
                p, pk = proj_tm(wt, "w_ds", 256, 512, c0, rows)
                sa, sk = stage_out(p, pk, rows, 512, [("dsa_k", 0, 256), ("dsa_v", 256, 256)], c0)
                S.copy("dve", Vt[0:rows, ti, :, 0:64], sa[:, 256:512].rearrange("p (h d) -> p h d", h=4), [sk], [("dsV", ti)])
                p, pk = proj_tm(wt, "w_ds", 1024, 136, c0, rows)
                sa, sk = stage_out(p, pk, rows, 136, [("dsa_kidx", 0, 32)], c0)
                S.ts("dve", WI[0:rows, ti, :], sa[:, 128:136], (8.0 ** -0.5) * (32.0 ** -0.5), None, ALU.mult, None,
                     [sk], [("dsWI", ti)])
            S.barrier()
            smw.close()
            NKT = max(len(u.keys) for u in units)
            MT = sb("dsMT", [128, NKT, 512], BF16, sm)
            SCW = NMETA + SEQ if isP else PAST + DEC
            score = [sb("dsSC%d" % i, [128, SCW], F32, sm) for i in range(2)]
            Mq = [sb("dsM%d" % i, [128, SCW], BF16, sm) for i in range(2)]
            bw = sb("dsbw", [128, 2, 8 + 2 * NBIS], F32, sm)
            for u in units:
                if not isP:
                    gs = u.sidx
                    Kc = sb("dsKc", [128, NPT, 384], BF16, sm) if gs == 0 else Kc
                    Vc = sb("dsVc", [128, NPT, 4, 65], BF16, sm) if gs == 0 else Vc
                    Vcs = sb("dsVcs", [128, NPT, 256], BF16, sm) if gs == 0 else Vcs
                    KTc = sb("dsKTc", [128, 2, PAST], BF16, sm) if gs == 0 else KTc
                    KIac = sb("dsKIac", [128, PAST], BF16, sm) if gs == 0 else KIac
                    KIbc = sb("dsKIbc", [128, PAST], BF16, sm) if gs == 0 else KIbc
                    if gs == 0:
                        S.memset("dve", Vc[:, :, :, 64:65], 1.0, ["dsVc"])
                    S.dma("sp", Kc[:, :, 0:256], CB["dsa_k"][l, gs].rearrange("(j p) f -> p j f", p=128), [("cb", "dsa_k", l)], ["dsKc"])
                    for r_ in range(4):
                        S.dma("sp", Kc[:, :, 256 + 32 * r_:288 + 32 * r_],
                              CB["dsa_kidx"][l, gs].rearrange("(j p) f -> p j f", p=128), [("cb", "dsa_kidx", l)], ["dsKc"])
                    S.dma("sp", Vcs[:], CB["dsa_v"][l, gs].rearrange("(j p) f -> p j f", p=128), [("cb", "dsa_v", l)], ["dsVcs"])
                    S.copy("dve", Vc[:, :, :, 0:64], Vcs[:].rearrange("p j (h d) -> p j h d", h=4), ["dsVcs"], ["dsVc"])
                    for j in range(NPT):
                        for ch in range(2):
                            transpose_into(KTc[:, ch, j * 128:(j + 1) * 128], "dsKTc", Kc[:, j, ch * 128:(ch + 1) * 128],
                                           "dsKc", 128, 128, "act" if ch else "dve")
                        p, pk = ps_s()
                        S.mm(p[:, 0:128], Kc[:, j, 256:384], ident[:], True, True, ["dsKc", "ident"], [pk])
                        S.ts("dve", KIac[:, j * 128:(j + 1) * 128], p[:, 0:128], mab[:, 0:1], None, ALU.mult, None, [pk, "mab"], ["dsKIc"])
                        S.ts("dve", KIbc[:, j * 128:(j + 1) * 128], p[:, 0:128], mab[:, 1:2], None, ALU.mult, None, [pk, "mab"], ["dsKIc"])
                topk = g.topk
                for qb in u.qblocks:
                    qc0, nq, fq0, subs = qb
                    def idx_scores(si):
                        so, rows, oti = subs[si]
                        sl = si % 2
                        sct = score[sl]; sck = ("dsSC", sl)
                        fqs = fq0 + so
                        vis = []
                        for ki, kt in enumerate(u.keys):
                            f = visible(kt, fqs, fqs + rows, False)
                            if f is not None:
                                vis.append((ki, kt))
                        ncols = max(kt["sc0"] + kt["nk"] for (_, kt) in vis)
                        blks = []
                        for (ki, kt) in vis:
                            if blks and blks[-1]["own"] == kt["own"] and blks[-1]["c0"] + blks[-1]["n"] == kt["c0"] \
                                    and blks[-1]["sc0"] + blks[-1]["n"] == kt["sc0"] and blks[-1]["n"] + kt["nk"] <= 512:
                                blks[-1]["n"] += kt["nk"]
                            else:
                                blks.append(dict(own=kt["own"], c0=kt["c0"], sc0=kt["sc0"], n=kt["nk"]))
                        qcol = qc0 + so
                        Dg, dgk = rot_buf("dsDg", 2, [128, 8, 128], BF16)
                        S.tt("dve", Dg[0:rows, :, 0:rows], ident[0:rows, 0:rows].unsqueeze(1).broadcast_to([rows, 8, rows]),
                             WI[0:rows, oti, :].unsqueeze(2).broadcast_to([rows, 8, rows]), ALU.mult,
                             ["ident", ("dsWI", oti)], [dgk])
                        for bi, bk in enumerate(blks):
                            n = bk["n"]
                            Rs = []
                            for ih in range(8):
                                ch, jj = ih // 4, ih % 4
                                po = (jj // 2) * 64
                                if bk["own"]:
                                    kis = (KIa if jj % 2 == 0 else KIb)[po:po + 64, bk["c0"]:bk["c0"] + n]
                                else:
                                    kis = (KIac if jj % 2 == 0 else KIbc)[po:po + 64, bk["c0"]:bk["c0"] + n]
                                p, pk = ps_s()
                                S.mm(p[0:rows, 0:n], QI[po:po + 64, ch, qcol:qcol + rows], kis, True, True,
                                     [("dsQI", qcol // 512), ("dsKI", bk["c0"] // 512), ("dsKI", (bk["c0"] + n - 1) // 512), "dsKIc"], [pk])
                                R, rk = rot_buf("dsR", 10, [128, 512], BF16)
                                S.act(R[0:rows, 0:n], p[0:rows, 0:n], AF.Relu, [pk], [rk])
                                Rs.append((R, rk))
                            pa, pak = ps_s()
                            for ih, (R, rk) in enumerate(Rs):
                                S.mm(pa[0:rows, 0:n], Dg[0:rows, ih, 0:rows], R[0:rows, 0:n], ih == 0, ih == 7, [dgk, rk], [pak])
                            S.copy("act", sct[0:rows, bk["sc0"]:bk["sc0"] + n], pa[0:rows, 0:n], [pak], [(sck, bi)])
                        allb = [(sck, bi) for bi in range(len(blks))]
                        return dict(si=si, so=so, rows=rows, oti=oti, sl=sl, vis=vis, ncols=ncols, allb=allb, fqs=fqs)

                    def bis_setup(c):
                        rows, sl, ncols, allb = c["rows"], c["sl"], c["ncols"], c["allb"]
                        sct = score[sl]
                        b_ = bw[0:rows, sl, :]
                        bk_ = ("dsbw", sl)
                        c["b"], c["bk"] = b_, bk_
                        S.reduce(b_[:, 0:1], sct[0:rows, 0:ncols], ALU.max, allb, [(bk_, "mx")])
                        S.reduce(b_[:, 1:2], sct[0:rows, 0:ncols], ALU.min, allb, [(bk_, "mn")])
                        for (ki, kt) in c["vis"]:
                            if c["fqs"] >= 0 and kt["fk0"] >= c["fqs"] and kt["nk"] > CHUNK:
                                S.memset("dve", sct[0:CHUNK, kt["sc0"] + CHUNK:kt["sc0"] + kt["nk"]], NEGBIG,
                                         allb + [(bk_, "mn"), (bk_, "mx")])
                        S.tt("dve", b_[:, 2:3], b_[:, 0:1], b_[:, 1:2], ALU.subtract, [(bk_, "mx"), (bk_, "mn")], [(bk_, "w0")])
                        S.ts("dve", b_[:, 8:8 + NBIS], pow2[0:rows, :], b_[:, 2:3], None, ALU.mult, None, ["pow2", (bk_, "w0")], [(bk_, "wt")])
                        S.memset("dve", b_[:, 8 + NBIS:8 + 2 * NBIS], 0.0, [(bk_, "cnt")])
                        S.tt("dve", b_[:, 4:5], b_[:, 1:2], b_[:, 8:9], ALU.add, [(bk_, "mn"), (bk_, "wt")], [(bk_, "mid")])
                        if c["sl"] == 1:
                            S.ts("dve", b_[:, 6:7], b_[:, 4:5], -1.0, None, ALU.mult, None, [(bk_, "mid")], [(bk_, "nm")])

                    def bis_iter(c, it):
                        rows, sl, ncols, allb, b_, bk_ = c["rows"], c["sl"], c["ncols"], c["allb"], c["b"], c["bk"]
                        sct = score[sl]; Mt = Mq[sl]; mk = ("dsM", sl)
                        if sl == 1:
                            S.act(Mt[0:rows, 0:ncols], sct[0:rows, 0:ncols], AF.Sign, allb + [(bk_, "nm"), (bk_, "cnt")],
                                  [mk, (bk_, "c%d" % it)], bias=b_[:, 6:7], accum_out=b_[:, 8 + NBIS + it:9 + NBIS + it])
                            S.ts("pool", b_[:, 5:6], b_[:, 8 + NBIS + it:9 + NBIS + it], 2.0 * topk - ncols - 0.5, 0.5,
                                 ALU.is_ge, ALU.subtract, [(bk_, "c%d" % it)], [(bk_, "stp")])
                            S.tt("pool", b_[:, 5:6], b_[:, 5:6], b_[:, 8 + it:9 + it], ALU.mult, [(bk_, "stp"), (bk_, "wt")], [(bk_, "stp")])
                            S.tt("pool", b_[:, 6:7], b_[:, 6:7], b_[:, 5:6], ALU.subtract, [(bk_, "nm"), (bk_, "stp")], [(bk_, "nm")])
                            return
                        S.ts("dve", Mt[0:rows, 0:ncols], sct[0:rows, 0:ncols], b_[:, 4:5], 0.0, ALU.is_ge, ALU.add,
                             allb + [(bk_, "mid"), (bk_, "cnt")], [mk, (bk_, "c%d" % it)],
                             accum_out=b_[:, 8 + NBIS + it:9 + NBIS + it])
                        S.ts("dve", b_[:, 5:6], b_[:, 8 + NBIS + it:9 + NBIS + it], topk - 0.5, 0.5,
                             ALU.is_ge, ALU.subtract, [(bk_, "c%d" % it)], [(bk_, "stp")])
                        S.stt("dve", b_[:, 4:5], b_[:, 5:6], b_[:, 8 + it:9 + it], b_[:, 4:5], ALU.mult, ALU.add,
                              [(bk_, "stp"), (bk_, "wt"), (bk_, "mid")], [(bk_, "mid")])

                    def bis_final(c):
                        rows, sl, ncols, allb, b_, bk_ = c["rows"], c["sl"], c["ncols"], c["allb"], c["b"], c["bk"]
                        so = c["so"]
                        sct = score[sl]; Mt = Mq[sl]; mk = ("dsM", sl)
                        if sl == 1:
                            S.ts("dve", b_[:, 4:5], b_[:, 6:7], -1.0, None, ALU.mult, None, [(bk_, "nm")], [(bk_, "mid")])
                        S.stt("dve", b_[:, 3:4], b_[:, 8 + NBIS - 1:8 + NBIS], -0.5, b_[:, 4:5], ALU.mult, ALU.add,
                              [(bk_, "wt"), (bk_, "mid")], [(bk_, "lo")])
                        S.ts("dve", Mt[0:rows, 0:ncols], sct[0:rows, 0:ncols], b_[:, 3:4], None, ALU.is_ge, None,
                             allb + [(bk_, "lo")], [mk])
                        vl = list(c["vis"])
                        for i0 in range(0, len(vl), 4):
                            grp = vl[i0:i0 + 4]
                            p, pk = ps_s()
                            for gi, (ki, kt) in enumerate(grp):
                                S.mm(p[0:kt["nk"], gi * 128:gi * 128 + rows], Mt[0:rows, kt["sc0"]:kt["sc0"] + kt["nk"]],
                                     ident[0:rows, 0:rows], True, True, [mk, "ident"], [pk])
                            for gi, (ki, kt) in enumerate(grp):
                                S.copy("act" if gi % 2 else "dve", MT[0:kt["nk"], ki, so:so + rows],
                                       p[0:kt["nk"], gi * 128:gi * 128 + rows], [pk], [("dsMT", ki, so // 128)])

                    for s0_ in range(0, len(subs), 2):
                        cs = [idx_scores(si) for si in range(s0_, min(s0_ + 2, len(subs)))]
                        for c in cs:
                            bis_setup(c)
                        for it in range(NBIS):
                            for c in cs:
                                bis_iter(c, it)
                        for c in cs:
                            bis_final(c)
                    def dsa_head(hh, tl, qb=qb):
                        ch, po = hh // 2, (hh % 2) * 64

                        def qk_fn(kt, qc0_, ql, nq_):
                            nk = kt["nk"]
                            if kt["own"]:
                                ka = KT[po:po + 64, ch, kt["c0"]:kt["c0"] + nk]
                            else:
                                ka = KTc[po:po + 64, ch, kt["c0"]:kt["c0"] + nk]
                            return [(ka, QT[po:po + 64, ch, qc0_ + ql:qc0_ + nq_])]

                        def v_fn(kt):
                            if kt["own"]:
                                return Vt[0:kt["nk"], kt["vi"], hh, :], ("dsV", kt["vi"])
                            return Vc[0:kt["nk"], kt["vi"], hh, :], "dsVc"

                        def out_fn(qb_, O, ok):
                            for (so, rows, oti) in qb_[3]:
                                sidx = so // 128
                                rc, rck = rot_buf("rc", 4, [128, 1], F32)
                                S.recip(rc[0:rows, :], O[0:rows, sidx * 65 + 64:sidx * 65 + 65], [ok], [rck])
                                S.ts("dve", ostage[0:rows, oti, hh * 64:hh * 64 + 64], O[0:rows, sidx * 65:sidx * 65 + 64],
                                     rc[0:rows, 0:1], None, ALU.mult, None, [ok, rck], [("dsO", oti)])

                        u1 = Group()
                        u1.sidx = u.sidx
                        u1.qblocks = [qb]
                        u1.keys = u.keys
                        kq_keys = ([("dsQT", i) for i in range(nblk)] + [("dsKT", i) for i in range(nblk)] + ["dsKTc"])
                        attn_softmax(u1, qk_fn, v_fn, sc, 4 + hh,
                                     (MT, lambda ki: [("dsMT", ki, s_) for s_ in range(4)]), out_fn, kq_keys, "ds", collect=tl)

                    tl = []
                    for hh in range(H):
                        dsa_head(hh, tl)
                    pipeline(tl, 4 if len(tl) > 8 else 3)
            finalize_o(ostage, "dsO", 3)
        S.barrier()
        rot.clear()


def phase_b(nc, S, cfg, dr, g, l, xT, oT, ident, ps_s, ps_o, xres_scr, layer_norm, to_xT, bcast_row, ln_pipeline):
    SEQ, NB, DEC, NS, PAST, DEPTH, TT = cfg.SEQ, cfg.NB, cfg.DEC, cfg.NS, cfg.PAST, cfg.DEPTH, cfg.TT
    ALPHA = cfg.DN_ALPHA
    last = (l == DEPTH - 1)
    stB = contextlib.ExitStack()

    def sb(name, shape, dt):
        _UID[0] += 1
        return stB.enter_context(nc.sbuf_tensor("%s_%d" % (name, _UID[0]), list(shape), dt))

    w_in = dr["w_in"][l].rearrange("(kc p) e -> p kc e", p=128)
    with stB:
        gA = sb("gA", [128, D], F32); bA = sb("bA", [128, D], F32)
        bf2 = sb("bf2", [128, D], F32)
        bf1 = sb("bf1", [128, 32], F32)
        bcast_row(bf2[:], dr["b_ff2"][l:l + 1, :], "bf2")
        S.dma("sp", bf1[:], dr["b_ff1_t"][l], (), ["bf1"])
        xr = sb("xr", [128, 4, D], F32)
        hT = sb("hT", [128, 32, 512], BF16)
        wbs = [sb("wbs%d" % i, [128, 2, 512], BF16) for i in range(3)]
        wf = [sb("wf%d" % i, [128, KC, 512], BF16) for i in range(3)]
        sg = [sb("sg%d" % i, [128, 512], F32) for i in range(2)]
        macc = sb("macc", [128, KC, 512], F32)
        rl = [sb("rl%d" % i, [128, 512], F32) for i in range(2)]
        junkB = sb("junkB", [128, D], BF16)
        wks = [dict(st=sb("stB%d" % i, [128, 8], F32), stk="stB%d" % i, junk=junkB, junkk="junkB",
                    xb=sb("xbB%d" % i, [128, D], BF16), xbk="xbB%d" % i) for i in range(4)]
        WB = dr["WB"]
        w_inb = WB["w_in"][l].rearrange("(kc p) e -> p kc e", p=128)
        gates = w_inb[:, :, GATE0:D_IN].rearrange("p k (n e) -> p k n e", n=4)
        wbr_v = WB["w_br"][l].rearrange("(n c p) e -> p n c e", p=128, c=2)
        wout_v = WB["w_out"][l].rearrange("(c p) e -> p c e", p=128)
        kin, kbr, kout, kf1, kf2 = [("wb", nm, l) for nm in ("w_in", "w_br", "w_out", "w_ff1", "w_ff2")]
        wctr = [0, 0, 0]
        w1 = WB["w_ff1"][l].rearrange("(kc p) f -> p kc f", p=128)
        w2 = WB["w_ff2"][l].rearrange("(fc p) e -> p fc e", p=128)
        for (c0, nb) in g.blocks:
            tiles = [(t0, r) for (t0, r) in g.ln_tiles if c0 <= t0 < c0 + nb]
            for ti, (t0, r) in enumerate(tiles):
                S.dma("sp", xr[0:r, ti, :], xres_scr[t0:t0 + r, :], [("xres", t0 // 128)], [("xr", ti)])
            xkeys = [("xT", i) for i in range(c0 // 128, (c0 + nb - 1) // 128 + 1)]
            okeys = [("oT", n, i) for n in range(4) for i in range(c0 // 128, (c0 + nb - 1) // 128 + 1)]
            for n in range(4):
                for hf in range(2):
                    wft = wf[wctr[1] % 3]; wfk = "wf%d" % (wctr[1] % 3); wctr[1] += 1
                    S.dma("sp", wft[:], gates[:, :, n, hf * 512:(hf + 1) * 512], [kin], [wfk])
                    wbt = wbs[wctr[2] % 3]; wbk = "wbs%d" % (wctr[2] % 3); wctr[2] += 1
                    S.dma("sp", wbt[:], wbr_v[:, n, :, hf * 512:(hf + 1) * 512], [kbr], [wbk])
                    for j in range(4):
                        dmc = hf * 4 + j
                        pg, pgk = ps_s()
                        for kc in range(KC):
                            S.mm(pg[:, 0:nb], wft[:, kc, j * 128:(j + 1) * 128], xT[:, kc, c0:c0 + nb], kc == 0, kc == KC - 1,
                                 [wfk] + xkeys, [pgk])
                        pb, pbk = ps_s()
                        for wc in range(2):
                            S.mm(pb[:, 0:nb], wbt[:, wc, j * 128:(j + 1) * 128], oT[:, 2 * n + wc, c0:c0 + nb],
                                 wc == 0, wc == 1, [wbk] + okeys, [pbk])
                        sgt = sg[dmc % 2]; sgk = "sg%d" % (dmc % 2)
                        mk_ = ("macc", dmc)
                        S.act(sgt[:, 0:nb], pg[:, 0:nb], AF.Sigmoid, [pgk], [sgk])
                        if n == 0:
                            S.tt("dve", macc[:, dmc, 0:nb], sgt[:, 0:nb], pb[:, 0:nb], ALU.mult, [sgk, pbk], [mk_])
                        else:
                            S.tt("dve", sgt[:, 0:nb], sgt[:, 0:nb], pb[:, 0:nb], ALU.mult, [sgk, pbk], [sgk])
                            if n < 3:
                                S.tt("pool", macc[:, dmc, 0:nb], macc[:, dmc, 0:nb], sgt[:, 0:nb], ALU.add, [mk_, sgk], [mk_])
                            else:
                                S.tt("pool", hT[:, 24 + dmc, 0:nb], macc[:, dmc, 0:nb], sgt[:, 0:nb], ALU.add, [mk_, sgk],
                                     [("hT", 24 + dmc)])
            mkeys = [("hT", 24 + i) for i in range(KC)]
            bcast_row(gA[:], dr["ln1_g"][l:l + 1, :], "gA")
            bcast_row(bA[:], dr["ln1_b"][l:l + 1, :], "bA")
            for hf in range(2):
                wft = wf[wctr[1] % 3]; wfk = "wf%d" % (wctr[1] % 3); wctr[1] += 1
                S.dma("sp", wft[:], wout_v[:, :, hf * 512:(hf + 1) * 512], [kout], [wfk])
                for ti, (t0, r) in enumerate(tiles):
                    lo = t0 - c0
                    py, pyk = ps_o()
                    for dmc in range(KC):
                        S.mm(py[0:r, :], hT[:, 24 + dmc, lo:lo + r], wft[:, dmc, :], dmc == 0, dmc == KC - 1,
                             mkeys + [wfk], [pyk])
                    S.stt("dve", xr[0:r, ti, hf * 512:(hf + 1) * 512], xr[0:r, ti, hf * 512:(hf + 1) * 512], ALPHA,
                          py[0:r, :], ALU.mult, ALU.add, [("xr", ti), pyk], [("xr", ti)])
            items = [(xr[0:r, ti, :], r, ("xr", ti), t0, wks[ti % 4], None) for ti, (t0, r) in enumerate(tiles)]
            ln_pipeline(items, gA, bA, ["gA", "bA"], lambda it_: True, beng="pool", ceng="act")
            for fb in range(DFF // 512):
                wft = wf[wctr[1] % 3]; wfk = "wf%d" % (wctr[1] % 3); wctr[1] += 1
                S.dma("sp", wft[:], w1[:, :, fb * 512:(fb + 1) * 512], [kf1], [wfk])
                for j in range(4):
                    fc = fb * 4 + j
                    ph, phk = ps_s()
                    for kc in range(KC):
                        S.mm(ph[:, 0:nb], wft[:, kc, j * 128:(j + 1) * 128], xT[:, kc, c0:c0 + nb], kc == 0, kc == KC - 1,
                             [wfk] + xkeys, [phk])
                    rt = rl[fc % 2]; rk = "rl%d" % (fc % 2)
                    S.act(rt[:, 0:nb], ph[:, 0:nb], AF.Relu, [phk, "bf1"], [rk], bias=bf1[:, fc:fc + 1])
                    S.tt("pool" if fc % 2 else "dve", hT[:, fc, 0:nb], rt[:, 0:nb], rt[:, 0:nb], ALU.mult, [rk], [("hT", fc)])
            hkeys = [("hT", i) for i in range(32)]
            bcast_row(gA[:], dr["ln2_g"][l:l + 1, :], "gA")
            bcast_row(bA[:], dr["ln2_b"][l:l + 1, :], "bA")
            for hf in range(2):
                accs = [ps_o() if i < 3 else ps_s() for i in range(len(tiles))]
                for fb in range(DFF // 512):
                    wft = wf[wctr[1] % 3]; wfk = "wf%d" % (wctr[1] % 3); wctr[1] += 1
                    S.dma("sp", wft[:, 0:4, :], w2[:, fb * 4:fb * 4 + 4, hf * 512:(hf + 1) * 512], [kf2], [wfk])
                    for j in range(4):
                        fc = fb * 4 + j
                        for ti, (t0, r) in enumerate(tiles):
                            lo = t0 - c0
                            S.mm(accs[ti][0][0:r, :], hT[:, fc, lo:lo + r], wft[:, j, :], fc == 0, fc == 31,
                                 hkeys + [wfk], [accs[ti][1]])
                for ti, (t0, r) in enumerate(tiles):
                    sl = slice(hf * 512, (hf + 1) * 512)
                    S.stt("dve", xr[0:r, ti, sl], xr[0:r, ti, sl], ALPHA, bf2[0:r, sl], ALU.mult, ALU.add,
                          [("xr", ti), "bf2"], [("xr", ti)])
                    S.tt("dve", xr[0:r, ti, sl], xr[0:r, ti, sl], accs[ti][0][0:r, :], ALU.add,
                         [("xr", ti), accs[ti][1]], [("xr", ti)])
            def after2(it_):
                ap, r, key, t0 = it_[0], it_[1], it_[2], it_[3]
                if last:
                    if g.kind == "p":
                        if t0 < SEQ:
                            S.dma("sp", dr["y_p"][g.idx, t0:t0 + r, :], ap, [key], [("y", t0)])
                    else:
                        S.dma("sp", dr["y_s"][t0:t0 + r, :], ap, [key], [("y", t0)])
                    return False
                S.dma("sp", xres_scr[t0:t0 + r, :], ap, [key], [("xres", t0 // 128)])
                return True

            items = [(xr[0:r, ti, :], r, ("xr", ti), t0, wks[ti % 4], None) for ti, (t0, r) in enumerate(tiles)]
            ln_pipeline(items, gA, bA, ["gA", "bA"], after2, beng="pool", ceng="act")

_CACHE = {}


def _get_program(cfg_key):
    if cfg_key not in _CACHE:
        cfg = Cfg(*cfg_key)
        nc, S = build(cfg)
        _CACHE[cfg_key] = (cfg, nc, S)
    return _CACHE[cfg_key]


def run(inputs, cfg_key):
    cfg, nc, S = _get_program(cfg_key)
    NB, NS, NCO = cfg.NB, cfg.NS, cfg.NCORES
    f32 = lambda a: np.ascontiguousarray(np.asarray(a, dtype=np.float32))
    consts = host_constants(cfg)
    shared = dict(consts)
    shared["meta"] = f32(inputs["meta"])
    shared["ln_in_g"] = f32(inputs["ln_in_g"]).reshape(1, D)
    shared["ln_in_b"] = f32(inputs["ln_in_b"]).reshape(1, D)
    shared["w_in"] = f32(inputs["w_in"])
    shared["qn_g"] = f32(inputs["mla_qnorm_g"])
    shared["w_uq"] = f32(inputs["mla_w_uq"])
    shared["kvn_g"] = f32(inputs["mla_kvnorm_g"])
    shared["w_uk"] = f32(inputs["mla_w_uk"]).reshape(cfg.DEPTH, 128, 256)
    shared["w_uv"] = f32(inputs["mla_w_uv"]).reshape(cfg.DEPTH, 128, 256)
    shared["lam_p"] = f32(inputs["diff_lambda"]).reshape(cfg.DEPTH, 128)
    shared["subln_g"] = f32(inputs["diff_subln_g"])
    shared["rel_bias"] = f32(inputs["rel_bias"])
    shared["w_br"] = f32(inputs["w_br"])
    shared["w_out"] = f32(inputs["w_out"])
    shared["ln1_g"] = f32(inputs["ln1_g"]); shared["ln1_b"] = f32(inputs["ln1_b"])
    shared["w_ff1"] = f32(inputs["w_ff1"])
    shared["b_ff1_t"] = f32(np.asarray(inputs["b_ff1"]).reshape(cfg.DEPTH, 32, 128).transpose(0, 2, 1))
    shared["w_ff2"] = f32(inputs["w_ff2"])
    shared["b_ff2"] = f32(inputs["b_ff2"])
    shared["ln2_g"] = f32(inputs["ln2_g"]); shared["ln2_b"] = f32(inputs["ln2_b"])
    xp = np.asarray(inputs["x_prompt"], dtype=np.float32)
    xs = np.asarray(inputs["x_sample"], dtype=np.float32)
    cache_in = {n: np.asarray(inputs["cache_" + n], dtype=np.float32) for n in CACHE_NAMES}
    in_maps = []
    for c in range(NCO):
        m = dict(shared)
        m["xp"] = np.ascontiguousarray(xp[c * NB:(c + 1) * NB])
        m["xs"] = np.ascontiguousarray(xs[c * NS:(c + 1) * NS].reshape(NS * cfg.DEC, D))
        for n, f in zip(CACHE_NAMES, CACHE_F):
            a = cache_in[n][:, c * NS:(c + 1) * NS]
            m["c_" + n] = np.ascontiguousarray(a.reshape(cfg.DEPTH, NS, cfg.PAST, f))
        in_maps.append(m)
    res = run_bass_kernel_spmd(nc, in_maps, core_ids=list(range(NCO)))
    R = res.results
    y_p = np.concatenate([r["y_p"] for r in R], axis=0)
    y_s = np.concatenate([r["y_s"].reshape(NS, cfg.DEC, D) for r in R], axis=0)
    trail = {"mla_kv": (128,), "mla_pe": (32,), "sb_k": (4, 64), "sb_v": (4, 64), "diff_k": (4, 2, 32),
             "diff_v": (4, 64), "dsa_k": (4, 64), "dsa_v": (4, 64), "dsa_kidx": (32,)}
    outs = [y_p, y_s]
    for n in CACHE_NAMES:
        a = np.concatenate([r["p_" + n] for r in R], axis=1)
        outs.append(a.reshape(cfg.DEPTH, cfg.BATCH, cfg.TT, *trail[n]))
    for n in CACHE_NAMES:
        a = np.concatenate([r["s_" + n].reshape(cfg.DEPTH, NS, cfg.DEC, -1) for r in R], axis=1)
        outs.append(a.reshape(cfg.DEPTH, cfg.DEC_BATCH, cfg.DEC, *trail[n]))
    return tuple(np.ascontiguousarray(o.astype(np.float32)) for o in outs)


def kernel(**inputs):
    return run(inputs, (2048, 2, 32, 4, 1024, 2, 16, 32))
```

```python
import math
import contextlib
import numpy as np
import concourse.bass as bass
import concourse.mybir as mybir
from concourse.bass_utils import run_bass_kernel_spmd

F32 = mybir.dt.float32
BF16 = mybir.dt.bfloat16
AF = mybir.ActivationFunctionType
ALU = mybir.AluOpType
AX = mybir.AxisListType

D = 1024
KC = 8
NMETA = 16
CHUNK = 64
H = 4
D_IN = 7112
DFF = 4096
GATE0 = 3016
LN_EPS = 1e-5
RMS_EPS = 1e-6
NBIS = 16
NEGBIG = -1.0e30

CACHE_NAMES = ["mla_kv", "mla_pe", "sb_k", "sb_v", "diff_k", "diff_v", "dsa_k", "dsa_v", "dsa_kidx"]
CACHE_F = [128, 32, 256, 256, 256, 256, 256, 256, 32]


class Cfg:
    def __init__(self, SEQ=2048, NB=2, DEC=32, NS=4, PAST=1024, DEPTH=2, BATCH=16, DEC_BATCH=32):
        self.SEQ, self.NB, self.DEC, self.NS, self.PAST, self.DEPTH = SEQ, NB, DEC, NS, PAST, DEPTH
        self.BATCH, self.DEC_BATCH = BATCH, DEC_BATCH
        self.TT = SEQ + NMETA
        self.TOPK_P = min(256, SEQ // 4)
        self.TOPK_S = min(256, (PAST + DEC) // 4)
        self.DN_ALPHA = (2 * DEPTH) ** 0.25
        self.NCORES = BATCH // NB
        assert DEC_BATCH // NS == self.NCORES


class Sched:
    def __init__(s, nc, st):
        s.nc = nc
        s.E = dict(pe=nc.tensor, act=nc.scalar, dve=nc.vector, pool=nc.gpsimd, sp=nc.sync)
        s.csem = {e: st.enter_context(nc.semaphore("c_" + e)) for e in ("pe", "act", "dve", "pool")}
        s.ccnt = {e: 0 for e in s.csem}
        s.NQ = 8
        s.qsem = {q: [st.enter_context(nc.semaphore("q_%s%d" % (q, i))) for i in range(s.NQ)]
                  for q in ("sp", "pool")}
        s.qn = {q: 0 for q in s.qsem}
        s.qsb = {q: [0] * s.NQ for q in s.qsem}
        s.seen = {}
        s.lastw = {}
        s.readers = {}
        s.nins = 0

    def _wait(s, eng, tok):
        key, h, val = tok
        k = (eng, key)
        if s.seen.get(k, 0) >= val:
            return
        s.E[eng].wait_ge(h, val)
        s.seen[k] = val
        s.nins += 1

    def _deps(s, eng, reads, writes):
        own = ("c", eng)
        for r in reads:
            w = s.lastw.get(r)
            if w is not None and not (w[0] == own and eng == "pe"):
                s._wait(eng, w)
        for r in writes:
            w = s.lastw.get(r)
            if w is not None and not (w[0] == own and eng == "pe"):
                s._wait(eng, w)
            for t in s.readers.get(r, {}).values():
                if not (t[0] == own and eng == "pe"):
                    s._wait(eng, t)

    def _commit(s, tok, reads, writes, rk):
        for r in writes:
            s.lastw[r] = tok
            s.readers[r] = {}
        for r in reads:
            s.readers.setdefault(r, {})[rk] = tok

    def op(s, eng, fn, reads=(), writes=()):
        pr = [r for r in reads if isinstance(r, tuple) and r[0] in ("psS", "psO")]
        if pr:
            writes = list(writes) + [r for r in pr if r not in writes]
        s._deps(eng, reads, writes)
        ins = fn(s.E[eng])
        s.ccnt[eng] += 1
        ins.then_inc(s.csem[eng], 1)
        s.nins += 1
        tok = (("c", eng), s.csem[eng], s.ccnt[eng])
        s._commit(tok, reads, writes, eng)

    def dma(s, q, out, in_, reads=(), writes=(), dram_only=False, **kw):
        s._deps(q, reads, writes)
        i = s.qn[q]
        slot = i % s.NQ
        gen = i // s.NQ
        h = s.qsem[q][slot]
        key = ("q", q, slot)
        if gen > 0:
            s._wait(q, (key, h, 16 * gen))
        s.E[q].dma_start(out=out, in_=in_, **kw).then_inc(h, 16)
        s.nins += 1
        s.qn[q] += 1
        tok = (key, h, 16 * (gen + 1))
        if not dram_only:
            s.qsb[q][slot] = 16 * (gen + 1)
        s._commit(tok, reads, writes, key)

    def all_tokens(s):
        toks = []
        for e in s.csem:
            if s.ccnt[e] > 0:
                toks.append((("c", e), s.csem[e], s.ccnt[e]))
        for q in s.qsem:
            for slot in range(s.NQ):
                n = s.qn[q]
                cnt = n // s.NQ + (1 if slot < n % s.NQ else 0)
                if cnt > 0:
                    toks.append((("q", q, slot), s.qsem[q][slot], 16 * cnt))
        return toks

    def barrier(s):
        toks = []
        for e in s.csem:
            if s.ccnt[e] > 0:
                toks.append((("c", e), s.csem[e], s.ccnt[e]))
        for q in s.qsem:
            for slot in range(s.NQ):
                if s.qsb[q][slot] > 0:
                    toks.append((("q", q, slot), s.qsem[q][slot], s.qsb[q][slot]))
        for eng in ("pe", "act", "dve", "pool", "sp"):
            for t in toks:
                s._wait(eng, t)
        s.lastw = {k: v for k, v in s.lastw.items() if isinstance(k, tuple) and k[0] in ("wb", "cb")}
        s.readers = {}

    def finish(s):
        for t in s.all_tokens():
            s._wait("sp", t)

    def mm(s, out, lhsT, rhs, start, stop, r, w, skip=False):
        if skip:
            s.op("pe", lambda e: e.matmul(out, lhsT=lhsT, rhs=rhs, start=start, stop=stop, skip_group_check=True), r, w)
        else:
            s.op("pe", lambda e: e.matmul(out, lhsT=lhsT, rhs=rhs, start=start, stop=stop), r, w)

    def act(s, out, in_, func, r, w, bias=0.0, scale=1.0, accum_out=None):
        if accum_out is None:
            s.op("act", lambda e: e.activation(out=out, in_=in_, func=func, bias=bias, scale=scale), r, w)
        else:
            s.op("act", lambda e: e.activation(out=out, in_=in_, func=func, bias=bias, scale=scale,
                                               accum_out=accum_out), r, w)

    def tt(s, eng, out, in0, in1, op, r, w):
        s.op(eng, lambda e: e.tensor_tensor(out=out, in0=in0, in1=in1, op=op), r, w)

    def ts(s, eng, out, in0, s1, s2, op0, op1, r, w, accum_out=None):
        if op1 is None:
            s.op(eng, lambda e: e.tensor_scalar(out=out, in0=in0, scalar1=s1, scalar2=None, op0=op0), r, w)
        elif accum_out is None:
            s.op(eng, lambda e: e.tensor_scalar(out=out, in0=in0, scalar1=s1, scalar2=s2, op0=op0, op1=op1), r, w)
        else:
            s.op(eng, lambda e: e.tensor_scalar(out=out, in0=in0, scalar1=s1, scalar2=s2, op0=op0, op1=op1,
                                                accum_out=accum_out), r, w)

    def stt(s, eng, out, in0, scalar, in1, op0, op1, r, w):
        s.op(eng, lambda e: e.scalar_tensor_tensor(out=out, in0=in0, scalar=scalar, in1=in1, op0=op0, op1=op1), r, w)

    def copy(s, eng, out, in_, r, w):
        if eng == "act":
            s.op("act", lambda e: e.activation(out=out, in_=in_, func=AF.Copy), r, w)
        else:
            s.op(eng, lambda e: e.tensor_copy(out=out, in_=in_), r, w)

    def memset(s, eng, ap, val, w):
        s.op(eng, lambda e: e.memset(ap, val), (), w)

    def reduce(s, out, in_, op, r, w):
        s.op("dve", lambda e: e.tensor_reduce(out=out, in_=in_, axis=AX.X, op=op), r, w)

    def recip(s, out, in_, r, w):
        s.op("dve", lambda e: e.reciprocal(out=out, in_=in_), r, w)


def _t5_bucket_np(rel):
    import jax
    import jax.numpy as jnp
    with jax.default_device(jax.devices("cpu")[0]):
        return _t5_bucket_impl(jnp, rel)


def _t5_bucket_impl(jnp, rel):
    rel = jnp.asarray(np.asarray(rel), dtype=jnp.int32)
    nb = 16
    max_exact = 8
    n = jnp.abs(rel)
    nf = jnp.maximum(n, 1).astype(jnp.float32)
    large = max_exact + (jnp.log(nf / max_exact) / math.log(128 / max_exact) * (nb - max_exact)).astype(jnp.int32)
    large = jnp.minimum(large, nb - 1)
    return np.asarray(jnp.where(rel > 0, nb, 0) + jnp.where(n < max_exact, n, large))


def host_constants(cfg):
    c = {}
    c["c_ident"] = np.eye(128, dtype=np.float32)
    c["c_J"] = np.eye(128, dtype=np.float32)[::-1].copy()
    j = np.arange(128)
    c["c_utri"] = (j[:, None] >= j[None, :]).astype(np.float32)
    c["c_cmask"] = (j[:, None] < j[None, :]).astype(np.float32)
    c["c_ones"] = np.ones((128, 128), np.float32)
    p = np.arange(128)
    mA = ((p // 32) % 2 == 0).astype(np.float32)
    c["c_mab"] = np.stack([mA, 1.0 - mA], axis=1).astype(np.float32)
    rel = 127 - np.arange(384)
    b = _t5_bucket_np(rel)
    oh = np.zeros((32, 384), np.float32)
    oh[b, np.arange(384)] = 1.0
    c["c_oh"] = oh
    bf = int(_t5_bucket_np(np.array([-1000]))[0])
    ohf = np.zeros((32, 128), np.float32)
    ohf[bf, :] = 1.0
    c["c_ohfar"] = ohf
    c["c_pow2"] = np.tile((0.5 ** (np.arange(NBIS) + 1)).astype(np.float32)[None], (128, 1))
    half = 16
    inv = (10000.0 ** (-np.arange(half, dtype=np.float32) / half)).astype(np.float32)

    def tabs(pos, scale):
        ang = pos.astype(np.float32)[:, None] * inv[None, :]
        cs = np.cos(ang).astype(np.float32)
        sn = np.sin(ang).astype(np.float32)
        cos2 = np.concatenate([cs, cs], axis=1) * scale
        sin2 = np.concatenate([-sn, sn], axis=1) * scale
        return np.concatenate([cos2, sin2], axis=1).astype(np.float32)

    pos_p = np.concatenate([NMETA + np.arange(cfg.SEQ), np.arange(NMETA)])
    pos_s = np.tile(NMETA + cfg.PAST + np.arange(cfg.DEC), cfg.NS)
    qs = 96.0 ** -0.5
    c["c_rope_p"] = np.concatenate([tabs(pos_p, 1.0), tabs(pos_p, qs)], axis=1)
    c["c_rope_s"] = np.concatenate([tabs(pos_s, 1.0), tabs(pos_s, qs)], axis=1)
    return c


class Group:
    pass


class _Stop(Exception):
    pass


def _stop(tag):
    import os
    return os.environ.get("KSTOP") == tag


_UID = [0]


def build(cfg):
    nc = bass.Bass("TRN2", target_bir_lowering=False)
    SEQ, NB, DEC, NS, PAST, DEPTH, TT = cfg.SEQ, cfg.NB, cfg.DEC, cfg.NS, cfg.PAST, cfg.DEPTH, cfg.TT
    NPT = PAST // 128
    ALPHA = cfg.DN_ALPHA
    dr = {}

    def din(name, shape):
        dr[name] = nc.dram_tensor(name, list(shape), F32, kind="ExternalInput").ap()

    def dout(name, shape):
        dr[name] = nc.dram_tensor(name, list(shape), F32, kind="ExternalOutput").ap()

    din("xp", [NB, SEQ, D])
    din("xs", [NS * DEC, D])
    for n, f in zip(CACHE_NAMES, CACHE_F):
        din("c_" + n, [DEPTH, NS, PAST, f])
    din("meta", [NMETA, D])
    din("ln_in_g", [1, D]); din("ln_in_b", [1, D])
    din("w_in", [DEPTH, D, D_IN])
    din("qn_g", [DEPTH, 256]); din("w_uq", [DEPTH, 256, 384]); din("kvn_g", [DEPTH, 128])
    din("w_uk", [DEPTH, 128, 256]); din("w_uv", [DEPTH, 128, 256])
    din("lam_p", [DEPTH, 128]); din("subln_g", [DEPTH, 64]); din("rel_bias", [32, 8])
    din("w_br", [DEPTH, 4, 256, D]); din("w_out", [DEPTH, D, D])
    din("ln1_g", [DEPTH, D]); din("ln1_b", [DEPTH, D])
    din("w_ff1", [DEPTH, D, DFF]); din("b_ff1_t", [DEPTH, 128, 32]); din("w_ff2", [DEPTH, DFF, D])
    din("b_ff2", [DEPTH, D]); din("ln2_g", [DEPTH, D]); din("ln2_b", [DEPTH, D])
    for n, shp in (("c_ident", [128, 128]), ("c_J", [128, 128]), ("c_utri", [128, 128]), ("c_cmask", [128, 128]),
                   ("c_ones", [128, 128]), ("c_mab", [128, 2]), ("c_oh", [32, 384]), ("c_ohfar", [32, 128]),
                   ("c_pow2", [128, NBIS]), ("c_rope_p", [TT, 128]), ("c_rope_s", [NS * DEC, 128])):
        din(n, shp)
    dout("y_p", [NB, SEQ, D])
    dout("y_s", [NS * DEC, D])
    for n, f in zip(CACHE_NAMES, CACHE_F):
        dout("p_" + n, [DEPTH, NB, TT, f])
        dout("s_" + n, [DEPTH, NS * DEC, f])
    bias_scr = nc.dram_tensor("bias_scr", [8, 384], F32, kind="Internal").ap()
    WB = {}
    for nm, shp in (("w_in", [DEPTH, D, D_IN]), ("w_uq", [DEPTH, 256, 384]), ("w_uk", [DEPTH, 128, 256]),
                    ("w_uv", [DEPTH, 128, 256]), ("w_br", [DEPTH, 4 * 256, D]), ("w_out", [DEPTH, D, D]),
                    ("w_ff1", [DEPTH, D, DFF]), ("w_ff2", [DEPTH, DFF, D])):
        WB[nm] = nc.dram_tensor("wb_" + nm, list(shp), BF16, kind="Internal").ap()
    dr["WB"] = WB
    CB = {}
    for n, f in zip(CACHE_NAMES, CACHE_F):
        CB[n] = nc.dram_tensor("cb_" + n, [DEPTH, NS, PAST, f], BF16, kind="Internal").ap()
    dr["CB"] = CB
    xres_scr = nc.dram_tensor("xres_scr", [TT, D], F32, kind="Internal").ap()

    st = contextlib.ExitStack()
    with st:
        S = Sched(nc, st)

        def sb(name, shape, dt, stack=st):
            _UID[0] += 1
            return stack.enter_context(nc.sbuf_tensor("%s_%d" % (name, _UID[0]), list(shape), dt))

        psS = [st.enter_context(nc.psum_tensor("psS%d" % i, [128, 512], F32)) for i in range(5)]
        psO = [st.enter_context(nc.psum_tensor("psO%d" % i, [128, 512], F32)) for i in range(3)]
        pctr = {"S": 0, "O": 0}

        def ps_s():
            i = pctr["S"] % len(psS); pctr["S"] += 1
            return psS[i], ("psS", i)

        def ps_o():
            i = pctr["O"] % len(psO); pctr["O"] += 1
            return psO[i], ("psO", i)

        ident = sb("ident", [128, 128], BF16)
        utri = sb("utri", [128, 128], BF16)
        cmask = sb("cmask", [128, 128], BF16)
        ones = sb("ones", [128, 128], BF16)
        mab = sb("mab", [128, 2], F32)
        pow2 = sb("pow2", [128, NBIS], F32)
        BT = sb("BT", [128, 8, 256], F32)
        cbias = sb("cbias", [128, 8], F32)
        cst = sb("cst", [128, 4], F32)
        S.memset("dve", cst[:, 0:1], LN_EPS, ["cst"])
        S.memset("dve", cst[:, 1:2], RMS_EPS, ["cst"])
        S.memset("dve", cst[:, 2:3], 1.0, ["cst"])
        S.dma("pool", ident[:], dr["c_ident"], (), ["ident"])
        S.dma("pool", utri[:], dr["c_utri"], (), ["utri"])
        S.dma("pool", cmask[:], dr["c_cmask"], (), ["cmask"])
        S.dma("pool", ones[:], dr["c_ones"], (), ["ones"])
        S.dma("sp", mab[:], dr["c_mab"], (), ["mab"])
        S.dma("sp", pow2[:], dr["c_pow2"], (), ["pow2"])
        with contextlib.ExitStack() as st0:
            Jt = sb("Jt", [128, 128], F32, st0)
            relb = sb("relb", [32, 8], F32, st0)
            oh = sb("oh", [32, 384], F32, st0)
            ohf = sb("ohf", [32, 128], F32, st0)
            arev = sb("arev", [8, 384], F32, st0)
            Gt = sb("Gt", [128, 8, 256], F32, st0)
            S.dma("sp", Jt[:], dr["c_J"], (), ["Jt"])
            S.dma("sp", relb[:], dr["rel_bias"], (), ["relb"])
            S.dma("sp", oh[:], dr["c_oh"], (), ["oh"])
            S.dma("sp", ohf[:], dr["c_ohfar"], (), ["ohf"])
            p, pk = ps_s()
            S.mm(p[0:8, 0:384], relb[:], oh[:], True, True, ["relb", "oh"], [pk])
            S.copy("act", arev[:], p[0:8, 0:384], [pk], ["arev"])
            S.dma("sp", bias_scr, arev[:], ["arev"], ["bias_scr"])
            src = bass.AP(tensor=bias_scr.tensor, offset=0, ap=[[1, 128], [384, 8], [1, 256]])
            S.dma("sp", Gt[:], src, ["bias_scr"], ["Gt"])
            for h in range(8):
                p, pk = ps_s()
                S.mm(p[:, 0:256], Jt[:], Gt[:, h, :], True, True, ["Jt", "Gt"], [pk])
                S.copy("act", BT[:, h, :], p[:, 0:256], [pk], ["BT"])
            p, pk = ps_s()
            S.mm(p[:, 0:8], ohf[:], relb[:], True, True, ["ohf", "relb"], [pk])
            S.copy("act", cbias[:], p[:, 0:8], [pk], ["cbias"])
            S.barrier()

        def conv(nm, l, rows_per, deps=()):
            src = dr[nm][l] if nm != "w_br" else dr[nm][l].rearrange("n w e -> (n w) e")
            dst = WB[nm][l]
            nrow = dst.shape[0]
            for r0 in range(0, nrow, rows_per):
                S.dma("pool", dst[r0:r0 + rows_per, :], src[r0:r0 + rows_per, :], list(deps), [("wb", nm, l)], dram_only=True)

        conv("w_in", 0, 128)
        conv("w_uq", 0, 256); conv("w_uk", 0, 128); conv("w_uv", 0, 128)
        first = [("wb", "w_in", 0), ("wb", "w_uq", 0), ("wb", "w_uk", 0), ("wb", "w_uv", 0)]
        conv("w_br", 0, 256, first); conv("w_out", 0, 256, first); conv("w_ff1", 0, 128, first); conv("w_ff2", 0, 512, first)
        for l_ in range(1, DEPTH):
            conv("w_in", l_, 128, first)
            conv("w_uq", l_, 256, first); conv("w_uk", l_, 128, first); conv("w_uv", l_, 128, first)
            conv("w_br", l_, 256, first); conv("w_out", l_, 256, first); conv("w_ff1", l_, 128, first); conv("w_ff2", l_, 512, first)
        for l_ in range(DEPTH):
            for n_ in CACHE_NAMES:
                for s_ in range(NS):
                    S.dma("pool", CB[n_][l_, s_], dr["c_" + n_][l_, s_], first, [("cb", n_, l_)], dram_only=True)

        xT = sb("xT", [128, KC, TT], BF16)
        oT = sb("oT", [128, KC, TT], BF16)
        if not _stop("const"):
            _main(locals())
        S.finish()
    return nc, S


def _main(L):
    (nc, S, cfg, dr, st, sb, xT, oT, ident, utri, cmask, ones, mab, pow2, BT, cbias, cst, ps_s, ps_o,
     xres_scr) = [L[k] for k in ("nc", "S", "cfg", "dr", "st", "sb", "xT", "oT", "ident", "utri", "cmask", "ones", "mab",
                                 "pow2", "BT", "cbias", "cst", "ps_s", "ps_o", "xres_scr")]
    first_pa = [False]
    SEQ, NB, DEC, NS, PAST, DEPTH, TT = cfg.SEQ, cfg.NB, cfg.DEC, cfg.NS, cfg.PAST, cfg.DEPTH, cfg.TT
    if True:

        def r_xT(c0):
            return ("xT", c0 // 128)

        def r_oT(n, c0):
            return ("oT", n, c0 // 128)

        groups = []
        for b in range(NB):
            g = Group()
            g.kind = "p"; g.idx = b; g.ntok = TT
            g.tiles = [(128 * i, 128) for i in range(SEQ // 128)] + [(SEQ, NMETA)]
            g.ln_tiles = list(g.tiles)
            g.blocks = [(512 * i, 512) for i in range(SEQ // 512)] + [(SEQ, NMETA)]
            g.blocks_b = [(512 * i, 512) for i in range(SEQ // 512 - 1)] + [(SEQ - 512, 512 + NMETA)]
            g.rope = dr["c_rope_p"]
            g.topk = cfg.TOPK_P
            groups.append(g)
        g = Group()
        g.kind = "s"; g.idx = 0; g.ntok = NS * DEC
        g.tiles = [(DEC * s_, DEC) for s_ in range(NS)]
        g.ln_tiles = [(0, NS * DEC)]
        g.blocks = [(0, NS * DEC)]
        g.blocks_b = [(0, NS * DEC)]
        g.rope = dr["c_rope_s"]
        g.topk = cfg.TOPK_S
        groups.append(g)

        def out_rows(g, name, l, c0, rows):
            if g.kind == "p":
                t = dr["p_" + name]
                if c0 >= SEQ:
                    return t[l, g.idx, 0:rows, :]
                return t[l, g.idx, NMETA + c0:NMETA + c0 + rows, :]
            return dr["s_" + name][l, c0:c0 + rows, :]

        def ln_stats(z, rows, zkey, wk):
            st_ = wk["st"]
            S.memset("dve", st_[0:rows, 0:8], 0.0, [wk["stk"]])
            S.act(wk["junk"][0:rows, :], z, AF.Copy, [zkey, wk["stk"]], [wk["junkk"], wk["stk"] + "a"],
                  accum_out=st_[0:rows, 0:1])
            S.act(wk["junk"][0:rows, :], z, AF.Square, [zkey, wk["stk"]], [wk["junkk"], wk["stk"] + "b"],
                  accum_out=st_[0:rows, 1:2])
            rd = [wk["stk"], wk["stk"] + "a", wk["stk"] + "b"]
            S.ts("dve", st_[0:rows, 2:3], st_[0:rows, 0:1], 1.0 / D, None, ALU.mult, None, rd, [wk["stk"] + "c"])
            S.stt("dve", st_[0:rows, 3:4], st_[0:rows, 2:3], -1.0, st_[0:rows, 2:3], ALU.mult, ALU.mult,
                  [wk["stk"] + "c"], [wk["stk"] + "d"])
            S.stt("dve", st_[0:rows, 4:5], st_[0:rows, 1:2], 1.0 / D, st_[0:rows, 3:4], ALU.mult, ALU.add,
                  rd + [wk["stk"] + "d"], [wk["stk"] + "e"])
            S.act(st_[0:rows, 5:6], st_[0:rows, 4:5], AF.Ln, [wk["stk"] + "e", "cst"], [wk["stk"] + "f"],
                  bias=cst[0:rows, 0:1])
            S.act(st_[0:rows, 5:6], st_[0:rows, 5:6], AF.Exp, [wk["stk"] + "f"], [wk["stk"] + "f"], scale=-0.5)

        def ln_apply(z, rows, zkey, gt, bt, gkeys, out, okey, wk, beng="dve"):
            st_ = wk["st"]
            S.ts("dve", out, z, st_[0:rows, 2:3], st_[0:rows, 5:6], ALU.subtract, ALU.mult,
                 [zkey, wk["stk"] + "c", wk["stk"] + "f"], [okey])
            S.tt("dve", out, out, gt[0:rows, :], ALU.mult, [okey] + gkeys, [okey])
            S.tt(beng, out, out, bt[0:rows, :], ALU.add, [okey] + gkeys, [okey])

        def layer_norm(z, rows, zkey, gt, bt, gkeys, out, okey, wk):
            ln_stats(z, rows, zkey, wk)
            ln_apply(z, rows, zkey, gt, bt, gkeys, out, okey, wk)

        def ln_pipeline(items, gt, bt, gkeys, after, beng="dve", ceng="act"):
            n = len(items)
            for t in range(n + 2):
                if t < n:
                    ap, rows, key, c0, wk, ex = items[t]
                    ln_stats(ap, rows, key, wk)
                if 1 <= t <= n:
                    ap, rows, key, c0, wk, ex = items[t - 1]
                    ln_apply(ap, rows, key, gt, bt, gkeys, ap, key, wk, beng)
                    do_x = after(items[t - 1])
                    items[t - 1] = items[t - 1] + (do_x,)
                if 2 <= t:
                    it_ = items[t - 2]
                    if it_[6]:
                        to_xT(it_[0], it_[1], it_[2], it_[3], it_[4], ceng=ceng)

        def to_xT(src, rows, skey, c0, wk, dst=None, dkeyf=None, ceng="act"):
            dst = xT if dst is None else dst
            dkey = r_xT(c0) if dkeyf is None else dkeyf
            xb = wk["xb"]
            S.copy(ceng, xb[0:rows, :], src, [skey], [wk["xbk"]])
            for hf in range(2):
                p, pk = ps_s()
                for j in range(4):
                    kc = hf * 4 + j
                    S.mm(p[:, j * 128:j * 128 + rows], xb[0:rows, kc * 128:(kc + 1) * 128], ident[0:rows, 0:rows],
                         True, True, [wk["xbk"], "ident"], [pk])
                pv = p[:].rearrange("p (j t) -> p j t", j=4)[:, :, 0:rows]
                S.copy("dve" if hf == 0 else "act", dst[:, hf * 4:hf * 4 + 4, c0:c0 + rows], pv, [pk], [dkey])

        def bcast_row(dst, src_row, key, q="sp"):
            S.dma(q, dst, src_row.partition_broadcast(128) if len(src_row.shape) == 1 else
                  src_row.broadcast_to([128] + list(src_row.shape[1:])), (), [key])

        for g in groups:
            with contextlib.ExitStack() as s0:
                gt = sb("lng", [128, D], F32, s0)
                bt = sb("lnb", [128, D], F32, s0)
                zt = [sb("z0_%d" % i, [128, D], F32, s0) for i in range(3)]
                ot = [sb("o0_%d" % i, [128, D], F32, s0) for i in range(3)]
                junk0 = sb("junk0", [128, D], BF16, s0)
                wks = [dict(st=sb("st0%d" % i, [128, 8], F32, s0), stk="st0%d" % i, junk=junk0,
                            junkk="junk0", xb=sb("xb0%d" % i, [128, D], BF16, s0), xbk="xb0%d" % i) for i in range(3)]
                bcast_row(gt[:], dr["ln_in_g"], "lng")
                bcast_row(bt[:], dr["ln_in_b"], "lnb")
                pend = None
                for ti, (c0, rows) in enumerate(g.ln_tiles):
                    z = zt[ti % 3]; o = ot[ti % 3]
                    zk = "z0_%d" % (ti % 3); ok = "o0_%d" % (ti % 3)
                    if g.kind == "p":
                        src = dr["meta"] if c0 >= SEQ else dr["xp"][g.idx, c0:c0 + rows, :]
                    else:
                        src = dr["xs"][c0:c0 + rows, :]
                    S.dma("sp", z[0:rows, :], src, (), [zk])
                    wk = wks[ti % 3]
                    layer_norm(z[0:rows, :], rows, zk, gt, bt, ["lng", "lnb"], o[0:rows, :], ok, wk)
                    S.dma("sp", xres_scr[c0:c0 + rows, :], o[0:rows, :], [ok], [("xres", c0 // 128)])
                    if pend is not None:
                        to_xT(*pend)
                    pend = (o[0:rows, :], rows, ok, c0, wk)
                if pend is not None:
                    to_xT(*pend)
                S.barrier()
                if _stop("s0"):
                    return

            for l in range(DEPTH):
                cb_ = None
                if phase_a(nc, S, cfg, dr, g, l, xT, oT, ident, utri, cmask, ones, mab, pow2, BT, cbias,
                           ps_s, ps_o, out_rows, cst, cb_):
                    return
                S.barrier()
                if _stop("pa"):
                    return
                phase_b(nc, S, cfg, dr, g, l, xT, oT, ident, ps_s, ps_o, xres_scr, layer_norm, to_xT, bcast_row, ln_pipeline)
                S.barrier()
                if _stop("pb"):
                    return


def phase_a(nc, S, cfg, dr, g, l, xT, oT, ident, utri, cmask, ones, mab, pow2, BT, cbias, ps_s, ps_o, out_rows, cst, conv_cb=None):
    SEQ, NB, DEC, NS, PAST, DEPTH, TT = cfg.SEQ, cfg.NB, cfg.DEC, cfg.NS, cfg.PAST, cfg.DEPTH, cfg.TT
    NPT = PAST // 128
    NT = len(g.tiles)
    isP = g.kind == "p"
    TTg = g.ntok
    KW = TT if isP else max(PAST + DEC, 128)
    WB = dr["WB"]
    CB = dr["CB"]
    direct = conv_cb is not None
    if direct:
        w_in = dr["w_in"][l].rearrange("(kc p) e -> p kc e", p=128)
        wsrc = lambda nm: dr[nm][l]
        wq, wdep = "pool", lambda nm: []
    else:
        w_in = WB["w_in"][l].rearrange("(kc p) e -> p kc e", p=128)
        wsrc = lambda nm: WB[nm][l]
        wq, wdep = "sp", lambda nm: [("wb", nm, l)]
    lam_init = 0.8 - 0.6 * math.exp(-0.3 * l)
    stA = contextlib.ExitStack()

    def sb(name, shape, dt, stack=None):
        _UID[0] += 1
        return (stack or stA).enter_context(nc.sbuf_tensor("%s_%d" % (name, _UID[0]), list(shape), dt))

    def r_oT(n, c0):
        return ("oT", n, c0 // 128)

    units = []
    if isP:
        u = Group()
        u.sidx = 0
        u.qblocks = []
        for qb in range(SEQ // 512):
            u.qblocks.append((512 * qb, 512, 512 * qb, [(128 * a, 128, 4 * qb + a) for a in range(4)]))
        u.qblocks.append((SEQ, NMETA, -NMETA, [(0, NMETA, SEQ // 128)]))
        u.keys = [dict(own=True, c0=SEQ, nk=NMETA, fk0=-NMETA, vi=SEQ // 128, sc0=0)]
        for kt in range(SEQ // 128):
            u.keys.append(dict(own=True, c0=128 * kt, nk=128, fk0=128 * kt, vi=kt, sc0=NMETA + 128 * kt))
        units.append(u)
    else:
        for s_ in range(NS):
            u = Group()
            u.sidx = s_
            u.qblocks = [(DEC * s_, DEC, PAST, [(0, DEC, s_)])]
            u.keys = [dict(own=False, c0=128 * j, nk=128, fk0=128 * j, vi=j, sc0=128 * j) for j in range(NPT)]
            u.keys.append(dict(own=True, c0=DEC * s_, nk=DEC, fk0=PAST, vi=s_, sc0=PAST))
            units.append(u)

    def visible(kt, fq_lo, fq_hi, causal):
        fk0 = kt["fk0"]
        if fk0 < 0:
            return fq_lo
        if fq_lo < 0:
            return None
        if fk0 >= fq_hi:
            return None
        return max(fq_lo, fk0)

    def load_w(dst, c0, n, key, dcol=0):
        S.dma(wq, dst[:, :, dcol:dcol + n], w_in[:, :, c0:c0 + n], wdep("w_in"), [key])

    def proj_fm(wt, wkey, wc0, M, c0, n):
        p, pk = ps_s()
        for kc in range(KC):
            S.mm(p[0:M, 0:n], wt[:, kc, wc0:wc0 + M], xT[:, kc, c0:c0 + n], kc == 0, kc == KC - 1,
                 [wkey, ("xT", c0 // 128)] + ([("xT", (c0 + n - 1) // 128)] if n > 128 else []), [pk])
        return p, pk

    def xT_keys(c0, n):
        return [("xT", i) for i in range(c0 // 128, (c0 + n - 1) // 128 + 1)]

    def proj_fm2(wt, wkey, wc0, M, c0, n):
        p, pk = ps_s()
        rk = [wkey] + xT_keys(c0, n)
        for kc in range(KC):
            S.mm(p[0:M, 0:n], wt[:, kc, wc0:wc0 + M], xT[:, kc, c0:c0 + n], kc == 0, kc == KC - 1, rk, [pk])
        return p, pk

    def proj_tm(wt, wkey, wc0, ncols, c0, rows):
        p, pk = ps_s()
        rk = [wkey] + xT_keys(c0, rows)
        for kc in range(KC):
            S.mm(p[0:rows, 0:ncols], xT[:, kc, c0:c0 + rows], wt[:, kc, wc0:wc0 + ncols], kc == 0, kc == KC - 1,
                 rk, [pk])
        return p, pk

    def transpose_into(dst_ap, dkey, src_ap, skey, rows, fcols, eng="dve", ps=None):
        p, pk = ps if ps is not None else ps_s()
        S.mm(p[0:fcols, 0:rows], src_ap, ident[0:rows, 0:rows], True, True, [skey, "ident"], [pk])
        S.copy(eng, dst_ap, p[0:fcols, 0:rows], [pk], [dkey])

    rot = {}
    cur = [stA]

    def rot_buf(name, n, shape, dt):
        if name not in rot:
            rot[name] = [[sb("%s%d" % (name, i), shape, dt, cur[0]) for i in range(n)], 0]
        lst, i = rot[name]
        rot[name][1] = i + 1
        return lst[i % n], "%s%d" % (name, i % n)

    def finalize_o(ostage, okey, n):
        for ti, (c0, rows) in enumerate(g.tiles):
            p, pk = ps_s()
            for wc in range(2):
                S.mm(p[:, wc * 128:wc * 128 + rows], ostage[0:rows, ti, wc * 128:(wc + 1) * 128],
                     ident[0:rows, 0:rows], True, True, [(okey, ti), "ident"], [pk])
            pv = p[:, 0:256].rearrange("p (w t) -> p w t", w=2)[:, :, 0:rows]
            S.copy("act" if ti % 2 else "dve", oT[:, 2 * n:2 * n + 2, c0:c0 + rows], pv, [pk], [r_oT(n, c0)])

    def pipeline(tasks, skew):
        n = len(tasks)
        ns = max(len(t) for t in tasks) if tasks else 0
        for t in range(n + (ns - 1) * skew):
            for j in reversed(range(ns)):
                k = t - j * skew
                if 0 <= k < n and j < len(tasks[k]):
                    tasks[k][j]()

    def attn_softmax(u, qk_fn, v_fn, scale, bias_h, maskT, out_fn, kq_keys, tag, collect=None):
        def do_qb(qb):
            qc0, nq, fq0, subs = qb
            O, ok = ps_o()
            o_started = [False]
            vis = []
            for ki, kt in enumerate(u.keys):
                f = visible(kt, fq0, fq0 + nq, False)
                if f is None:
                    continue
                vis.append((ki, kt, f - fq0 if fq0 >= 0 else 0))

            def stage_a(ctx, ki, kt, ql):
                nk = kt["nk"]
                p, pk = ps_s()
                pairs = qk_fn(kt, qc0, ql, nq)
                for i_, (lt, rh) in enumerate(pairs):
                    S.mm(p[0:nk, ql:nq], lt, rh, i_ == 0, i_ == len(pairs) - 1, kq_keys, [pk])
                PT, ptk = rot_buf("PT", 8, [128, 512], BF16)
                ctx["PT"], ctx["ptk"] = PT, ptk
                if bias_h is None:
                    segs = [("plain", ql, nq)]
                else:
                    x0 = fq0 - kt["fk0"]
                    n_lo = max(ql, -x0)
                    n_hi = min(nq, 256 - x0)
                    segs = []
                    if n_hi > n_lo:
                        segs.append(("near", n_lo, n_hi))
                        if n_hi < nq:
                            segs.append(("far", n_hi, nq))
                    else:
                        segs.append(("far", ql, nq))
                dst_is_tmp = maskT is not None
                if dst_is_tmp:
                    ET, etk = rot_buf("ET", 4, [128, 512], F32)
                for kind, a_, b_ in segs:
                    dst = (ET if dst_is_tmp else PT)
                    dk = etk if dst_is_tmp else ptk
                    if kind == "plain":
                        S.act(dst[0:nk, a_:b_], p[0:nk, a_:b_], AF.Exp, [pk], [dk], scale=scale)
                    elif kind == "far":
                        S.act(dst[0:nk, a_:b_], p[0:nk, a_:b_], AF.Exp, [pk, "cbias"], [dk], scale=scale,
                              bias=cbias[0:nk, bias_h:bias_h + 1])
                    else:
                        x0 = fq0 - kt["fk0"]
                        TB, tbk = rot_buf("TB", 4, [128, 256], F32)
                        S.stt("dve", TB[0:nk, 0:b_ - a_], p[0:nk, a_:b_], scale, BT[0:nk, bias_h, a_ + x0:b_ + x0],
                              ALU.mult, ALU.add, [pk, "BT"], [tbk])
                        S.act(dst[0:nk, a_:b_], TB[0:nk, 0:b_ - a_], AF.Exp, [tbk], [dk])
                if maskT is not None:
                    MT, mkf = maskT
                    S.tt("dve", PT[0:nk, ql:nq], ET[0:nk, ql:nq], MT[0:nk, ki, ql:nq], ALU.mult,
                         [etk] + mkf(ki), [ptk])
                elif fq0 >= 0 and kt["fk0"] >= fq0 and nk > CHUNK:
                    S.memset("dve", PT[CHUNK:nk, ql:ql + CHUNK], 0.0, [ptk])

            def stage_b(ctx, ki, kt, ql):
                nk = kt["nk"]
                PT, ptk = ctx["PT"], ctx["ptk"]
                vap, vkey = v_fn(kt)
                for (so, rows, oti) in subs:
                    if so + rows <= ql:
                        continue
                    sidx = so // 128
                    S.mm(O[0:rows, sidx * 65:sidx * 65 + 65], PT[0:nk, so:so + rows], vap, not o_started[0], True,
                         [ptk, vkey], [ok], skip=True)
                    o_started[0] = True

            tasks = []
            for (ki, kt, ql) in vis:
                ctx = {}
                tasks.append([lambda c=ctx, a=ki, b_=kt, d=ql, fa=stage_a: fa(c, a, b_, d),
                              lambda c=ctx, a=ki, b_=kt, d=ql, fb=stage_b: fb(c, a, b_, d)])
            if collect is None:
                pipeline(tasks, 3)
                out_fn(qb, O, ok)
            else:
                tasks[-1].append(lambda q_=qb, o_=O, k_=ok, f_=out_fn: f_(q_, o_, k_))
                collect.extend(tasks)

        for qb_ in u.qblocks:
            do_qb(qb_)

    with stA:
        qng = sb("qng", [128, 256], F32)
        kvg = sb("kvg", [128, 128], F32)
        slg = sb("slg", [128, 64], F32)
        lamt = sb("lamt", [128, 128], F32)
        lamw = sb("lamw", [128, 8], F32)
        S.dma("sp", qng[:], dr["qn_g"][l:l + 1, :].broadcast_to([128, 256]), (), ["qng"])
        S.dma("sp", kvg[:], dr["kvn_g"][l:l + 1, :].broadcast_to([128, 128]), (), ["kvg"])
        S.dma("sp", slg[:], dr["subln_g"][l:l + 1, :].broadcast_to([128, 64]), (), ["slg"])
        S.dma("sp", lamt[:], dr["lam_p"][l:l + 1, :].broadcast_to([128, 128]), (), ["lamt"])
        lt4 = lamt[:].rearrange("p (a b d) -> p a b d", a=2, b=2)
        lpr = sb("lpr", [128, 2, 32], F32)
        S.tt("dve", lpr[:], lt4[:, :, 0, :], lt4[:, :, 1, :], ALU.mult, ["lamt"], ["lpr"])
        S.reduce(lamw[:, 0:2], lpr[:], ALU.add, ["lpr"], ["lamw0"])
        S.act(lamw[:, 2:4], lamw[:, 0:2], AF.Exp, ["lamw0"], ["lamw1"])
        S.ts("dve", lamw[:, 4:5], lamw[:, 2:3], lamw[:, 3:4], lam_init, ALU.subtract, ALU.add, ["lamw1"], ["lamw2"])
        S.ts("dve", lamw[:, 5:6], lamw[:, 4:5], -1.0, None, ALU.mult, None, ["lamw2"], ["lamw3"])
        S.ts("dve", slg[:], slg[:], 1.0 - lam_init, None, ALU.mult, None, ["slg"], ["slg"])

        if _stop("pre"):
            return True
        stg = sb("stg", [128, 2, 512], F32)
        stg_i = [0]

        def stage_out(p, pk, rows, ncols, name_cols, c0):
            i = stg_i[0] % 2; stg_i[0] += 1
            sk = ("stg", i)
            S.copy("act", stg[0:rows, i, 0:ncols], p[0:rows, 0:ncols], [pk], [sk])
            for (nm, cc, n) in name_cols:
                S.dma("sp", out_rows(g, nm, l, c0, rows), stg[0:rows, i, cc:cc + n], [sk], [("out", nm, c0)])
            return stg[0:rows, i, :], sk

        with contextlib.ExitStack() as sm:
            cur[0] = sm
            wt = sb("w_sb", [128, KC, 768], BF16, sm)
            load_w(wt, 416, 768, "w_sb")
            QT = sb("sbQT", [128, 2, TTg], BF16, sm)
            NQT = sb("sbNQT", [128, 2, TTg], BF16, sm)
            KT = sb("sbKT", [128, 2, TTg], BF16, sm)
            Vt = sb("sbV", [128, NT, 256], BF16, sm)
            ostage = sb("sbO", [128, NT, 256], BF16, sm)
            sc = 64.0 ** -0.5
            for (c0, n) in g.blocks:
                for ch in range(2):
                    p, pk = proj_fm2(wt, "w_sb", ch * 128, 128, c0, n)
                    S.act(QT[:, ch, c0:c0 + n], p[:, 0:n], AF.Copy, [pk], [("sbQT", c0 // 512)], scale=sc)
                    S.ts("dve", NQT[:, ch, c0:c0 + n], p[:, 0:n], -sc, None, ALU.mult, None, [pk], [("sbNQT", c0 // 512)])
                    p, pk = proj_fm2(wt, "w_sb", 256 + ch * 128, 128, c0, n)
                    S.copy("act" if ch else "dve", KT[:, ch, c0:c0 + n], p[:, 0:n], [pk], [("sbKT", c0 // 512)])
            for ti, (c0, rows) in enumerate(g.tiles):
                p, pk = proj_tm(wt, "w_sb", 256, 512, c0, rows)
                sa, sk = stage_out(p, pk, rows, 512, [("sb_k", 0, 256), ("sb_v", 256, 256)], c0)
                S.copy("dve", Vt[0:rows, ti, :], sa[:, 256:512], [sk], [("sbV", ti)])
            if _stop("sbproj"):
                return True
            for u in units:
                if not isP:
                    Kc = sb("sbKc", [128, NPT, 256], BF16, sm) if u.sidx == 0 else Kc
                    Vc = sb("sbVc", [128, NPT, 256], BF16, sm) if u.sidx == 0 else Vc
                    KTc = sb("sbKTc", [128, 2, PAST], BF16, sm) if u.sidx == 0 else KTc
                    gs = u.sidx
                    S.dma("sp", Kc[:], CB["sb_k"][l, gs].rearrange("(j p) f -> p j f", p=128), [("cb", "sb_k", l)], ["sbKc"])
                    S.dma("sp", Vc[:], CB["sb_v"][l, gs].rearrange("(j p) f -> p j f", p=128), [("cb", "sb_v", l)], ["sbVc"])
                    for j in range(NPT):
                        for ch in range(2):
                            transpose_into(KTc[:, ch, j * 128:(j + 1) * 128], "sbKTc", Kc[:, j, ch * 128:(ch + 1) * 128],
                                           "sbKc", 128, 128, "act" if ch else "dve")
                def sb_block(hh, qb, tl):
                    ch, po = hh // 2, (hh % 2) * 64
                    if True:
                        qc0, nq, fq0, subs = qb
                        O, ok = ps_o()
                        o_started = [False]
                        vis = []
                        for ki, kt in enumerate(u.keys):
                            fk0, nk = kt["fk0"], kt["nk"]
                            if fk0 < 0:
                                if fq0 < 0:
                                    vis.append((ki, kt, 0, True))
                                else:
                                    vis.append((ki, kt, 0, False))
                                continue
                            if fq0 < 0:
                                continue
                            if fk0 >= fq0 + nq:
                                continue
                            ql = max(0, fk0 - fq0)
                            vis.append((ki, kt, ql, fk0 >= fq0))
                        vis.sort(key=lambda t: -t[1]["fk0"])
                        Ls = []

                        def sb_ops(kt, ql):
                            nk = kt["nk"]
                            if kt["own"]:
                                kap = KT[po:po + 64, ch, kt["c0"]:kt["c0"] + nk]
                                kk = ("sbKT", kt["c0"] // 512)
                                vap = Vt[0:nk, kt["vi"], hh * 64:hh * 64 + 64]; vk = ("sbV", kt["vi"])
                            else:
                                kap = KTc[po:po + 64, ch, kt["c0"]:kt["c0"] + nk]; kk = "sbKTc"
                                vap = Vc[0:nk, kt["vi"], hh * 64:hh * 64 + 64]; vk = "sbVc"
                            qap = QT[po:po + 64, ch, qc0 + ql:qc0 + nq]
                            nqap = NQT[po:po + 64, ch, qc0 + ql:qc0 + nq]
                            qk_ = [("sbQT", qc0 // 512), ("sbNQT", qc0 // 512), kk]
                            return nk, kap, vap, vk, qap, nqap, qk_

                        def sb_a(ctx, ki, kt, ql, diag):
                            nk, kap, vap, vk, qap, nqap, qk_ = sb_ops(kt, ql)
                            p, pk = ps_s()
                            S.mm(p[0:nk, ql:nq], kap, qap, True, True, qk_, [pk])
                            ET, etk = rot_buf("ET", 4, [128, 512], F32)
                            S.act(ET[0:nk, ql:nq], p[0:nk, ql:nq], AF.Exp, [pk], [etk])
                            Lt, ltk = rot_buf("sbL", 26, [128, 512], BF16)
                            S.act(Lt[0:nk, ql:nq], ET[0:nk, ql:nq], AF.Ln, [etk, "cst"], [ltk], bias=cst[0:nk, 2:3])
                            dw = min(nk, nq - ql)
                            if diag:
                                S.tt("dve", Lt[0:nk, ql:ql + dw], Lt[0:nk, ql:ql + dw], cmask[0:nk, 0:dw], ALU.mult,
                                     [ltk, "cmask"], [ltk])
                            ctx["later"] = list(Ls)
                            ctx["L"] = (Lt, ltk)
                            Ls.append((Lt, ltk, nk, ql))

                        def sb_c(ctx, ki, kt, ql, diag):
                            nk, kap, vap, vk, qap, nqap, qk_ = sb_ops(kt, ql)
                            Lt, ltk = ctx["L"]
                            later = ctx["later"]
                            c_, ck = ps_s()
                            S.mm(c_[0:nk, ql:nq], utri[0:nk, 0:nk], Lt[0:nk, ql:nq], True, False, ["utri", ltk], [ck])
                            nlater = len(later)
                            S.mm(c_[0:nk, ql:nq], kap, nqap, False, nlater == 0, qk_, [ck])
                            for li, (L2, l2k, nk2, ql2) in enumerate(later):
                                S.mm(c_[0:nk, ql2:nq], ones[0:nk2, 0:nk], L2[0:nk2, ql2:nq], False, li == nlater - 1,
                                     ["ones", l2k], [ck])
                            PT, ptk = rot_buf("PT", 8, [128, 512], BF16)
                            ctx["PT"] = (PT, ptk)
                            S.act(PT[0:nk, ql:nq], c_[0:nk, ql:nq], AF.Exp, [ck], [ptk], scale=-1.0)
                            dw = min(nk, nq - ql)
                            if diag:
                                S.tt("dve", PT[0:nk, ql:ql + dw], PT[0:nk, ql:ql + dw], cmask[0:nk, 0:dw], ALU.mult,
                                     [ptk, "cmask"], [ptk])

                        def sb_e(ctx, ki, kt, ql, diag):
                            nk, kap, vap, vk, qap, nqap, qk_ = sb_ops(kt, ql)
                            PT, ptk = ctx["PT"]
                            for (so, rows, oti) in subs:
                                if so + rows <= ql:
                                    continue
                                sidx = so // 128
                                S.mm(O[0:rows, sidx * 64:sidx * 64 + 64], PT[0:nk, so:so + rows], vap, not o_started[0], True,
                                     [ptk, vk], [ok], skip=True)
                                o_started[0] = True

                        tasks = []
                        for (ki, kt, ql, diag) in vis:
                            ctx = {}
                            tasks.append([lambda c=ctx, a=ki, b_=kt, d=ql, e=diag: sb_a(c, a, b_, d, e),
                                          lambda c=ctx, a=ki, b_=kt, d=ql, e=diag: sb_c(c, a, b_, d, e),
                                          lambda c=ctx, a=ki, b_=kt, d=ql, e=diag: sb_e(c, a, b_, d, e)])

                        def sb_out():
                            for (so, rows, oti) in subs:
                                sidx = so // 128
                                S.copy("act", ostage[0:rows, oti, hh * 64:hh * 64 + 64], O[0:rows, sidx * 64:sidx * 64 + 64],
                                       [ok], [("sbO", oti)])

                        tasks[-1].append(sb_out)
                        tl.extend(tasks)

                tl = []
                for hh in range(H):
                    for qb in u.qblocks:
                        sb_block(hh, qb, tl)
                pipeline(tl, 5)
            finalize_o(ostage, "sbO", 1)
            rot.pop("sbL", None)
        S.barrier()
        rot.clear()
        if _stop("sb"):
            return True

        with contextlib.ExitStack() as sm:
            cur[0] = sm
            wt = sb("w_mla", [128, KC, 416], BF16, sm)
            load_w(wt, 0, 416, "w_mla")
            wuq = sb("wuq", [128, 2, 384], BF16, sm)
            S.dma(wq, wuq[:], wsrc("w_uq").rearrange("(c p) e -> p c e", p=128), wdep("w_uq"), ["wuq"])
            wuk = sb("wuk", [128, 256], BF16, sm)
            S.dma(wq, wuk[:], wsrc("w_uk"), wdep("w_uk"), ["wuk"])
            wuv = sb("wuv", [128, 256], BF16, sm)
            S.dma(wq, wuv[:], wsrc("w_uv"), wdep("w_uv"), ["wuv"])
            wukT = sb("wukT", [64, 4, 128], BF16, sm)
            for hh in range(H):
                transpose_into(wukT[:, hh, :], "wukT", wuk[:, hh * 64:(hh + 1) * 64], "wuk", 128, 64)
            rope = sb("rope", [128, NT, 128], F32, sm)
            for ti, (c0, rows) in enumerate(g.tiles):
                S.dma("sp", rope[0:rows, ti, :], g.rope[c0:c0 + rows, :], (), ["rope"])
            cqT = sb("cqT", [128, 2, TTg], BF16, sm)
            ckvT = sb("ckvT", [128, TTg], BF16, sm)
            kpeT = sb("kpeT", [128, TTg], BF16, sm)
            Vm = sb("Vm", [128, NT, 4, 65], BF16, sm)
            QA = sb("QA", [128, NT, 4, 96], BF16, sm)
            ostage = sb("mlO", [128, NT, 256], BF16, sm)
            wk_st = sb("mst", [128, 8], F32, sm)
            S.memset("dve", Vm[:, :, :, 64:65], 1.0, [("Vm", i) for i in range(NT)])
            for ti, (c0, rows) in enumerate(g.tiles):
                p, pk = proj_tm(wt, "w_mla", 0, 416, c0, rows)
                jk, jkk = rot_buf("mjunk", 2, [128, 256], BF16)
                st_k = ("mst", ti % 2)
                stc = wk_st[0:rows, (ti % 2) * 4:(ti % 2) * 4 + 4]
                S.memset("dve", stc, 0.0, [st_k])
                S.act(jk[0:rows, 0:256], p[0:rows, 0:256], AF.Square, [pk, st_k], [jkk, (st_k, "a")],
                      accum_out=stc[:, 0:1])
                S.act(jk[0:rows, 0:128], p[0:rows, 256:384], AF.Square, [pk, st_k], [jkk, (st_k, "b")],
                      accum_out=stc[:, 1:2])
                S.act(stc[:, 2:3], stc[:, 0:1], AF.Ln, [st_k, (st_k, "a"), "cst"], [(st_k, "c")], scale=1.0 / 256, bias=cst[0:rows, 1:2])
                S.act(stc[:, 2:3], stc[:, 2:3], AF.Exp, [(st_k, "c")], [(st_k, "c")], scale=-0.5)
                S.act(stc[:, 3:4], stc[:, 1:2], AF.Ln, [st_k, (st_k, "b"), "cst"], [(st_k, "d")], scale=1.0 / 128, bias=cst[0:rows, 1:2])
                S.act(stc[:, 3:4], stc[:, 3:4], AF.Exp, [(st_k, "d")], [(st_k, "d")], scale=-0.5)
                cqn, cqk = rot_buf("cqn", 2, [128, 256], BF16)
                S.stt("dve", cqn[0:rows, :], p[0:rows, 0:256], stc[:, 2:3], qng[0:rows, :], ALU.mult, ALU.mult,
                      [pk, (st_k, "c"), "qng"], [cqk])
                i = stg_i[0] % 2; stg_i[0] += 1
                sk = ("stg", i)
                S.stt("dve", stg[0:rows, i, 0:128], p[0:rows, 256:384], stc[:, 3:4], kvg[0:rows, :], ALU.mult, ALU.mult,
                      [pk, (st_k, "d"), "kvg"], [sk])
                S.tt("dve", stg[0:rows, i, 128:160], p[0:rows, 384:416], rope[0:rows, ti, 0:32], ALU.mult, [pk, "rope"], [sk])
                rt, rtk = rot_buf("rtmp", 2, [128, 4, 32], F32)
                S.tt("dve", rt[0:rows, 0, 0:16], p[0:rows, 400:416], rope[0:rows, ti, 32:48], ALU.mult, [pk, "rope"], [rtk])
                S.tt("dve", rt[0:rows, 0, 16:32], p[0:rows, 384:400], rope[0:rows, ti, 48:64], ALU.mult, [pk, "rope"], [rtk])
                S.tt("dve", stg[0:rows, i, 128:160], stg[0:rows, i, 128:160], rt[0:rows, 0, :], ALU.add, [sk, rtk], [sk])
                S.dma("sp", out_rows(g, "mla_kv", l, c0, rows), stg[0:rows, i, 0:128], [sk], [("out", "mla_kv", c0)])
                S.dma("sp", out_rows(g, "mla_pe", l, c0, rows), stg[0:rows, i, 128:160], [sk], [("out", "mla_pe", c0)])
                kb, kbk = rot_buf("kb", 2, [128, 160], BF16)
                S.copy("act", kb[0:rows, :], stg[0:rows, i, 0:160], [sk], [kbk])
                for cc in range(2):
                    transpose_into(cqT[:, cc, c0:c0 + rows], ("cqT", ti), cqn[0:rows, cc * 128:(cc + 1) * 128], cqk,
                                   rows, 128, "act" if cc else "dve")
                transpose_into(ckvT[:, c0:c0 + rows], ("ckvT", ti), kb[0:rows, 0:128], kbk, rows, 128, "dve")
                transpose_into(kpeT[64:96, c0:c0 + rows], ("kpeT", ti), kb[0:rows, 128:160], kbk, rows, 32, "dve")
                p2, p2k = ps_s()
                S.mm(p2[0:rows, 0:256], ckvT[:, c0:c0 + rows], wuv[:], True, True, [("ckvT", ti), "wuv"], [p2k])
                S.copy("act", Vm[0:rows, ti, :, 0:64], p2[0:rows, 0:256].rearrange("p (h d) -> p h d", h=4), [p2k], [("Vm", ti)])
                p3, p3k = ps_s()
                for cc in range(2):
                    S.mm(p3[0:rows, 0:384], cqT[:, cc, c0:c0 + rows], wuq[:, cc, :], cc == 0, cc == 1,
                         [("cqT", ti), "wuq"], [p3k])
                p3v = p3[0:rows, 0:384].rearrange("p (h e) -> p h e", h=4)
                S.act(QA[0:rows, ti, :, 0:64], p3v[:, :, 0:64], AF.Copy, [p3k], [("QA", ti)], scale=96.0 ** -0.5)
                cosq = rope[0:rows, ti, 64:96].unsqueeze(1).broadcast_to([rows, 4, 32])
                nsin = rope[0:rows, ti, 96:112].unsqueeze(1).broadcast_to([rows, 4, 16])
                psin = rope[0:rows, ti, 112:128].unsqueeze(1).broadcast_to([rows, 4, 16])
                r1, r1k = rot_buf("rtmp", 2, [128, 4, 32], F32)
                r2, r2k = rot_buf("rtmp2", 2, [128, 4, 32], F32)
                S.tt("dve", r1[0:rows], p3v[:, :, 64:96], cosq, ALU.mult, [p3k, "rope"], [r1k])
                S.tt("dve", r2[0:rows, :, 0:16], p3v[:, :, 80:96], nsin, ALU.mult, [p3k, "rope"], [r2k])
                S.tt("dve", r2[0:rows, :, 16:32], p3v[:, :, 64:80], psin, ALU.mult, [p3k, "rope"], [r2k])
                S.tt("dve", QA[0:rows, ti, :, 64:96], r1[0:rows], r2[0:rows], ALU.add, [r1k, r2k], [("QA", ti)])
            for u in units:
                if not isP:
                    gs = u.sidx
                    Kc = sb("mlKc", [128, NPT, 160], BF16, sm) if gs == 0 else Kc
                    ckvTc = sb("ckvTc", [128, PAST], BF16, sm) if gs == 0 else ckvTc
                    kpeTc = sb("kpeTc", [128, PAST], BF16, sm) if gs == 0 else kpeTc
                    Vmc = sb("Vmc", [128, NPT, 4, 65], BF16, sm) if gs == 0 else Vmc
                    if gs == 0:
                        S.memset("dve", Vmc[:, :, :, 64:65], 1.0, ["Vmc"])
                    S.dma("sp", Kc[:, :, 0:128], CB["mla_kv"][l, gs].rearrange("(j p) f -> p j f", p=128), [("cb", "mla_kv", l)], ["mlKc"])
                    S.dma("sp", Kc[:, :, 128:160], CB["mla_pe"][l, gs].rearrange("(j p) f -> p j f", p=128), [("cb", "mla_pe", l)], ["mlKc"])
                    for j in range(NPT):
                        transpose_into(ckvTc[:, j * 128:(j + 1) * 128], "ckvTc", Kc[:, j, 0:128], "mlKc", 128, 128, "dve")
                        transpose_into(kpeTc[64:96, j * 128:(j + 1) * 128], "kpeTc", Kc[:, j, 128:160], "mlKc", 128, 32, "act")
                        p2, p2k = ps_s()
                        S.mm(p2[:, 0:256], ckvTc[:, j * 128:(j + 1) * 128], wuv[:], True, True, ["ckvTc", "wuv"], [p2k])
                        S.copy("act", Vmc[:, j, :, 0:64], p2[:, 0:256].rearrange("p (h d) -> p h d", h=4), [p2k], ["Vmc"])
                for hh in range(H):
                    QhT = sb("QhT", [128, TTg], BF16, sm) if (hh == 0 and u.sidx == 0) else QhT
                    QlT = sb("QlT", [128, TTg], BF16, sm) if (hh == 0 and u.sidx == 0) else QlT
                    own_tiles = [(ti, c0, rows) for ti, (c0, rows) in enumerate(g.tiles)
                                 if isP or ti == u.sidx]
                    for (ti, c0, rows) in own_tiles:
                        transpose_into(QhT[0:96, c0:c0 + rows], ("QhT", ti), QA[0:rows, ti, hh, :], ("QA", ti), rows, 96,
                                       "act" if ti % 2 else "dve")
                    for (ti, c0, rows) in own_tiles:
                        p4, p4k = ps_s()
                        S.mm(p4[:, 0:rows], wukT[:, hh, :], QhT[0:64, c0:c0 + rows], True, True, ["wukT", ("QhT", ti)], [p4k])
                        S.copy("dve" if ti % 2 else "act", QlT[:, c0:c0 + rows], p4[:, 0:rows], [p4k], [("QlT", ti)])

                    def qk_fn(kt, qc0, ql, nq):
                        nk = kt["nk"]
                        if kt["own"]:
                            ka, kb_ = ckvT[:, kt["c0"]:kt["c0"] + nk], kpeT[64:96, kt["c0"]:kt["c0"] + nk]
                        else:
                            ka, kb_ = ckvTc[:, kt["c0"]:kt["c0"] + nk], kpeTc[64:96, kt["c0"]:kt["c0"] + nk]
                        return [(ka, QlT[:, qc0 + ql:qc0 + nq]), (kb_, QhT[64:96, qc0 + ql:qc0 + nq])]

                    def v_fn(kt):
                        if kt["own"]:
                            return Vm[0:kt["nk"], kt["vi"], hh, :], ("Vm", kt["vi"])
                        return Vmc[0:kt["nk"], kt["vi"], hh, :], "Vmc"

                    def out_fn(qb, O, ok):
                        qc0, nq, fq0, subs = qb
                        for (so, rows, oti) in subs:
                            sidx = so // 128
                            rc, rck = rot_buf("rc", 4, [128, 1], F32)
                            S.recip(rc[0:rows, :], O[0:rows, sidx * 65 + 64:sidx * 65 + 65], [ok], [rck])
                            S.ts("dve", ostage[0:rows, oti, hh * 64:hh * 64 + 64], O[0:rows, sidx * 65:sidx * 65 + 64],
                                 rc[0:rows, 0:1], None, ALU.mult, None, [ok, rck], [("mlO", oti)])

                    kq_keys = ([("ckvT", i) for i in range(NT)] + [("kpeT", i) for i in range(NT)]
                               + [("QhT", i) for i in range(NT)] + [("QlT", i) for i in range(NT)]
                               + ["ckvTc", "kpeTc"])
                    tl = []
                    attn_softmax(u, qk_fn, v_fn, 1.0, None, None, out_fn, kq_keys, "ml", collect=tl)
                    pipeline(tl, 5)
            finalize_o(ostage, "mlO", 0)
        S.barrier()
        rot.clear()
        if _stop("mla"):
            return True

        with contextlib.ExitStack() as sm:
            cur[0] = sm
            wt = sb("w_df", [128, KC, 768], BF16, sm)
            load_w(wt, 1184, 768, "w_df")
            QT = sb("dfQT", [128, 2, TTg], BF16, sm)
            KA = sb("dfKA", [128, 2, TTg], BF16, sm)
            KB = sb("dfKB", [128, 2, TTg], BF16, sm)
            Vt = sb("dfV", [128, NT, 4, 65], BF16, sm)
            ostage = sb("dfO", [128, NT, 256], BF16, sm)
            S.memset("dve", Vt[:, :, :, 64:65], 1.0, [("dfV", i) for i in range(NT)])
            sc = 32.0 ** -0.5
            for (c0, n) in g.blocks:
                for ch in range(2):
                    p, pk = proj_fm2(wt, "w_df", ch * 128, 128, c0, n)
                    S.copy("act", QT[:, ch, c0:c0 + n], p[:, 0:n], [pk], [("dfQT", c0 // 512)])
                    p, pk = proj_fm2(wt, "w_df", 256 + ch * 128, 128, c0, n)
                    S.ts("dve", KA[:, ch, c0:c0 + n], p[:, 0:n], mab[:, 0:1], None, ALU.mult, None, [pk, "mab"], [("dfKA", c0 // 512)])
                    S.ts("dve", KB[:, ch, c0:c0 + n], p[:, 0:n], mab[:, 1:2], None, ALU.mult, None, [pk, "mab"], [("dfKB", c0 // 512)])
            for ti, (c0, rows) in enumerate(g.tiles):
                p, pk = proj_tm(wt, "w_df", 256, 512, c0, rows)
                sa, sk = stage_out(p, pk, rows, 512, [("diff_k", 0, 256), ("diff_v", 256, 256)], c0)
                S.copy("dve", Vt[0:rows, ti, :, 0:64], sa[:, 256:512].rearrange("p (h d) -> p h d", h=4), [sk], [("dfV", ti)])
            for u in units:
                if not isP:
                    gs = u.sidx
                    Kc = sb("dfKc", [128, NPT, 256], BF16, sm) if gs == 0 else Kc
                    Vc = sb("dfVc", [128, NPT, 4, 65], BF16, sm) if gs == 0 else Vc
                    Vcs = sb("dfVcs", [128, NPT, 256], BF16, sm) if gs == 0 else Vcs
                    KAc = sb("dfKAc", [128, 2, PAST], BF16, sm) if gs == 0 else KAc
                    KBc = sb("dfKBc", [128, 2, PAST], BF16, sm) if gs == 0 else KBc
                    if gs == 0:
                        S.memset("dve", Vc[:, :, :, 64:65], 1.0, ["dfVc"])
                    S.dma("sp", Kc[:], CB["diff_k"][l, gs].rearrange("(j p) f -> p j f", p=128), [("cb", "diff_k", l)], ["dfKc"])
                    S.dma("sp", Vcs[:], CB["diff_v"][l, gs].rearrange("(j p) f -> p j f", p=128), [("cb", "diff_v", l)], ["dfVcs"])
                    S.copy("dve", Vc[:, :, :, 0:64], Vcs[:].rearrange("p j (h d) -> p j h d", h=4), ["dfVcs"], ["dfVc"])
                    for j in range(NPT):
                        for ch in range(2):
                            p, pk = ps_s()
                            S.mm(p[:, 0:128], Kc[:, j, ch * 128:(ch + 1) * 128], ident[:], True, True, ["dfKc", "ident"], [pk])
                            S.ts("dve", KAc[:, ch, j * 128:(j + 1) * 128], p[:, 0:128], mab[:, 0:1], None, ALU.mult, None,
                                 [pk, "mab"], ["dfKAc"])
                            S.ts("dve", KBc[:, ch, j * 128:(j + 1) * 128], p[:, 0:128], mab[:, 1:2], None, ALU.mult, None,
                                 [pk, "mab"], ["dfKBc"])
                def diff_head(hh, tl):
                    ch, po = hh // 2, (hh % 2) * 64
                    keep = {}

                    def v_fn(kt):
                        if kt["own"]:
                            return Vt[0:kt["nk"], kt["vi"], hh, :], ("dfV", kt["vi"])
                        return Vc[0:kt["nk"], kt["vi"], hh, :], "dfVc"

                    for mi in range(2):
                        def qk_fn(kt, qc0, ql, nq, mi=mi):
                            nk = kt["nk"]
                            if kt["own"]:
                                src = KA if mi == 0 else KB
                            else:
                                src = KAc if mi == 0 else KBc
                            return [(src[po:po + 64, ch, kt["c0"]:kt["c0"] + nk], QT[po:po + 64, ch, qc0 + ql:qc0 + nq])]

                        def out_fn(qb, O, ok, mi=mi):
                            qc0, nq, fq0, subs = qb
                            for (so, rows, oti) in subs:
                                sidx = so // 128
                                rc, rck = rot_buf("rc", 4, [128, 1], F32)
                                S.recip(rc[0:rows, :], O[0:rows, sidx * 65 + 64:sidx * 65 + 65], [ok], [rck])
                                if mi == 0:
                                    d0, d0k = rot_buf("d0_", NT + 2, [128, 64], F32)
                                    keep[(qc0, so)] = (d0, d0k)
                                    S.ts("dve", d0[0:rows, :], O[0:rows, sidx * 65:sidx * 65 + 64], rc[0:rows, 0:1], None,
                                         ALU.mult, None, [ok, rck], [d0k])
                                else:
                                    d0, d0k = keep[(qc0, so)]
                                    d1, d1k = rot_buf("d1_", 2, [128, 64], F32)
                                    S.ts("dve", d1[0:rows, :], O[0:rows, sidx * 65:sidx * 65 + 64], rc[0:rows, 0:1],
                                         lamw[0:rows, 5:6], ALU.mult, ALU.mult, [ok, rck, "lamw3"], [d1k])
                                    S.tt("dve", d1[0:rows, :], d1[0:rows, :], d0[0:rows, :], ALU.add, [d1k, d0k], [d1k])
                                    ss, ssk = rot_buf("ss_", 2, [128, 2], F32)
                                    jk, jkk = rot_buf("dj_", 2, [128, 64], F32)
                                    S.memset("dve", ss[0:rows, :], 0.0, [ssk])
                                    S.act(jk[0:rows, :], d1[0:rows, :], AF.Square, [d1k, ssk], [jkk, (ssk, "a")],
                                          accum_out=ss[0:rows, 0:1])
                                    S.act(ss[0:rows, 1:2], ss[0:rows, 0:1], AF.Ln, [(ssk, "a"), ssk, "cst"], [(ssk, "b")],
                                          scale=1.0 / 64, bias=cst[0:rows, 1:2])
                                    S.act(ss[0:rows, 1:2], ss[0:rows, 1:2], AF.Exp, [(ssk, "b")], [(ssk, "b")], scale=-0.5)
                                    S.stt("dve", ostage[0:rows, oti, hh * 64:hh * 64 + 64], d1[0:rows, :], ss[0:rows, 1:2],
                                          slg[0:rows, :], ALU.mult, ALU.mult, [d1k, (ssk, "b"), "slg"], [("dfO", oti)])

                        kq_keys = ([("dfQT", i) for i in range((TTg + 511) // 512)] + [("dfKA", i) for i in range((TTg + 511) // 512)]
                                   + [("dfKB", i) for i in range((TTg + 511) // 512)] + ["dfKAc", "dfKBc"])
                        attn_softmax(u, qk_fn, v_fn, sc, hh, None, out_fn, kq_keys, "df", collect=tl)

                tl = []
                for hh in range(H):
                    diff_head(hh, tl)
                pipeline(tl, 5)
            finalize_o(ostage, "dfO", 2)
        S.barrier()
        rot.clear()
        if _stop("df"):
            return True

        with contextlib.ExitStack() as sm:
            cur[0] = sm
            QT = sb("dsQT", [128, 2, TTg], BF16, sm)
            KT = sb("dsKT", [128, 2, TTg], BF16, sm)
            QI = sb("dsQI", [128, 2, TTg], BF16, sm)
            KIa = sb("dsKIa", [128, TTg], BF16, sm)
            KIb = sb("dsKIb", [128, TTg], BF16, sm)
            WI = sb("dsWI", [128, NT, 8], F32, sm)
            Vt = sb("dsV", [128, NT, 4, 65], BF16, sm)
            ostage = sb("dsO", [128, NT, 256], BF16, sm)
            smw = contextlib.ExitStack()
            wt = sb("w_ds", [128, KC, 1168], BF16, smw)
            load_w(wt, 1952, 1024, "w_ds")
            for r_ in range(4):
                load_w(wt, 2976, 32, "w_ds", dcol=1024 + 32 * r_)
            load_w(wt, 3008, 8, "w_ds", dcol=1152)
            if conv_cb is not None:
                conv_cb()
            S.memset("dve", Vt[:, :, :, 64:65], 1.0, [("dsV", i) for i in range(NT)])
            sc = 64.0 ** -0.5
            nblk = (TTg + 511) // 512
            for (c0, n) in g.blocks:
                for ch in range(2):
                    p, pk = proj_fm2(wt, "w_ds", ch * 128, 128, c0, n)
                    S.copy("act", QT[:, ch, c0:c0 + n], p[:, 0:n], [pk], [("dsQT", c0 // 512)])
                    p, pk = proj_fm2(wt, "w_ds", 256 + ch * 128, 128, c0, n)
                    S.copy("dve", KT[:, ch, c0:c0 + n], p[:, 0:n], [pk], [("dsKT", c0 // 512)])
                    p, pk = proj_fm2(wt, "w_ds", 768 + ch * 128, 128, c0, n)
                    S.copy("act", QI[:, ch, c0:c0 + n], p[:, 0:n], [pk], [("dsQI", c0 // 512)])
                p, pk = proj_fm2(wt, "w_ds", 1024, 128, c0, n)
                S.ts("dve", KIa[:, c0:c0 + n], p[:, 0:n], mab[:, 0:1], None, ALU.mult, None, [pk, "mab"], [("dsKI", c0 // 512)])
                S.ts("dve", KIb[:, c0:c0 + n], p[:, 0:n], mab[:, 1:2], None, ALU.mult, None, [pk, "mab"], [("dsKI", c0 // 512)])
            for ti, (c0, rows) in enumerate(g.tiles):
                p, pk = proj_tm(wt, "w_ds", 256, 512, c0, rows)
                sa, sk = stage_out(p, pk, rows, 512, [("dsa_k", 0, 256), ("dsa_v", 256, 256)], c0)
                S.copy("dve", Vt[0:rows, ti, :, 0:64], sa[:, 256:512].rearrange("p (h d) -> p h d", h=4), [sk], [("dsV", ti)])
                p, pk = proj_tm(wt, "w_ds", 1024, 136, c0, rows)
                sa, sk = stage_out(p, pk, rows, 136, [("dsa_kidx", 0, 32)], c0)
                S.ts("dve", WI[0:rows, ti, :], sa[:, 128:136], (8.0 ** -0.5) * (32.0 ** -0.5), None, ALU.mult, None,
                     [sk], [("dsWI", ti)])
            S.barrier()
            smw.close()
            NKT = max(len(u.keys) for u in units)
            MT = sb("dsMT", [128, NKT, 512], BF16, sm)
            SCW = NMETA + SEQ if isP else PAST + DEC
            score = [sb("dsSC%d" % i, [128, SCW], F32, sm) for i in range(2)]
            Mq = [sb("dsM%d" % i, [128, SCW], BF16, sm) for i in range(2)]
            bw = sb("dsbw", [128, 2, 8 + 2 * NBIS], F32, sm)
            for u in units:
                if not isP:
                    gs = u.sidx
                    Kc = sb("dsKc", [128, NPT, 384], BF16, sm) if gs == 0 else Kc
                    Vc = sb("dsVc", [128, NPT, 4, 65], BF16, sm) if gs == 0 else Vc
                    Vcs = sb("dsVcs", [128, NPT, 256], BF16, sm) if gs == 0 else Vcs
                    KTc = sb("dsKTc", [128, 2, PAST], BF16, sm) if gs == 0 else KTc
                    KIac = sb("dsKIac", [128, PAST], BF16, sm) if gs == 0 else KIac
                    KIbc = sb("dsKIbc", [128, PAST], BF16, sm) if gs == 0 else KIbc
                    if gs == 0:
                        S.memset("dve", Vc[:, :, :, 64:65], 1.0, ["dsVc"])
                    S.dma("sp", Kc[:, :, 0:256], CB["dsa_k"][l, gs].rearrange("(j p) f -> p j f", p=128), [("cb", "dsa_k", l)], ["dsKc"])
                    for r_ in range(4):
                        S.dma("sp", Kc[:, :, 256 + 32 * r_:288 + 32 * r_],
                              CB["dsa_kidx"][l, gs].rearrange("(j p) f -> p j f", p=128), [("cb", "dsa_kidx", l)], ["dsKc"])
                    S.dma("sp", Vcs[:], CB["dsa_v"][l, gs].rearrange("(j p) f -> p j f", p=128), [("cb", "dsa_v", l)], ["dsVcs"])
                    S.copy("dve", Vc[:, :, :, 0:64], Vcs[:].rearrange("p j (h d) -> p j h d", h=4), ["dsVcs"], ["dsVc"])
                    for j in range(NPT):
                        for ch in range(2):
                            transpose_into(KTc[:, ch, j * 128:(j + 1) * 128], "dsKTc", Kc[:, j, ch * 128:(ch + 1) * 128],
                                           "dsKc", 128, 128, "act" if ch else "dve")
                        p, pk = ps_s()
                        S.mm(p[:, 0:128], Kc[:, j, 256:384], ident[:], True, True, ["dsKc", "ident"], [pk])
                        S.ts("dve", KIac[:, j * 128:(j + 1) * 128], p[:, 0:128], mab[:, 0:1], None, ALU.mult, None, [pk, "mab"], ["dsKIc"])
                        S.ts("dve", KIbc[:, j * 128:(j + 1) * 128], p[:, 0:128], mab[:, 1:2], None, ALU.mult, None, [pk, "mab"], ["dsKIc"])
                topk = g.topk
                for qb in u.qblocks:
                    qc0, nq, fq0, subs = qb
                    def idx_scores(si):
                        so, rows, oti = subs[si]
                        sl = si % 2
                        sct = score[sl]; sck = ("dsSC", sl)
                        fqs = fq0 + so
                        vis = []
                        for ki, kt in enumerate(u.keys):
                            f = visible(kt, fqs, fqs + rows, False)
                            if f is not None:
                                vis.append((ki, kt))
                        ncols = max(kt["sc0"] + kt["nk"] for (_, kt) in vis)
                        blks = []
                        for (ki, kt) in vis:
                            if blks and blks[-1]["own"] == kt["own"] and blks[-1]["c0"] + blks[-1]["n"] == kt["c0"] \
                                    and blks[-1]["sc0"] + blks[-1]["n"] == kt["sc0"] and blks[-1]["n"] + kt["nk"] <= 512:
                                blks[-1]["n"] += kt["nk"]
                            else:
                                blks.append(dict(own=kt["own"], c0=kt["c0"], sc0=kt["sc0"], n=kt["nk"]))
                        qcol = qc0 + so
                        Dg, dgk = rot_buf("dsDg", 2, [128, 8, 128], BF16)
                        S.tt("dve", Dg[0:rows, :, 0:rows], ident[0:rows, 0:rows].unsqueeze(1).broadcast_to([rows, 8, rows]),
                             WI[0:rows, oti, :].unsqueeze(2).broadcast_to([rows, 8, rows]), ALU.mult,
                             ["ident", ("dsWI", oti)], [dgk])
                        for bi, bk in enumerate(blks):
                            n = bk["n"]
                            Rs = []
                            for ih in range(8):
                                ch, jj = ih // 4, ih % 4
                                po = (jj // 2) * 64
                                if bk["own"]:
                                    kis = (KIa if jj % 2 == 0 else KIb)[po:po + 64, bk["c0"]:bk["c0"] + n]
                                else:
                                    kis = (KIac if jj % 2 == 0 else KIbc)[po:po + 64, bk["c0"]:bk["c0"] + n]
                                p, pk = ps_s()
                                S.mm(p[0:rows, 0:n], QI[po:po + 64, ch, qcol:qcol + rows], kis, True, True,
                                     [("dsQI", qcol // 512), ("dsKI", bk["c0"] // 512), ("dsKI", (bk["c0"] + n - 1) // 512), "dsKIc"], [pk])
                                R, rk = rot_buf("dsR", 10, [128, 512], BF16)
                                S.act(R[0:rows, 0:n], p[0:rows, 0:n], AF.Relu, [pk], [rk])
                                Rs.append((R, rk))
                            pa, pak = ps_s()
                            for ih, (R, rk) in enumerate(Rs):
                                S.mm(pa[0:rows, 0:n], Dg[0:rows, ih, 0:rows], R[0:rows, 0:n], ih == 0, ih == 7, [dgk, rk], [pak])
                            S.copy("act", sct[0:rows, bk["sc0"]:bk["sc0"] + n], pa[0:rows, 0:n], [pak], [(sck, bi)])
                        allb = [(sck, bi) for bi in range(len(blks))]
                        return dict(si=si, so=so, rows=rows, oti=oti, sl=sl, vis=vis, ncols=ncols, allb=allb, fqs=fqs)

                    def bis_setup(c):
                        rows, sl, ncols, allb = c["rows"], c["sl"], c["ncols"], c["allb"]
                        sct = score[sl]
                        b_ = bw[0:rows, sl, :]
                        bk_ = ("dsbw", sl)
                        c["b"], c["bk"] = b_, bk_
                        S.reduce(b_[:, 0:1], sct[0:rows, 0:ncols], ALU.max, allb, [(bk_, "mx")])
                        S.reduce(b_[:, 1:2], sct[0:rows, 0:ncols], ALU.min, allb, [(bk_, "mn")])
                        for (ki, kt) in c["vis"]:
                            if c["fqs"] >= 0 and kt["fk0"] >= c["fqs"] and kt["nk"] > CHUNK:
                                S.memset("dve", sct[0:CHUNK, kt["sc0"] + CHUNK:kt["sc0"] + kt["nk"]], NEGBIG,
                                         allb + [(bk_, "mn"), (bk_, "mx")])
                        S.tt("dve", b_[:, 2:3], b_[:, 0:1], b_[:, 1:2], ALU.subtract, [(bk_, "mx"), (bk_, "mn")], [(bk_, "w0")])
                        S.ts("dve", b_[:, 8:8 + NBIS], pow2[0:rows, :], b_[:, 2:3], None, ALU.mult, None, ["pow2", (bk_, "w0")], [(bk_, "wt")])
                        S.memset("dve", b_[:, 8 + NBIS:8 + 2 * NBIS], 0.0, [(bk_, "cnt")])
                        S.tt("dve", b_[:, 4:5], b_[:, 1:2], b_[:, 8:9], ALU.add, [(bk_, "mn"), (bk_, "wt")], [(bk_, "mid")])
                        if c["sl"] == 1:
                            S.ts("dve", b_[:, 6:7], b_[:, 4:5], -1.0, None, ALU.mult, None, [(bk_, "mid")], [(bk_, "nm")])

                    def bis_iter(c, it):
                        rows, sl, ncols, allb, b_, bk_ = c["rows"], c["sl"], c["ncols"], c["allb"], c["b"], c["bk"]
                        sct = score[sl]; Mt = Mq[sl]; mk = ("dsM", sl)
                        if sl == 1:
                            S.act(Mt[0:rows, 0:ncols], sct[0:rows, 0:ncols], AF.Sign, allb + [(bk_, "nm"), (bk_, "cnt")],
                                  [mk, (bk_, "c%d" % it)], bias=b_[:, 6:7], accum_out=b_[:, 8 + NBIS + it:9 + NBIS + it])
                            S.ts("pool", b_[:, 5:6], b_[:, 8 + NBIS + it:9 + NBIS + it], 2.0 * topk - ncols - 0.5, 0.5,
                                 ALU.is_ge, ALU.subtract, [(bk_, "c%d" % it)], [(bk_, "stp")])
                            S.tt("pool", b_[:, 5:6], b_[:, 5:6], b_[:, 8 + it:9 + it], ALU.mult, [(bk_, "stp"), (bk_, "wt")], [(bk_, "stp")])
                            S.tt("pool", b_[:, 6:7], b_[:, 6:7], b_[:, 5:6], ALU.subtract, [(bk_, "nm"), (bk_, "stp")], [(bk_, "nm")])
                            return
                        S.ts("dve", Mt[0:rows, 0:ncols], sct[0:rows, 0:ncols], b_[:, 4:5], 0.0, ALU.is_ge, ALU.add,
                             allb + [(bk_, "mid"), (bk_, "cnt")], [mk, (bk_, "c%d" % it)],
                             accum_out=b_[:, 8 + NBIS + it:9 + NBIS + it])
                        S.ts("dve", b_[:, 5:6], b_[:, 8 + NBIS + it:9 + NBIS + it], topk - 0.5, 0.5,
                             ALU.is_ge, ALU.subtract, [(bk_, "c%d" % it)], [(bk_, "stp")])
                        S.stt("dve", b_[:, 4:5], b_[:, 5:6], b_[:, 8 + it:9 + it], b_[:, 4:5], ALU.mult, ALU.add,
                              [(bk_, "stp"), (bk_, "wt"), (bk_, "mid")], [(bk_, "mid")])

                    def bis_final(c):
                        rows, sl, ncols, allb, b_, bk_ = c["rows"], c["sl"], c["ncols"], c["allb"], c["b"], c["bk"]
                        so = c["so"]
                        sct = score[sl]; Mt = Mq[sl]; mk = ("dsM", sl)
                        if sl == 1:
                            S.ts("dve", b_[:, 4:5], b_[:, 6:7], -1.0, None, ALU.mult, None, [(bk_, "nm")], [(bk_, "mid")])
                        S.stt("dve", b_[:, 3:4], b_[:, 8 + NBIS - 1:8 + NBIS], -0.5, b_[:, 4:5], ALU.mult, ALU.add,
                              [(bk_, "wt"), (bk_, "mid")], [(bk_, "lo")])
                        S.ts("dve", Mt[0:rows, 0:ncols], sct[0:rows, 0:ncols], b_[:, 3:4], None, ALU.is_ge, None,
                             allb + [(bk_, "lo")], [mk])
                        vl = list(c["vis"])
                        for i0 in range(0, len(vl), 4):
                            grp = vl[i0:i0 + 4]
                            p, pk = ps_s()
                            for gi, (ki, kt) in enumerate(grp):
                                S.mm(p[0:kt["nk"], gi * 128:gi * 128 + rows], Mt[0:rows, kt["sc0"]:kt["sc0"] + kt["nk"]],
                                     ident[0:rows, 0:rows], True, True, [mk, "ident"], [pk])
                            for gi, (ki, kt) in enumerate(grp):
                                S.copy("act" if gi % 2 else "dve", MT[0:kt["nk"], ki, so:so + rows],
                                       p[0:kt["nk"], gi * 128:gi * 128 + rows], [pk], [("dsMT", ki, so // 128)])

                    for s0_ in range(0, len(subs), 2):
                        cs = [idx_scores(si) for si in range(s0_, min(s0_ + 2, len(subs)))]
                        for c in cs:
                            bis_setup(c)
                        for it in range(NBIS):
                            for c in cs:
                                bis_iter(c, it)
                        for c in cs:
                            bis_final(c)
                    def dsa_head(hh, tl, qb=qb):
                        ch, po = hh // 2, (hh % 2) * 64

                        def qk_fn(kt, qc0_, ql, nq_):
                            nk = kt["nk"]
                            if kt["own"]:
                                ka = KT[po:po + 64, ch, kt["c0"]:kt["c0"] + nk]
                            else:
                                ka = KTc[po:po + 64, ch, kt["c0"]:kt["c0"] + nk]
                            return [(ka, QT[po:po + 64, ch, qc0_ + ql:qc0_ + nq_])]

                        def v_fn(kt):
                            if kt["own"]:
                                return Vt[0:kt["nk"], kt["vi"], hh, :], ("dsV", kt["vi"])
                            return Vc[0:kt["nk"], kt["vi"], hh, :], "dsVc"

                        def out_fn(qb_, O, ok):
                            for (so, rows, oti) in qb_[3]:
                                sidx = so // 128
                                rc, rck = rot_buf("rc", 4, [128, 1], F32)
                                S.recip(rc[0:rows, :], O[0:rows, sidx * 65 + 64:sidx * 65 + 65], [ok], [rck])
                                S.ts("dve", ostage[0:rows, oti, hh * 64:hh * 64 + 64], O[0:rows, sidx * 65:sidx * 65 + 64],
                                     rc[0:rows, 0:1], None, ALU.mult, None, [ok, rck], [("dsO", oti)])

                        u1 = Group()
                        u1.sidx = u.sidx
                        u1.qblocks = [qb]
                        u1.keys = u.keys
                        kq_keys = ([("dsQT", i) for i in range(nblk)] + [("dsKT", i) for i in range(nblk)] + ["dsKTc"])
                        attn_softmax(u1, qk_fn, v_fn, sc, 4 + hh,
                                     (MT, lambda ki: [("dsMT", ki, s_) for s_ in range(4)]), out_fn, kq_keys, "ds", collect=tl)

                    tl = []
                    for hh in range(H):
                        dsa_head(hh, tl)
                    pipeline(tl, 5 if len(tl) > 8 else 3)
            finalize_o(ostage, "dsO", 3)
        S.barrier()
        rot.clear()


def phase_b(nc, S, cfg, dr, g, l, xT, oT, ident, ps_s, ps_o, xres_scr, layer_norm, to_xT, bcast_row, ln_pipeline):
    SEQ, NB, DEC, NS, PAST, DEPTH, TT = cfg.SEQ, cfg.NB, cfg.DEC, cfg.NS, cfg.PAST, cfg.DEPTH, cfg.TT
    ALPHA = cfg.DN_ALPHA
    last = (l == DEPTH - 1)
    stB = contextlib.ExitStack()

    def sb(name, shape, dt):
        _UID[0] += 1
        return stB.enter_context(nc.sbuf_tensor("%s_%d" % (name, _UID[0]), list(shape), dt))

    w_in = dr["w_in"][l].rearrange("(kc p) e -> p kc e", p=128)
    with stB:
        gA = sb("gA", [128, D], F32); bA = sb("bA", [128, D], F32)
        bf2 = sb("bf2", [128, D], F32)
        bf1 = sb("bf1", [128, 32], F32)
        bcast_row(bf2[:], dr["b_ff2"][l:l + 1, :], "bf2")
        S.dma("sp", bf1[:], dr["b_ff1_t"][l], (), ["bf1"])
        xr = sb("xr", [128, 5, D], F32)
        hT = sb("hT", [128, 32, 528], BF16)
        wbs = [sb("wbs%d" % i, [128, 2, 512], BF16) for i in range(3)]
        wf = [sb("wf%d" % i, [128, KC, 512], BF16) for i in range(3)]
        sg = [sb("sg%d" % i, [128, 528], F32) for i in range(2)]
        macc = sb("macc", [128, KC, 528], F32)
        rl = [sb("rl%d" % i, [128, 528], F32) for i in range(2)]
        junkB = sb("junkB", [128, D], BF16)
        wks = [dict(st=sb("stB%d" % i, [128, 8], F32), stk="stB%d" % i, junk=junkB, junkk="junkB",
                    xb=sb("xbB%d" % i, [128, D], BF16), xbk="xbB%d" % i) for i in range(4)]
        WB = dr["WB"]
        w_inb = WB["w_in"][l].rearrange("(kc p) e -> p kc e", p=128)
        gates = w_inb[:, :, GATE0:D_IN].rearrange("p k (n e) -> p k n e", n=4)
        wbr_v = WB["w_br"][l].rearrange("(n c p) e -> p n c e", p=128, c=2)
        wout_v = WB["w_out"][l].rearrange("(c p) e -> p c e", p=128)
        kin, kbr, kout, kf1, kf2 = [("wb", nm, l) for nm in ("w_in", "w_br", "w_out", "w_ff1", "w_ff2")]
        wctr = [0, 0, 0]
        w1 = WB["w_ff1"][l].rearrange("(kc p) f -> p kc f", p=128)
        w2 = WB["w_ff2"][l].rearrange("(fc p) e -> p fc e", p=128)
        for (c0, nb) in g.blocks_b:
            nchunks = [(0, min(nb, 512))] + ([(512, nb - 512)] if nb > 512 else [])
            tiles = [(t0, r) for (t0, r) in g.ln_tiles if c0 <= t0 < c0 + nb]
            for ti, (t0, r) in enumerate(tiles):
                S.dma("sp", xr[0:r, ti, :], xres_scr[t0:t0 + r, :], [("xres", t0 // 128)], [("xr", ti)])
            xkeys = [("xT", i) for i in range(c0 // 128, (c0 + nb - 1) // 128 + 1)]
            okeys = [("oT", n, i) for n in range(4) for i in range(c0 // 128, (c0 + nb - 1) // 128 + 1)]
            for n in range(4):
                for hf in range(2):
                    wft = wf[wctr[1] % 3]; wfk = "wf%d" % (wctr[1] % 3); wctr[1] += 1
                    S.dma("sp", wft[:], gates[:, :, n, hf * 512:(hf + 1) * 512], [kin], [wfk])
                    wbt = wbs[wctr[2] % 3]; wbk = "wbs%d" % (wctr[2] % 3); wctr[2] += 1
                    S.dma("sp", wbt[:], wbr_v[:, n, :, hf * 512:(hf + 1) * 512], [kbr], [wbk])
                    for j in range(4):
                      for (n0, nn) in nchunks:
                        dmc = hf * 4 + j
                        cs_ = slice(n0, n0 + nn)
                        pg, pgk = ps_s()
                        for kc in range(KC):
                            S.mm(pg[:, 0:nn], wft[:, kc, j * 128:(j + 1) * 128], xT[:, kc, c0 + n0:c0 + n0 + nn], kc == 0, kc == KC - 1,
                                 [wfk] + xkeys, [pgk])
                        pb, pbk = ps_s()
                        for wc in range(2):
                            S.mm(pb[:, 0:nn], wbt[:, wc, j * 128:(j + 1) * 128], oT[:, 2 * n + wc, c0 + n0:c0 + n0 + nn],
                                 wc == 0, wc == 1, [wbk] + okeys, [pbk])
                        si_ = (dmc + (1 if n0 else 0)) % 2
                        sgt = sg[si_]; sgk = "sg%d" % si_
                        mk_ = ("macc", dmc, n0)
                        S.act(sgt[:, cs_], pg[:, 0:nn], AF.Sigmoid, [pgk], [sgk])
                        if n == 0:
                            S.tt("dve", macc[:, dmc, cs_], sgt[:, cs_], pb[:, 0:nn], ALU.mult, [sgk, pbk], [mk_])
                        else:
                            S.tt("dve", sgt[:, cs_], sgt[:, cs_], pb[:, 0:nn], ALU.mult, [sgk, pbk], [sgk])
                            if n < 3:
                                S.tt("pool", macc[:, dmc, cs_], macc[:, dmc, cs_], sgt[:, cs_], ALU.add, [mk_, sgk], [mk_])
                            else:
                                S.tt("pool", hT[:, 24 + dmc, cs_], macc[:, dmc, cs_], sgt[:, cs_], ALU.add, [mk_, sgk],
                                     [("hT", 24 + dmc)])
            mkeys = [("hT", 24 + i) for i in range(KC)]
            bcast_row(gA[:], dr["ln1_g"][l:l + 1, :], "gA")
            bcast_row(bA[:], dr["ln1_b"][l:l + 1, :], "bA")
            for hf in range(2):
                wft = wf[wctr[1] % 3]; wfk = "wf%d" % (wctr[1] % 3); wctr[1] += 1
                S.dma("sp", wft[:], wout_v[:, :, hf * 512:(hf + 1) * 512], [kout], [wfk])
                for ti, (t0, r) in enumerate(tiles):
                    lo = t0 - c0
                    py, pyk = ps_o()
                    for dmc in range(KC):
                        S.mm(py[0:r, :], hT[:, 24 + dmc, lo:lo + r], wft[:, dmc, :], dmc == 0, dmc == KC - 1,
                             mkeys + [wfk], [pyk])
                    S.stt("dve", xr[0:r, ti, hf * 512:(hf + 1) * 512], xr[0:r, ti, hf * 512:(hf + 1) * 512], ALPHA,
                          py[0:r, :], ALU.mult, ALU.add, [("xr", ti), pyk], [("xr", ti)])
            items = [(xr[0:r, ti, :], r, ("xr", ti), t0, wks[ti % 4], None) for ti, (t0, r) in enumerate(tiles)]
            ln_pipeline(items, gA, bA, ["gA", "bA"], lambda it_: True, beng="pool", ceng="act")
            for fb in range(DFF // 512):
                wft = wf[wctr[1] % 3]; wfk = "wf%d" % (wctr[1] % 3); wctr[1] += 1
                S.dma("sp", wft[:], w1[:, :, fb * 512:(fb + 1) * 512], [kf1], [wfk])
                for j in range(4):
                  for (n0, nn) in nchunks:
                    fc = fb * 4 + j
                    cs_ = slice(n0, n0 + nn)
                    ph, phk = ps_s()
                    for kc in range(KC):
                        S.mm(ph[:, 0:nn], wft[:, kc, j * 128:(j + 1) * 128], xT[:, kc, c0 + n0:c0 + n0 + nn], kc == 0, kc == KC - 1,
                             [wfk] + xkeys, [phk])
                    ri_ = (fc + (1 if n0 else 0)) % 2
                    rt = rl[ri_]; rk = "rl%d" % ri_
                    S.act(rt[:, cs_], ph[:, 0:nn], AF.Relu, [phk, "bf1"], [rk], bias=bf1[:, fc:fc + 1])
                    S.tt("pool" if fc % 2 else "dve", hT[:, fc, cs_], rt[:, cs_], rt[:, cs_], ALU.mult, [rk], [("hT", fc)])
            hkeys = [("hT", i) for i in range(32)]
            bcast_row(gA[:], dr["ln2_g"][l:l + 1, :], "gA")
            bcast_row(bA[:], dr["ln2_b"][l:l + 1, :], "bA")
            for hf in range(2):
                accs = [ps_o() if i < 3 else ps_s() for i in range(len(tiles))]
                for fb in range(DFF // 512):
                    wft = wf[wctr[1] % 3]; wfk = "wf%d" % (wctr[1] % 3); wctr[1] += 1
                    S.dma("sp", wft[:, 0:4, :], w2[:, fb * 4:fb * 4 + 4, hf * 512:(hf + 1) * 512], [kf2], [wfk])
                    for j in range(4):
                        fc = fb * 4 + j
                        for ti, (t0, r) in enumerate(tiles):
                            lo = t0 - c0
                            S.mm(accs[ti][0][0:r, :], hT[:, fc, lo:lo + r], wft[:, j, :], fc == 0, fc == 31,
                                 hkeys + [wfk], [accs[ti][1]])
                for ti, (t0, r) in enumerate(tiles):
                    sl = slice(hf * 512, (hf + 1) * 512)
                    S.stt("dve", xr[0:r, ti, sl], xr[0:r, ti, sl], ALPHA, bf2[0:r, sl], ALU.mult, ALU.add,
                          [("xr", ti), "bf2"], [("xr", ti)])
                    S.tt("dve", xr[0:r, ti, sl], xr[0:r, ti, sl], accs[ti][0][0:r, :], ALU.add,
                         [("xr", ti), accs[ti][1]], [("xr", ti)])
            def after2(it_):
                ap, r, key, t0 = it_[0], it_[1], it_[2], it_[3]
                if last:
                    if g.kind == "p":
                        if t0 < SEQ:
                            S.dma("sp", dr["y_p"][g.idx, t0:t0 + r, :], ap, [key], [("y", t0)])
                    else:
                        S.dma("sp", dr["y_s"][t0:t0 + r, :], ap, [key], [("y", t0)])
                    return False
                S.dma("sp", xres_scr[t0:t0 + r, :], ap, [key], [("xres", t0 // 128)])
                return True

            items = [(xr[0:r, ti, :], r, ("xr", ti), t0, wks[ti % 4], None) for ti, (t0, r) in enumerate(tiles)]
            ln_pipeline(items, gA, bA, ["gA", "bA"], after2, beng="pool", ceng="act")

_CACHE = {}


def _get_program(cfg_key):
    if cfg_key not in _CACHE:
        cfg = Cfg(*cfg_key)
        nc, S = build(cfg)
        _CACHE[cfg_key] = (cfg, nc, S)
    return _CACHE[cfg_key]


def run(inputs, cfg_key):
    cfg, nc, S = _get_program(cfg_key)
    NB, NS, NCO = cfg.NB, cfg.NS, cfg.NCORES
    f32 = lambda a: np.ascontiguousarray(np.asarray(a, dtype=np.float32))
    consts = host_constants(cfg)
    shared = dict(consts)
    shared["meta"] = f32(inputs["meta"])
    shared["ln_in_g"] = f32(inputs["ln_in_g"]).reshape(1, D)
    shared["ln_in_b"] = f32(inputs["ln_in_b"]).reshape(1, D)
    shared["w_in"] = f32(inputs["w_in"])
    shared["qn_g"] = f32(inputs["mla_qnorm_g"])
    shared["w_uq"] = f32(inputs["mla_w_uq"])
    shared["kvn_g"] = f32(inputs["mla_kvnorm_g"])
    shared["w_uk"] = f32(inputs["mla_w_uk"]).reshape(cfg.DEPTH, 128, 256)
    shared["w_uv"] = f32(inputs["mla_w_uv"]).reshape(cfg.DEPTH, 128, 256)
    shared["lam_p"] = f32(inputs["diff_lambda"]).reshape(cfg.DEPTH, 128)
    shared["subln_g"] = f32(inputs["diff_subln_g"])
    shared["rel_bias"] = f32(inputs["rel_bias"])
    shared["w_br"] = f32(inputs["w_br"])
    shared["w_out"] = f32(inputs["w_out"])
    shared["ln1_g"] = f32(inputs["ln1_g"]); shared["ln1_b"] = f32(inputs["ln1_b"])
    shared["w_ff1"] = f32(inputs["w_ff1"])
    shared["b_ff1_t"] = f32(np.asarray(inputs["b_ff1"]).reshape(cfg.DEPTH, 32, 128).transpose(0, 2, 1))
    shared["w_ff2"] = f32(inputs["w_ff2"])
    shared["b_ff2"] = f32(inputs["b_ff2"])
    shared["ln2_g"] = f32(inputs["ln2_g"]); shared["ln2_b"] = f32(inputs["ln2_b"])
    xp = np.asarray(inputs["x_prompt"], dtype=np.float32)
    xs = np.asarray(inputs["x_sample"], dtype=np.float32)
    cache_in = {n: np.asarray(inputs["cache_" + n], dtype=np.float32) for n in CACHE_NAMES}
    in_maps = []
    for c in range(NCO):
        m = dict(shared)
        m["xp"] = np.ascontiguousarray(xp[c * NB:(c + 1) * NB])
        m["xs"] = np.ascontiguousarray(xs[c * NS:(c + 1) * NS].reshape(NS * cfg.DEC, D))
        for n, f in zip(CACHE_NAMES, CACHE_F):
            a = cache_in[n][:, c * NS:(c + 1) * NS]
            m["c_" + n] = np.ascontiguousarray(a.reshape(cfg.DEPTH, NS, cfg.PAST, f))
        in_maps.append(m)
    res = run_bass_kernel_spmd(nc, in_maps, core_ids=list(range(NCO)))
    R = res.results
    y_p = np.concatenate([r["y_p"] for r in R], axis=0)
    y_s = np.concatenate([r["y_s"].reshape(NS, cfg.DEC, D) for r in R], axis=0)
    trail = {"mla_kv": (128,), "mla_pe": (32,), "sb_k": (4, 64), "sb_v": (4, 64), "diff_k": (4, 2, 32),
             "diff_v": (4, 64), "dsa_k": (4, 64), "dsa_v": (4, 64), "dsa_kidx": (32,)}
    outs = [y_p, y_s]
    for n in CACHE_NAMES:
        a = np.concatenate([r["p_" + n] for r in R], axis=1)
        outs.append(a.reshape(cfg.DEPTH, cfg.BATCH, cfg.TT, *trail[n]))
    for n in CACHE_NAMES:
        a = np.concatenate([r["s_" + n].reshape(cfg.DEPTH, NS, cfg.DEC, -1) for r in R], axis=1)
        outs.append(a.reshape(cfg.DEPTH, cfg.DEC_BATCH, cfg.DEC, *trail[n]))
    return tuple(np.ascontiguousarray(o.astype(np.float32)) for o in outs)


def kernel(**inputs):
    return run(inputs, (2048, 2, 32, 4, 1024, 2, 16, 32))
```
